# Optimizing a Trainium2 kernel written in Bass

```python
import math
import jax, jax.numpy as jnp
from jax import lax
import numpy as np


D_MODEL = 1024
BATCH = 16
SEQ = 2048
DEPTH = 1

MEM_LEN = 256
EPS = 1e-6
SSD_INNER = 2 * D_MODEL
SSD_HEAD_DIM = 64
SSD_HEADS = SSD_INNER // SSD_HEAD_DIM
SSD_GROUPS = 4
SSD_HPG = SSD_HEADS // SSD_GROUPS
SSD_STATE = 128
SSD_CONV = 5
SSD_CHUNK = 128
SSD_XBC = SSD_INNER + 2 * SSD_GROUPS * SSD_STATE
ATTN_HEAD_DIM = 64
ATTN_Q_HEADS = D_MODEL // ATTN_HEAD_DIM
ATTN_KV_HEADS = 4
ATTN_GQA = ATTN_Q_HEADS // ATTN_KV_HEADS
ATTN_WIDTH = ATTN_Q_HEADS * ATTN_HEAD_DIM
ATTN_KV_WIDTH = ATTN_KV_HEADS * ATTN_HEAD_DIM
ATTN_WINDOW = 128
ATTN_BLOCK = 128
ROPE_THETA = 10000.0
XATTN_HEADS = 4
XATTN_HEAD_DIM = D_MODEL // XATTN_HEADS
FFN_HIDDEN = ((8 * D_MODEL + 3 * 256 - 1) // (3 * 256)) * 256
IN_SPLITS = (SSD_INNER, SSD_XBC, SSD_HEADS, SSD_HEADS, ATTN_WIDTH, ATTN_KV_WIDTH, ATTN_KV_WIDTH, 2 * D_MODEL)
IN_WIDTH = SSD_INNER + SSD_XBC + 2 * SSD_HEADS + ATTN_WIDTH + 2 * ATTN_KV_WIDTH + 2 * D_MODEL

kernel_name = 'bidir_hybrid_ssd_swa_gated_block'


def rms_norm(x, g):
    xf = x.astype(jnp.float32)
    y = xf * lax.rsqrt(jnp.mean(xf * xf, axis=-1, keepdims=True) + EPS)
    return (y * g.astype(jnp.float32)).astype(x.dtype)


def split_cols(a, sizes):
    return jnp.split(a, np.cumsum(sizes)[:-1].tolist(), axis=-1)


def depthwise_centred_conv(u, w, b):
    c = u.shape[-1]
    pad = (SSD_CONV - 1) // 2
    out = lax.conv_general_dilated(u, w[:, None, :].astype(u.dtype), (1,), [(pad, pad)],
                                   dimension_numbers=('NWC', 'WIO', 'NWC'), feature_group_count=c)
    return out + b.astype(u.dtype)


def ssd_chunked(xh, dt, a_head, b_in, c_in):
    bsz, s, g, r, p = xh.shape
    n = b_in.shape[-1]
    nc = s // SSD_CHUNK
    dtype = xh.dtype
    a_cum = jnp.cumsum((dt * a_head).reshape(bsz, nc, SSD_CHUNK, g, r), axis=2)
    x_c = (xh * dt[..., None].astype(dtype)).reshape(bsz, nc, SSD_CHUNK, g, r, p)
    b_c = b_in.reshape(bsz, nc, SSD_CHUNK, g, n)
    c_c = c_in.reshape(bsz, nc, SSD_CHUNK, g, n)
    lower = jnp.tril(jnp.ones((SSD_CHUNK, SSD_CHUNK), bool))
    seg = a_cum[:, :, :, None] - a_cum[:, :, None]
    decay = jnp.exp(jnp.where(lower[:, :, None, None], seg, -jnp.inf)).astype(dtype)
    cb = jnp.einsum('bclgn,bcsgn->bclsg', c_c, b_c)
    y_diag = jnp.einsum('bclsg,bclsgr,bcsgrp->bclgrp', cb, decay, x_c)
    to_end = jnp.exp(a_cum[:, :, -1:] - a_cum).astype(dtype)
    states = jnp.einsum('bclgn,bclgr,bclgrp->bcgrpn', b_c, to_end, x_c)
    chunk_decay = jnp.exp(a_cum[:, :, -1]).astype(dtype)

    def carry_state(state, inp):
        st, dec = inp
        return state * dec[..., None, None] + st, state

    h0 = jnp.zeros((bsz, g, r, p, n), dtype)
    _, h_in = lax.scan(carry_state, h0, (jnp.moveaxis(states, 1, 0), jnp.moveaxis(chunk_decay, 1, 0)))
    h_in = jnp.moveaxis(h_in, 0, 1)
    y_off = jnp.einsum('bclgn,bcgrpn,bclgr->bclgrp', c_c, h_in, jnp.exp(a_cum).astype(dtype))
    return (y_diag + y_off).reshape(bsz, s, g, r, p)


def gated_group_rmsnorm(y, z, g):
    bsz, s, d = y.shape
    v = (y * jax.nn.silu(z)).astype(jnp.float32).reshape(bsz, s, SSD_GROUPS, d // SSD_GROUPS)
    v = v * lax.rsqrt(jnp.mean(v * v, axis=-1, keepdims=True) + EPS)
    return (v.reshape(bsz, s, d) * g.astype(jnp.float32)).astype(y.dtype)


def rotary(u, pos):
    half = u.shape[-1] // 2
    inv_freq = ROPE_THETA ** (-jnp.arange(half, dtype=jnp.float32) / half)
    ang = pos.astype(jnp.float32)[:, None] * inv_freq[None]
    cos = jnp.cos(ang)[None, :, None, :]
    sin = jnp.sin(ang)[None, :, None, :]
    uf = u.astype(jnp.float32)
    u1, u2 = uf[..., :half], uf[..., half:]
    return jnp.concatenate([u1 * cos - u2 * sin, u2 * cos + u1 * sin], axis=-1).astype(u.dtype)


def band_blocks(u):
    b, s, h, d = u.shape
    nb = s // ATTN_BLOCK
    up = jnp.pad(u, ((0, 0), (ATTN_BLOCK, ATTN_BLOCK), (0, 0), (0, 0))).reshape(b, nb + 2, ATTN_BLOCK, h, d)
    return jnp.concatenate([up[:, :-2], up[:, 1:-1], up[:, 2:]], axis=2)


def windowed_gqa_with_sink(q, k, v, sink):
    b, s = q.shape[:2]
    nb = s // ATTN_BLOCK
    qb = q.reshape(b, nb, ATTN_BLOCK, ATTN_KV_HEADS, ATTN_GQA, ATTN_HEAD_DIM)
    kb = band_blocks(k)
    vb = band_blocks(v)
    scores = jnp.einsum('bnqhrd,bnkhd->bnhrqk', qb, kb).astype(jnp.float32) * (ATTN_HEAD_DIM ** -0.5)
    qpos = jnp.arange(s).reshape(nb, ATTN_BLOCK)
    kpos = (jnp.arange(nb)[:, None] - 1) * ATTN_BLOCK + jnp.arange(3 * ATTN_BLOCK)[None]
    valid = ((kpos[:, None, :] >= 0) & (kpos[:, None, :] < s)
             & (jnp.abs(qpos[:, :, None] - kpos[:, None, :]) <= ATTN_WINDOW))
    scores = jnp.where(valid[None, :, None, None], scores, -jnp.inf)
    sink_col = jnp.broadcast_to(sink.astype(jnp.float32).reshape(1, 1, ATTN_KV_HEADS, ATTN_GQA, 1, 1),
                                scores.shape[:-1] + (1,))
    probs = jax.nn.softmax(jnp.concatenate([scores, sink_col], axis=-1), axis=-1)[..., :-1]
    out = jnp.einsum('bnhrqk,bnkhd->bnqhrd', probs.astype(v.dtype), vb)
    return out.reshape(b, s, ATTN_WIDTH)


def hybrid_mixer(h, pos, w_in, conv_w, conv_b, dt_bias_f, dt_bias_b, a_log_f, a_log_b, d_skip,
                 ssd_norm_g, sink, w_br_ssd, w_br_attn, w_out):
    b, s, _ = h.shape
    z, xbc, dt_f, dt_b, q, k, v, gates = split_cols(h @ w_in, IN_SPLITS)
    xbc = jax.nn.silu(depthwise_centred_conv(xbc, conv_w, conv_b))
    xs, b_in, c_in = split_cols(xbc, (SSD_INNER, SSD_GROUPS * SSD_STATE, SSD_GROUPS * SSD_STATE))
    xh = xs.reshape(b, s, SSD_GROUPS, SSD_HPG, SSD_HEAD_DIM)
    b_in = b_in.reshape(b, s, SSD_GROUPS, SSD_STATE)
    c_in = c_in.reshape(b, s, SSD_GROUPS, SSD_STATE)
    dtf = jax.nn.softplus((dt_f + dt_bias_f).astype(jnp.float32)).reshape(b, s, SSD_GROUPS, SSD_HPG)
    dtb = jax.nn.softplus((dt_b + dt_bias_b).astype(jnp.float32)).reshape(b, s, SSD_GROUPS, SSD_HPG)
    af = -jnp.exp(a_log_f.astype(jnp.float32)).reshape(SSD_GROUPS, SSD_HPG)
    ab = -jnp.exp(a_log_b.astype(jnp.float32)).reshape(SSD_GROUPS, SSD_HPG)
    y_fwd = ssd_chunked(xh, dtf, af, b_in, c_in)
    y_bwd = jnp.flip(ssd_chunked(jnp.flip(xh, 1), jnp.flip(dtb, 1), ab,
                                 jnp.flip(b_in, 1), jnp.flip(c_in, 1)), 1)
    y = y_fwd + y_bwd + d_skip.reshape(SSD_GROUPS, SSD_HPG, 1).astype(xh.dtype) * xh
    y = gated_group_rmsnorm(y.reshape(b, s, SSD_INNER), z, ssd_norm_g)
    branch_ssd = y @ w_br_ssd
    q = rotary(q.reshape(b, s, ATTN_Q_HEADS, ATTN_HEAD_DIM), pos)
    k = rotary(k.reshape(b, s, ATTN_KV_HEADS, ATTN_HEAD_DIM), pos)
    v = v.reshape(b, s, ATTN_KV_HEADS, ATTN_HEAD_DIM)
    branch_attn = windowed_gqa_with_sink(q, k, v, sink) @ w_br_attn
    g_ssd, g_attn = jnp.split(jax.nn.sigmoid(gates.astype(jnp.float32)).astype(h.dtype), 2, axis=-1)
    return (g_ssd * branch_ssd + g_attn * branch_attn) @ w_out


def memory_cross_attention(h, mem_n, w_q, w_kv, w_o):
    b, s, _ = h.shape
    m = mem_n.shape[1]
    q = (h @ w_q).reshape(b, s, XATTN_HEADS, XATTN_HEAD_DIM)
    k, v = jnp.split(mem_n @ w_kv, 2, axis=-1)
    k = k.reshape(b, m, XATTN_HEADS, XATTN_HEAD_DIM)
    v = v.reshape(b, m, XATTN_HEADS, XATTN_HEAD_DIM)
    scores = jnp.einsum('bqhd,bkhd->bhqk', q, k).astype(jnp.float32) * (XATTN_HEAD_DIM ** -0.5)
    probs = jax.nn.softmax(scores, axis=-1).astype(v.dtype)
    out = jnp.einsum('bhqk,bkhd->bqhd', probs, v).reshape(b, s, D_MODEL)
    return out @ w_o


def swiglu_ffn(h, w_in, w_out):
    gate, up = jnp.split(h @ w_in, 2, axis=-1)
    return (jax.nn.silu(gate) * up) @ w_out


def setup_inputs(seed: int = 0) -> dict:
    key = jax.random.key(seed)
    ks = jax.random.split(key, 32)
    L = DEPTH

    def normal(k, shape, scale):
        return jax.random.normal(k, shape, jnp.float32) * scale

    def gain(k, n):
        return 1.0 + 0.02 * jax.random.normal(k, (L, n), jnp.float32)

    def dt_bias(k):
        dt0 = jnp.exp(jax.random.uniform(k, (L, SSD_HEADS), jnp.float32, math.log(1e-3), math.log(1e-1)))
        return dt0 + jnp.log(-jnp.expm1(-dt0))

    def a_log(k):
        return jnp.log(jax.random.uniform(k, (L, SSD_HEADS), jnp.float32, 1.0, 16.0))

    return {
        'x': normal(ks[0], (BATCH, SEQ, D_MODEL), 1.0),
        'mem': normal(ks[1], (BATCH, MEM_LEN, D_MODEL), 1.0),
        'norm_mix_g': gain(ks[2], D_MODEL),
        'w_in': normal(ks[3], (L, D_MODEL, IN_WIDTH), D_MODEL ** -0.5),
        'conv_w': normal(ks[4], (L, SSD_CONV, SSD_XBC), SSD_CONV ** -0.5),
        'conv_b': normal(ks[5], (L, SSD_XBC), 0.02),
        'dt_bias_fwd': dt_bias(ks[6]),
        'dt_bias_bwd': dt_bias(ks[7]),
        'a_log_fwd': a_log(ks[8]),
        'a_log_bwd': a_log(ks[9]),
        'd_skip': 1.0 + normal(ks[10], (L, SSD_HEADS), 0.1),
        'ssd_norm_g': gain(ks[11], SSD_INNER),
        'attn_sink': normal(ks[12], (L, ATTN_Q_HEADS), 0.5),
        'w_branch_ssd': normal(ks[13], (L, SSD_INNER, D_MODEL), SSD_INNER ** -0.5),
        'w_branch_attn': normal(ks[14], (L, ATTN_WIDTH, D_MODEL), ATTN_WIDTH ** -0.5),
        'w_mix_out': normal(ks[15], (L, D_MODEL, D_MODEL), D_MODEL ** -0.5),
        'norm_xattn_g': gain(ks[16], D_MODEL),
        'norm_mem_g': gain(ks[17], D_MODEL),
        'w_xattn_q': normal(ks[18], (L, D_MODEL, D_MODEL), D_MODEL ** -0.5),
        'w_xattn_kv': normal(ks[19], (L, D_MODEL, 2 * D_MODEL), D_MODEL ** -0.5),
        'w_xattn_out': normal(ks[20], (L, D_MODEL, D_MODEL), D_MODEL ** -0.5),
        'norm_ffn_g': gain(ks[21], D_MODEL),
        'w_ffn_in': normal(ks[22], (L, D_MODEL, 2 * FFN_HIDDEN), D_MODEL ** -0.5),
        'w_ffn_out': normal(ks[23], (L, FFN_HIDDEN, D_MODEL), FFN_HIDDEN ** -0.5),
        'norm_final_g': 1.0 + 0.02 * jax.random.normal(ks[24], (D_MODEL,), jnp.float32),
    }


def reference(x, mem, norm_mix_g, w_in, conv_w, conv_b, dt_bias_fwd, dt_bias_bwd, a_log_fwd, a_log_bwd,
              d_skip, ssd_norm_g, attn_sink, w_branch_ssd, w_branch_attn, w_mix_out, norm_xattn_g,
              norm_mem_g, w_xattn_q, w_xattn_kv, w_xattn_out, norm_ffn_g, w_ffn_in, w_ffn_out, norm_final_g):
    pos = jnp.arange(x.shape[1], dtype=jnp.int32)
    for layer in range(DEPTH):
        x = x + hybrid_mixer(rms_norm(x, norm_mix_g[layer]), pos, w_in[layer], conv_w[layer], conv_b[layer],
                             dt_bias_fwd[layer], dt_bias_bwd[layer], a_log_fwd[layer], a_log_bwd[layer],
                             d_skip[layer], ssd_norm_g[layer], attn_sink[layer], w_branch_ssd[layer],
                             w_branch_attn[layer], w_mix_out[layer])
        x = x + memory_cross_attention(rms_norm(x, norm_xattn_g[layer]), rms_norm(mem, norm_mem_g[layer]),
                                       w_xattn_q[layer], w_xattn_kv[layer], w_xattn_out[layer])
        x = x + swiglu_ffn(rms_norm(x, norm_ffn_g[layer]), w_ffn_in[layer], w_ffn_out[layer])
    return rms_norm(x, norm_final_g)
```

```python
import math
from contextlib import ExitStack
import numpy as np
import concourse.bass as bass
import concourse.mybir as mybir
from concourse.bass_utils import run_bass_kernel_spmd

F32 = mybir.dt.float32
BF16 = mybir.dt.bfloat16
U8 = mybir.dt.uint8
AF = mybir.ActivationFunctionType
ALU = mybir.AluOpType
AX = mybir.AxisListType

D_MODEL = 1024
EPS = 1e-6
FFN_H = 2816
OZ, OXBC, ODT, OQ, OQP, OK_, OKP, OV, OG = 0, 2048, 5120, 5184, 6208, 7232, 7744, 8256, 8512
WIN = 10560


class Buf:
    __slots__ = ("name", "w", "r")

    def __init__(self, name=""):
        self.name = name
        self.w = None
        self.r = {}


class Ev:
    __slots__ = ("sem", "val", "clock", "eng")

    def __init__(self, eng):
        self.sem = None
        self.val = None
        self.clock = None
        self.eng = eng


class Sched:
    ENGS = ("pe", "act", "dve", "pool", "sp")
    EPOCH = 16000

    def __init__(self, n_dma_sems=48):
        self.streams = {e: [] for e in self.ENGS}
        self.cnt = {e: 0 for e in self.ENGS}
        self.know = {e: {} for e in self.ENGS}
        self.pending = {e: [] for e in self.ENGS}
        self.last = {e: None for e in self.ENGS}
        self.n_dma = n_dma_sems
        self.dma_cnt = [0] * n_dma_sems
        self.dma_last = [None] * n_dma_sems
        self.dma_rr = 0
        self.dma_rr_sw = 0
        self.dma_open = []
        self.out_events = []
        self.epoch = {e: 0 for e in self.ENGS}
        self.nops = 0

    def _need(self, eng, ev, waits):
        if ev is None:
            return
        if ev.sem is None and ev.eng == eng:
            return
        assert ev.sem is not None, "dependency on unresolved (non-inc) op"
        k = self.know[eng]
        if k.get(ev.sem, 0) >= ev.val:
            return
        if ev.sem[0] != "dma":
            for ep in range(ev.sem[1] + 1, self.epoch[ev.sem[0]] + 1):
                if k.get((ev.sem[0], ep), 0) >= 1:
                    return
        waits.append((ev.sem, ev.val))
        for s, v in ev.clock.items():
            if k.get(s, 0) < v:
                k[s] = v

    @staticmethod
    def _dedupe(waits):
        best = {}
        for s, v in waits:
            if best.get(s, 0) < v:
                best[s] = v
        return list(best.items())

    def _deps(self, eng, reads, writes):
        waits = []
        for b in reads:
            self._need(eng, b.w, waits)
        for b in writes:
            if b.w is not None and (b.w.eng != eng or eng != "pe"):
                self._need(eng, b.w, waits)
            for e2, ev in b.r.items():
                if e2 != eng or eng != "pe":
                    self._need(eng, ev, waits)
        return self._dedupe(waits)

    def _mark(self, ev, eng, reads, writes):
        for b in reads:
            b.r[eng] = ev
        for b in writes:
            b.w = ev
            b.r = {}

    def op(self, eng, fn, reads=(), writes=(), inc=True):
        self.nops += 1
        waits = self._deps(eng, reads, writes)
        if not inc:
            ev = Ev(eng)
            self.pending[eng].append(ev)
            self._mark(ev, eng, reads, writes)
            self.streams[eng].append((waits, fn, None))
            return ev
        self.cnt[eng] += 1
        if self.cnt[eng] > self.EPOCH:
            self.cnt[eng] = 1
            self.epoch[eng] += 1
        n = self.cnt[eng]
        ev = Ev(eng)
        ev.sem = (eng, self.epoch[eng])
        ev.val = n
        ev.clock = dict(self.know[eng])
        ev.clock[ev.sem] = n
        for p in self.pending[eng]:
            p.sem, p.val, p.clock = ev.sem, ev.val, ev.clock
        self.pending[eng] = []
        self._mark(ev, eng, reads, writes)
        self.streams[eng].append((waits, fn, (ev.sem, 1)))
        self.last[eng] = ev
        return ev

    def dma(self, eng, fn, reads=(), writes=(), is_output=False):
        self.nops += 1
        waits = self._deps(eng, reads, writes)
        n_sw = self.n_dma // 3
        if eng == "pool":
            j = self.dma_rr_sw
            self.dma_rr_sw = (j + 1) % n_sw
        else:
            j = n_sw + self.dma_rr
            self.dma_rr = (self.dma_rr + 1) % (self.n_dma - n_sw)
        prev = self.dma_last[j]
        if prev is not None:
            self._need(eng, prev, waits)
            waits = self._dedupe(waits)
        self.dma_cnt[j] += 16
        ev = Ev(eng)
        ev.sem = ("dma", j)
        ev.val = self.dma_cnt[j]
        ev.clock = dict(self.know[eng])
        ev.clock[ev.sem] = ev.val
        self.dma_last[j] = ev
        self._mark(ev, eng, reads, writes)
        self.streams[eng].append((waits, fn, (ev.sem, 16)))
        self.dma_open.append(ev)
        if is_output:
            self.out_events.append(ev)
        return ev

    def barrier(self, engines=("pe", "act", "dve", "pool")):
        for e in self.ENGS:
            assert not self.pending[e], "barrier with pending non-inc ops"
        for e in engines:
            waits = []
            for e2 in ("pe", "act", "dve", "pool"):
                if e2 != e:
                    self._need(e, self.last[e2], waits)
            for ev in self.dma_open:
                self._need(e, ev, waits)
            self.streams[e].append((self._dedupe(waits), None, None))
        self.dma_open = []

    def finish(self):
        waits = []
        for ev in self.out_events:
            self._need("sp", ev, waits)
        self.streams["sp"].append((self._dedupe(waits), None, None))

    def emit(self, nc, stack):
        sems = {}
        for e in ("pe", "act", "dve", "pool"):
            for ep in range(self.epoch[e] + 1):
                sems[(e, ep)] = stack.enter_context(nc.semaphore("s_%s%d" % (e, ep)))
        for j in range(self.n_dma):
            sems[("dma", j)] = stack.enter_context(nc.semaphore("s_dma%d" % j))
        block = stack.enter_context(nc.Block())
        streams = self.streams

        def run(handle, lst):
            for waits, fn, inc in lst:
                for s, v in waits:
                    handle.wait_ge(sems[s], v)
                if fn is None:
                    continue
                ins = fn(handle)
                if inc is not None:
                    ins.then_inc(sems[inc[0]], inc[1])

        @block.sync
        def _(e):
            run(e, streams["sp"])

        @block.tensor
        def _(e):
            run(e, streams["pe"])

        @block.scalar
        def _(e):
            run(e, streams["act"])

        @block.vector
        def _(e):
            run(e, streams["dve"])

        @block.gpsimd
        def _(e):
            run(e, streams["pool"])


class LazyReg:
    def __init__(self, value):
        self.value = value
        self.reg = None

    def get(self, e):
        if self.reg is None:
            self.reg = e.to_reg(self.value)
        return self.reg


def _I(name, *args, **kwargs):
    def f(e):
        kw = {k: (v.get(e) if isinstance(v, LazyReg) else v) for k, v in kwargs.items()}
        return getattr(e, name)(*args, **kw)
    return f


def _bytes(dt):
    return 4 if dt == F32 else 2


class Arena:
    def __init__(self, ap_u8, size):
        self.ap = ap_u8
        self.size = size
        self.off = 0
        self.peak = 0

    def alloc(self, n_elems, dt, parts=128):
        nb = n_elems * _bytes(dt)
        off = (self.off + 63) // 64 * 64
        assert off + nb <= self.size, "SBUF arena overflow: need %d have %d" % (off + nb, self.size)
        self.off = off + nb
        self.peak = max(self.peak, self.off)
        return self.ap[0:parts, off:off + nb].bitcast(dt)

    def mark(self):
        return self.off

    def reset(self, m):
        self.off = m


def v3(ap, a):
    return ap.rearrange("p (a b) -> p a b", a=a)


def v4(ap, a, b):
    return ap.rearrange("p (a b c) -> p a b c", a=a, b=b)


def build_program(SEQ, NSEQ, dbg=None, STAGE=3):
    NCH = SEQ // 128
    NSB = SEQ // 512
    NTOK = NSEQ * SEQ
    nc = bass.Bass("TRN2", target_bir_lowering=False)
    S = Sched()
    dbg = dbg or {}

    def din(name, shape, dt=F32):
        return nc.dram_tensor(name, shape, dt, kind="ExternalInput").ap()

    def dscr(name, shape, dt):
        return nc.dram_tensor(name, shape, dt, kind="Internal").ap()

    x_d = din("x", [NTOK, 1024])
    mem_d = din("mem", [NSEQ * 256, 1024])
    w_in_d = din("w_in_r", [1024, WIN])
    w_bs_d = din("w_bs", [2048, 1024])
    w_ba_d = din("w_ba", [1024, 1024])
    w_mo_d = din("w_mo", [1024, 1024])
    w_xq_d = din("w_xq", [1024, 1024])
    w_xkv_d = din("w_xkv", [1024, 2048])
    w_xo_d = din("w_xo", [1024, 1024])
    w_fi_d = din("w_fi", [1024, 2 * FFN_H])
    w_fo_d = din("w_fo", [FFN_H, 1024])
    gains_d = din("gains", [5, 1024])
    convp_d = din("convp", [128, 24 * 8])
    vecs_d = din("vecs", [1, 192])
    ssdg_d = din("ssdg", [1, 2048])
    cst_d = din("cst", [128, 6 * 128])
    rope_d = din("rope", [128, 2 * SEQ])
    out_d = nc.dram_tensor("out", [NTOK, 1024], F32, kind="ExternalOutput").ap()
    dbg_d = {k: nc.dram_tensor("dbg_" + k, list(shp), F32, kind="ExternalOutput").ap() for k, shp in dbg.items()}

    wb_in = dscr("wb_in", [1024, WIN], BF16)
    wb_bs = dscr("wb_bs", [2048, 1024], BF16)
    wb_ba = dscr("wb_ba", [1024, 1024], BF16)
    wb_mo = dscr("wb_mo", [1024, 1024], BF16)
    wb_xq = dscr("wb_xq", [1024, 1024], BF16)
    wb_xkv = dscr("wb_xkv", [1024, 2048], BF16)
    wb_xo = dscr("wb_xo", [1024, 1024], BF16)
    wb_fi = dscr("wb_fi", [1024, 2 * FFN_H], BF16)
    wb_fo = dscr("wb_fo", [FFN_H, 1024], BF16)
    yn_scr = dscr("yn_scr", [NTOK, 2048], BF16)
    ac_scr = dscr("ac_scr", [2 * NCH * 4096], F32)

    with ExitStack() as st:
        ARENA_BYTES = 212736
        arena_t = st.enter_context(nc.sbuf_tensor("arena", [128, ARENA_BYTES], U8))
        A = Arena(arena_t, ARENA_BYTES)
        PS = []
        PSB = []
        for i in range(8):
            t = st.enter_context(nc.psum_tensor("psb%d" % i, [128, 512], F32))
            PS.append(t)
            PSB.append(Buf("ps%d" % i))
        ps_rr = [0]

        ps_sub = [list(range(8))]
        ps_cnt = {}

        def psum(sub=None):
            sub = tuple(sub if sub is not None else ps_sub[0])
            k = ps_cnt.get(sub, 0)
            ps_cnt[sub] = k + 1
            i = sub[k % len(sub)]
            return PS[i], PSB[i]

        WB = {}

        def cast_weight(key, src, dst, rows, step):
            bl = []
            for r0 in range(0, rows, step):
                r1 = min(rows, r0 + step)
                b = Buf("w_%s_%d" % (key, r0))
                S.dma("pool", _I("dma_start", out=dst[r0:r1, :], in_=src[r0:r1, :]), writes=[b])
                bl.append(b)
            WB[key] = bl

        cast_weight("xkv", w_xkv_d, wb_xkv, 1024, 512)
        cast_weight("in", w_in_d, wb_in, 1024, 128)

        cst_f = A.alloc(768, F32)
        b_cst = Buf("cst")
        S.dma("sp", _I("dma_start", out=cst_f, in_=cst_d[:, :]), writes=[b_cst])
        ident_f = cst_f[:, 0:128]
        triF = cst_f[:, 128:256]
        triB = cst_f[:, 256:384]
        ones_f = cst_f[:, 384:512]
        maskF = cst_f[:, 512:640]
        maskB = cst_f[:, 640:768]
        cst_b = A.alloc(512, BF16)
        b_cstb = Buf("cstb")
        S.op("dve", _I("tensor_copy", out=cst_b, in_=cst_f[:, 0:512]), reads=[b_cst], writes=[b_cstb])
        ident_b = cst_b[:, 0:128]
        ones_b = cst_b[:, 384:512]
        vecs = A.alloc(192, F32)
        b_vecs = Buf("vecs")
        S.dma("sp", _I("dma_start", out=vecs, in_=vecs_d[0:1, :].partition_broadcast(128)[:, 0, :]), writes=[b_vecs])
        convp = A.alloc(24 * 8, F32)
        b_convp = Buf("convp")
        S.dma("sp", _I("dma_start", out=convp, in_=convp_d[:, :]), writes=[b_convp])
        negA = A.alloc(64, F32)
        esink = A.alloc(16, F32)
        b_negA = Buf("negA")
        b_esink = Buf("esink")
        S.op("act", _I("activation", out=negA, in_=vecs[:, 64:128], func=AF.Exp), reads=[b_vecs], writes=[b_negA])
        S.op("dve", _I("tensor_scalar", out=negA, in0=negA, scalar1=-1.0, scalar2=None, op0=ALU.mult), reads=[b_negA], writes=[b_negA])
        S.op("act", _I("activation", out=esink, in_=vecs[:, 160:176], func=AF.Exp), reads=[b_vecs], writes=[b_esink])
        KxT = A.alloc(8 * 256, BF16)
        Vx = A.alloc(2 * 1024, BF16)
        b_KxT = Buf("KxT")
        b_Vx = Buf("Vx")
        ss_t = [A.alloc(2, F32) for _ in range(4)]
        ss_b = [Buf("ss%d" % i) for i in range(4)]
        ss_rr = [0]
        junk = A.alloc(1024, BF16)
        b_junk = Buf("junk")
        neghalf = A.alloc(2, F32)
        b_neghalf = Buf("neghalf")
        S.op("pool", _I("memset", neghalf, -0.5), writes=[b_neghalf])
        PERSIST_MARK = A.mark()

        def dbg_tap(name, src_ap, src_bufs, dst_slice):
            if name in dbg_d:
                S.dma("pool", _I("dma_start", out=dst_slice(dbg_d[name]), in_=src_ap), reads=src_bufs,
                      writes=[Buf()], is_output=True)

        def rms_rows(xt, xb, g_ap, g_buf, out_bf, out_buf, dim=1024, eps=EPS, pre=1.0, post=1.0):
            i = ss_rr[0]
            ss_rr[0] = (i + 1) % 4
            ss, sb_ = ss_t[i], ss_b[i]
            S.op("act", _I("activation", out=junk[:, 0:dim], in_=xt, func=AF.Square, accum_out=ss[:, 0:1]),
                 reads=[xb], writes=[b_junk, sb_])
            S.op("pool", _I("tensor_scalar", out=ss[:, 1:2], in0=ss[:, 0:1], scalar1=pre / dim, scalar2=eps,
                                                   op0=ALU.mult, op1=ALU.add), reads=[sb_], writes=[sb_])
            S.op("pool", _I("tensor_tensor", out=ss[:, 1:2], in0=ss[:, 1:2], in1=neghalf[:, 0:1], op=ALU.pow),
                 reads=[sb_, b_neghalf], writes=[sb_])
            if post != 1.0:
                S.op("pool", _I("tensor_scalar", out=ss[:, 1:2], in0=ss[:, 1:2], scalar1=post, scalar2=None, op0=ALU.mult),
                     reads=[sb_], writes=[sb_])
            S.op("dve", _I("scalar_tensor_tensor", out=out_bf, in0=xt, scalar=ss[:, 1:2], in1=g_ap,
                                                         op0=ALU.mult, op1=ALU.mult),
                 reads=[xb, sb_, g_buf], writes=[out_buf])

        def transpose_blocks(src_bf, src_bufs, nblk, dst_view_fn, dst_bufs, evac="act", dst_bufs_fn=None):
            k = 0
            while k < nblk:
                n = min(8, nblk - k)
                pt, pb = psum()
                ptb = pt[:].bitcast(BF16)
                for j in range(n):
                    S.op("pe", _I("transpose", out=ptb[:, j * 128:(j + 1) * 128],
                                                                       in_=src_bf[:, (k + j) * 128:(k + j + 1) * 128],
                                                                       identity=ident_b),
                         reads=list(src_bufs) + [b_cstb], writes=[pb], inc=(j == n - 1))
                dst = dst_view_fn(k, n)
                srcv = v3(ptb[:, 0:n * 128], n)
                if dst_bufs_fn is not None:
                    dst_bufs = dst_bufs_fn(k, n)
                if evac == "act":
                    S.op("act", _I("copy", out=dst, in_=srcv), reads=[pb], writes=dst_bufs)
                else:
                    S.op(evac, _I("tensor_copy", out=dst, in_=srcv), reads=[pb], writes=dst_bufs)
                k += n

        def wtile_load(dst, dst_buf, wsrc, wbufs, r0, nkc, c0, ncols, eng="sp"):
            src = wsrc[r0:r0 + nkc * 128, c0:c0 + ncols].rearrange("(k p) n -> p k n", p=128)
            S.dma(eng, _I("dma_start", out=dst, in_=src), reads=wbufs, writes=[dst_buf])

        gains_t = {}

        def load_gain(idx, name):
            g = A.alloc(1024, F32)
            b = Buf("g_" + name)
            S.dma("sp", _I("dma_start", out=g, in_=gains_d[idx:idx + 1, :].partition_broadcast(128)[:, 0, :]), writes=[b])
            gains_t[name] = (g, b)
            return g, b

        def mm(out, lhsT, rhs, start, stop, reads, wbuf, inc, skip=False):
            if skip:
                S.op("pe", _I("matmul", out, lhsT=lhsT, rhs=rhs, start=start, stop=stop, skip_group_check=True),
                     reads=reads, writes=[wbuf], inc=inc)
            else:
                S.op("pe", _I("matmul", out, lhsT=lhsT, rhs=rhs, start=start, stop=stop),
                     reads=reads, writes=[wbuf], inc=inc)

        NEGBIG = LazyReg(-30000.0)

        def bc8(ap8):
            return ap8.unsqueeze(2).to_broadcast([128, 8, 64])

        acw = ac_scr.rearrange("(d c h l) -> d h c l", d=2, c=NCH, h=32, l=128)
        acr = ac_scr.rearrange("(r k) -> r k", k=1024)
        b_ac = [[Buf("ac%d_%d" % (d, c)) for c in range(NCH)] for d in range(2)]
        b_yn = [[Buf("yn%d_%d" % (c, g)) for g in range(4)] for c in range(NSEQ * NCH)]

        def ssd_phase(sq, tok0, hT3, b_hT):
            convp3 = v3(convp, 24)
            for g in range(4):
                mg = A.mark()
                Wz = A.alloc(8 * 512, BF16)
                Wz3 = v3(Wz, 8)
                b_Wz = Buf("Wz")
                wtile_load(Wz3, b_Wz, wb_in, WB["in"], 0, 8, OZ + g * 512, 512)
                Wdt = A.alloc(8 * 16, BF16)
                Wdt3 = v3(Wdt, 8)
                b_Wdt = Buf("Wdt")
                wtile_load(Wdt3, b_Wdt, wb_in, WB["in"], 0, 8, ODT + g * 16, 16)
                ssdg = A.alloc(512, F32)
                b_ssdg = Buf("ssdg")
                S.dma("sp", _I("dma_start", out=ssdg, in_=ssdg_d[0:1, g * 512:(g + 1) * 512].partition_broadcast(128)[:, 0, :]),
                      writes=[b_ssdg])

                NV = NCH * 16

                def small():
                    return A.alloc(NV, F32), Buf("small")

                dtv, b_dtv = small()
                dA, b_dA = small()
                a_sb, b_a = small()
                nega, b_nega = small()
                ea, b_ea = small()
                te, b_te = small()
                cd, b_cd = small()
                sdte, b_sdte = small()
                dtv3, dA3, a3, nega3, ea3, te3, cd3, sdte3 = [v3(t, NCH) for t in (dtv, dA, a_sb, nega, ea, te, cd, sdte)]
                pt, pb = psum()
                for c in range(NCH):
                    for kc in range(8):
                        mm(pt[:, c * 16:(c + 1) * 16], hT3[:, kc, c * 128:(c + 1) * 128], Wdt3[:, kc, :], kc == 0, kc == 7,
                           [b_hT[c], b_Wdt], pb, inc=(kc == 7 and c == NCH - 1))
                bias_g = vecs[:, g * 16:(g + 1) * 16]
                S.op("dve", _I("tensor_tensor", out=dtv3, in0=v3(pt[:, 0:NV], NCH),
                                                      in1=bias_g.unsqueeze(1).to_broadcast([128, NCH, 16]), op=ALU.add),
                     reads=[pb, b_vecs], writes=[b_dtv])
                S.op("act", _I("activation", out=dtv, in_=dtv, func=AF.Exp), reads=[b_dtv], writes=[b_dtv])
                S.op("act", _I("activation", out=dtv, in_=dtv, func=AF.Ln, bias=1.0), reads=[b_dtv], writes=[b_dtv])
                negA_g = negA[:, g * 16:(g + 1) * 16]
                S.op("dve", _I("tensor_tensor", out=dA3, in0=dtv3, in1=negA_g.unsqueeze(1).to_broadcast([128, NCH, 16]), op=ALU.mult),
                     reads=[b_dtv, b_negA], writes=[b_dA])
                pa, pab = psum()
                ptot, ptotb = psum()
                for c in range(NCH):
                    mm(pa[:, c * 16:c * 16 + 8], triF, dA3[:, c, 0:8], True, True, [b_cst, b_dA], pab, inc=False)
                    mm(pa[:, c * 16 + 8:c * 16 + 16], triB, dA3[:, c, 8:16], True, True, [b_cst, b_dA], pab, inc=(c == NCH - 1))
                for c in range(NCH):
                    mm(ptot[:, c * 16:(c + 1) * 16], ones_f, dA3[:, c, :], True, True, [b_cst, b_dA], ptotb, inc=(c == NCH - 1))
                S.op("act", _I("copy", out=a_sb, in_=pa[:, 0:NV]), reads=[pab], writes=[b_a])
                S.op("dve", _I("tensor_scalar", out=nega, in0=a_sb, scalar1=-1.0, scalar2=None, op0=ALU.mult), reads=[b_a], writes=[b_nega])
                S.op("act", _I("activation", out=ea, in_=a_sb, func=AF.Exp), reads=[b_a], writes=[b_ea])
                S.op("dve", _I("tensor_tensor", out=te, in0=ptot[:, 0:NV], in1=a_sb, op=ALU.subtract), reads=[ptotb, b_a], writes=[b_te])
                S.op("act", _I("activation", out=te, in_=te, func=AF.Exp), reads=[b_te], writes=[b_te])
                S.op("act", _I("activation", out=cd, in_=ptot[:, 0:NV], func=AF.Exp), reads=[ptotb], writes=[b_cd])
                S.op("dve", _I("tensor_tensor", out=sdte, in0=dtv, in1=te, op=ALU.mult), reads=[b_dtv, b_te], writes=[b_sdte])
                aTs = [(A.alloc(512, F32, parts=8), Buf("aT%d" % i)) for i in range(2)]
                for d in range(2):
                    tri = triF if d == 0 else triB
                    for cb in range(0, NCH, 4):
                        pT_, pTb = psum()
                        for j in range(4):
                            mm(pT_[0:8, j * 128:(j + 1) * 128], dA3[:, cb + j, d * 8:(d + 1) * 8], tri, True, True, [b_cst, b_dA], pTb, inc=(j == 3))
                        aT, b_aT = aTs[(d * (NCH // 4) + cb // 4) % 2]
                        S.op("act", _I("copy", out=aT, in_=pT_[0:8, :]), reads=[pTb], writes=[b_aT])
                        S.dma("pool", _I("dma_start", out=acw[d, 8 * g:8 * g + 8, cb:cb + 4, :], in_=v3(aT, 4)),
                              reads=[b_aT], writes=[b_ac[d][cb + j] for j in range(4)])

                BT = A.alloc(SEQ, BF16)
                b_BT = [Buf("BT%d" % s_) for s_ in range(NSB)]
                CT = A.alloc(SEQ, BF16)
                b_CT = [Buf("CT%d" % s_) for s_ in range(NSB)]
                x_tok = A.alloc(NCH * 512, BF16)
                x_tok3 = v3(x_tok, NCH)
                b_xtok = [Buf("xtok%d" % c) for c in range(NCH)]
                B_tok = A.alloc(NCH * 128, BF16)
                B_tok3 = v3(B_tok, NCH)
                b_Btok = [Buf("Btok%d" % c) for c in range(NCH)]
                mG2 = A.mark()
                Wg = A.alloc(8 * 768, BF16)
                Wg3 = v3(Wg, 8)
                b_Wg = [Buf("Wg%d" % i) for i in range(3)]
                wtile_load(Wg3[:, :, 0:512], b_Wg[0], wb_in, WB["in"], 0, 8, OXBC + g * 512, 512)
                wtile_load(Wg3[:, :, 512:640], b_Wg[1], wb_in, WB["in"], 0, 8, OXBC + 2048 + g * 128, 128)
                wtile_load(Wg3[:, :, 640:768], b_Wg[2], wb_in, WB["in"], 0, 8, OXBC + 2560 + g * 128, 128)
                Dg = A.alloc(30 * 128, BF16)
                Dg3 = v3(Dg, 30)
                b_Dg = Buf("Dg")

                def ctg(j):
                    return g * 4 + j if j < 4 else (16 + g if j == 4 else 20 + g)

                for j in range(6):
                    for k in range(5):
                        S.op("dve", _I("tensor_scalar", out=Dg3[:, j * 5 + k, :], in0=ident_b,
                                                                        scalar1=convp3[:, ctg(j), k:k + 1], scalar2=None, op0=ALU.mult),
                             reads=[b_cstb, b_convp], writes=[b_Dg])
                upad = []
                for i in range(2):
                    u = A.alloc(SEQ + 4, BF16)
                    bu = [Buf("u%d_%d" % (i, s_)) for s_ in range(NSB)] + [Buf("upad%d" % i)]
                    S.op("pool", _I("memset", u[:, 0:2], 0.0), writes=[bu[NSB]], inc=False)
                    S.op("pool", _I("memset", u[:, SEQ + 2:SEQ + 4], 0.0), writes=[bu[NSB]])
                    upad.append((u, bu))
                xTt = [(A.alloc(SEQ, BF16), [Buf("xTt%d_%d" % (i, s_)) for s_ in range(NSB)]) for i in range(2)]
                for j in range(6):
                    u, bu = upad[j % 2]
                    wcol = j * 128
                    wb_ = b_Wg[0] if j < 4 else b_Wg[j - 3]
                    for sb in range(NSB):
                        pt, pb = psum()
                        for kc in range(8):
                            mm(pt[:, :], Wg3[:, kc, wcol:wcol + 128], hT3[:, kc, sb * 512:(sb + 1) * 512], kc == 0, kc == 7,
                               [wb_] + b_hT[sb * 4:sb * 4 + 4], pb, inc=(kc == 7))
                        S.op("act", _I("copy", out=u[:, 2 + sb * 512:2 + (sb + 1) * 512], in_=pt[:, :]),
                             reads=[pb], writes=[bu[sb]])
                    if j < 4:
                        dst, b_dst = xTt[j % 2]
                    elif j == 4:
                        dst, b_dst = BT, b_BT
                    else:
                        dst, b_dst = CT, b_CT
                    for sb in range(NSB):
                        pt, pb = psum()
                        rd = [b_Dg, bu[NSB]] + [bu[s_] for s_ in range(max(0, sb - 1), min(NSB, sb + 2))]
                        for k in range(5):
                            mm(pt[:, :], Dg3[:, j * 5 + k, :], u[:, sb * 512 + k:sb * 512 + k + 512], k == 0, k == 4, rd, pb, inc=(k == 4))
                        S.op("act", _I("activation", out=dst[:, sb * 512:(sb + 1) * 512], in_=pt[:, :], func=AF.Silu,
                                                                                     bias=convp3[:, ctg(j), 5:6], scale=1.0),
                             reads=[pb, b_convp], writes=[b_dst[sb]])
                    if j < 4:
                        transpose_blocks(dst, b_dst, NCH, lambda k0, n, j=j: x_tok3[:, k0:k0 + n, j * 128:(j + 1) * 128],
                                         None, evac="dve", dst_bufs_fn=lambda k0, n: b_xtok[k0:k0 + n])
                    elif j == 4:
                        transpose_blocks(dst, b_dst, NCH, lambda k0, n: B_tok3[:, k0:k0 + n, :],
                                         None, evac="dve", dst_bufs_fn=lambda k0, n: b_Btok[k0:k0 + n])

                S.barrier(("pe", "act", "dve", "pool", "sp"))
                A.reset(mG2)
                hbin = A.alloc(NCH * 512, BF16)
                hbin3 = v3(hbin, NCH)
                b_hbin = [Buf("hbin%d" % c) for c in range(NCH)]
                hb = A.alloc(512, F32)
                b_hb = Buf("hb")
                S.op("pool", _I("memset", hb, 0.0), writes=[b_hb])
                xvs = [(A.alloc(512, BF16), Buf("xv%d" % i)) for i in range(7)]
                xv_rr = [0]

                def variant(c, scale_ap8, reads, eng="dve"):
                    i = xv_rr[0]
                    xv_rr[0] = (i + 1) % 7
                    xv, bx = xvs[i]
                    S.op(eng, _I("tensor_tensor", out=v3(xv, 8), in0=v3(x_tok3[:, c, :], 8), in1=bc8(scale_ap8), op=ALU.mult),
                         reads=[b_xtok[c]] + reads, writes=[bx])
                    return xv, bx

                for c in range(NCH - 1, -1, -1):
                    S.op("act", _I("copy", out=hbin3[:, c, :], in_=hb), reads=[b_hb], writes=[b_hbin[c]])
                    if c == 0:
                        break
                    xv, bx = variant(c, sdte3[:, c, 8:16], [b_sdte])
                    pt, pb = psum()
                    mm(pt[:, :], B_tok3[:, c, :], xv, True, True, [b_Btok[c], bx], pb, inc=True)
                    S.op("dve", _I("tensor_tensor", out=v3(hb, 8), in0=v3(hb, 8), in1=bc8(cd3[:, c, 8:16]), op=ALU.mult),
                         reads=[b_hb, b_cd], writes=[b_hb])
                    S.op("dve", _I("tensor_tensor", out=hb, in0=hb, in1=pt[:, :], op=ALU.add), reads=[b_hb, pb], writes=[b_hb])

                hf = A.alloc(512, F32)
                b_hf = Buf("hf")
                hfb = A.alloc(512, BF16)
                b_hfb = Buf("hfb")
                S.op("pool", _I("memset", hf, 0.0), writes=[b_hf])
                S.op("pool", _I("memset", hfb, 0.0), writes=[b_hfb])
                bcs = [[(A.alloc(1024, F32), Buf("bc%d_%d" % (d, i))) for i in range(2)] for d in range(2)]
                Es = [[(A.alloc(1024, BF16), Buf("E%d_%d" % (d, i))) for i in range(2)] for d in range(2)]
                Ms = [[(A.alloc(1024, BF16), Buf("M%d_%d" % (d, i))) for i in range(2)] for d in range(2)]
                f32s = [(A.alloc(512, F32), Buf("f32_%d" % i)) for i in range(4)]
                f_rr = [0]

                def f32tile():
                    i = f_rr[0]
                    f_rr[0] = (i + 1) % 4
                    return f32s[i]

                yns = [(A.alloc(512, BF16), Buf("yn%d" % i)) for i in range(2)]
                D_g = vecs[:, 128 + g * 8:128 + (g + 1) * 8]
                for c in range(NCH):
                    par = c % 2
                    EM = []
                    for d in range(2):
                        bc, b_bc = bcs[d][par]
                        row = (d * NCH + c) * 4 + g
                        S.dma("sp", _I("dma_start", out=bc, in_=acr[row:row + 1, :].partition_broadcast(128)[:, 0, :]),
                              reads=[b_ac[d][c]], writes=[b_bc])
                        E, b_E = Es[d][par]
                        E3 = v3(E, 8)
                        bc3 = v3(bc, 8)
                        msk = maskF if d == 0 else maskB
                        S.op("pool", _I("tensor_tensor", out=bc3, in0=bc3, in1=msk.unsqueeze(1).to_broadcast([128, 8, 128]), op=ALU.add),
                             reads=[b_bc, b_cst], writes=[b_bc])
                        for h in range(8):
                            S.op("act", _I("activation", out=E3[:, h, :], in_=bc3[:, h, :], func=AF.Exp,
                                           bias=nega3[:, c, d * 8 + h:d * 8 + h + 1], scale=1.0),
                                 reads=[b_bc, b_nega], writes=[b_E], inc=(h == 7))
                        EM.append((E3, b_E))
                    pcb, pcbb = psum()
                    sbi = c // 4
                    mm(pcb[:, 0:128], BT[:, c * 128:(c + 1) * 128], CT[:, c * 128:(c + 1) * 128], True, True, [b_BT[sbi], b_CT[sbi]], pcbb, inc=True)
                    Mt = []
                    for d in range(2):
                        M, b_M = Ms[d][par]
                        M3 = v3(M, 8)
                        E3, b_E = EM[d]
                        S.op("dve", _I("tensor_tensor", out=M3, in0=E3, in1=pcb[:, 0:128].unsqueeze(1).to_broadcast([128, 8, 128]),
                                                                                  op=ALU.mult), reads=[b_E, pcbb], writes=[b_M])
                        Mt.append((M3, b_M))
                    xdf, b_xdf = variant(c, dtv3[:, c, 0:8], [b_dtv])
                    xdb, b_xdb = variant(c, dtv3[:, c, 8:16], [b_dtv])
                    xD, b_xD = variant(c, D_g, [b_vecs], eng="pool")
                    pY, pYb = psum()
                    mm(pY[:, :], ident_b, xD, True, False, [b_cstb, b_xD], pYb, inc=False, skip=True)
                    for h in range(8):
                        mm(pY[:, h * 64:(h + 1) * 64], Mt[0][0][:, h, :], xdf[:, h * 64:(h + 1) * 64], False, False, [Mt[0][1], b_xdf], pYb, inc=False, skip=True)
                        mm(pY[:, h * 64:(h + 1) * 64], Mt[1][0][:, h, :], xdb[:, h * 64:(h + 1) * 64], False, h == 7, [Mt[1][1], b_xdb], pYb, inc=(h == 7), skip=True)
                    pF, pFb = psum()
                    mm(pF[:, :], CT[:, c * 128:(c + 1) * 128], hfb, True, True, [b_CT[sbi], b_hfb], pFb, inc=True)
                    pB, pBb = psum()
                    mm(pB[:, :], CT[:, c * 128:(c + 1) * 128], hbin3[:, c, :], True, True, [b_CT[sbi], b_hbin[c]], pBb, inc=True)
                    if c < NCH - 1:
                        xef, b_xef = variant(c, sdte3[:, c, 0:8], [b_sdte], eng="pool")
                        pS, pSb = psum()
                        mm(pS[:, :], B_tok3[:, c, :], xef, True, True, [b_Btok[c], b_xef], pSb, inc=True)
                        S.op("dve", _I("tensor_tensor", out=v3(hf, 8), in0=v3(hf, 8), in1=bc8(cd3[:, c, 0:8]), op=ALU.mult),
                             reads=[b_hf, b_cd], writes=[b_hf])
                        S.op("dve", _I("tensor_tensor", out=hf, in0=hf, in1=pS[:, :], op=ALU.add), reads=[b_hf, pSb], writes=[b_hf])
                        S.op("act", _I("copy", out=hfb, in_=hf), reads=[b_hf], writes=[b_hfb])
                    t1, b_t1 = f32tile()
                    t2, b_t2 = f32tile()
                    S.op("dve", _I("tensor_tensor", out=v3(t1, 8), in0=v3(pF[:, :], 8), in1=bc8(ea3[:, c, 0:8]), op=ALU.mult),
                         reads=[pFb, b_ea], writes=[b_t1])
                    S.op("dve", _I("tensor_tensor", out=v3(t2, 8), in0=v3(pB[:, :], 8), in1=bc8(ea3[:, c, 8:16]), op=ALU.mult),
                         reads=[pBb, b_ea], writes=[b_t2])
                    S.op("pool", _I("tensor_tensor", out=t1, in0=t1, in1=t2, op=ALU.add), reads=[b_t1, b_t2], writes=[b_t1])
                    S.op("dve", _I("tensor_tensor", out=t1, in0=t1, in1=pY[:, :], op=ALU.add), reads=[b_t1, pYb], writes=[b_t1])
                    if "y" in dbg_d and sq == 0:
                        dbg_tap("y", t1, [b_t1], lambda d_, c=c: d_[c * 128:(c + 1) * 128, g * 512:(g + 1) * 512])
                    pZ, pZb = psum()
                    for kc in range(8):
                        mm(pZ[:, :], hT3[:, kc, c * 128:(c + 1) * 128], Wz3[:, kc, :], kc == 0, kc == 7, [b_hT[c], b_Wz], pZb, inc=(kc == 7))
                    th, b_th = f32tile()
                    S.op("act", _I("activation", out=th, in_=pZ[:, :], func=AF.Tanh, scale=0.5), reads=[pZb], writes=[b_th])
                    S.op("dve", _I("scalar_tensor_tensor", out=th, in0=th, scalar=1.0, in1=pZ[:, :], op0=ALU.add, op1=ALU.mult),
                         reads=[b_th, pZb], writes=[b_th])
                    S.op("pool", _I("tensor_tensor", out=th, in0=th, in1=t1, op=ALU.mult), reads=[b_th, b_t1], writes=[b_th])
                    yn, b_ynt = yns[par]
                    rms_rows(th, b_th, ssdg, b_ssdg, yn, b_ynt, dim=512, pre=0.25, post=0.5)
                    S.dma("pool", _I("dma_start", out=yn_scr[tok0 + c * 128:tok0 + (c + 1) * 128, g * 512:(g + 1) * 512], in_=yn),
                          reads=[b_ynt], writes=[b_yn[sq * NCH + c][g]])
                S.barrier(("pe", "act", "dve", "pool", "sp"))
                A.reset(mg)

        def token_phase(sq, tok0):
            g_mix, b_gmix = load_gain(0, "mix")
            g_xa, b_gxa = load_gain(1, "xattn")
            g_ffn, b_gffn = load_gain(3, "ffn")
            g_fin, b_gfin = load_gain(4, "final")
            wslots = [(A.alloc(8 * 512, BF16), Buf("wslot%d" % i)) for i in range(4)]
            w_rr = [0]

            def wload(wsrc, wkey, r0, nkc, c0, ncols):
                i = w_rr[0]
                w_rr[0] = (i + 1) % len(wslots)
                t, b = wslots[i]
                view = v3(t[:, 0:nkc * ncols], nkc)
                wtile_load(view, b, wsrc, WB[wkey], r0, nkc, c0, ncols)
                return view, b

            xres = A.alloc(4 * 1024, F32)
            xres3 = v3(xres, 4)
            b_x = [Buf("xres%d" % i) for i in range(4)]
            xh = A.alloc(1024, F32)
            b_xh = Buf("xh")
            hns = [(A.alloc(1024, BF16), Buf("hn%d" % i)) for i in range(2)]
            hn_rr = [0]
            hT6 = A.alloc(8 * 768, BF16)
            hT63 = v3(hT6, 8)
            b_hT6 = [Buf("hT6_%d" % i) for i in range(6)]
            KrT6 = A.alloc(4 * 2 * 768, BF16)
            KrT64 = v4(KrT6, 4, 2)
            b_KrT = Buf("KrT6")
            S.op("pool", _I("memset", KrT6, 0.0), writes=[b_KrT])
            Vaug = A.alloc(6 * 260, BF16)
            Vaug4 = v4(Vaug, 6, 4)
            b_Vaug = Buf("Vaug")
            cos6 = A.alloc(768, F32)
            sin6 = A.alloc(768, F32)
            b_cos = Buf("cos6")
            b_sin = Buf("sin6")
            FM = {}
            for nm in ("M1", "QT", "AT", "MT"):
                t = A.alloc(8 * 512, BF16)
                FM[nm] = (v3(t, 8), Buf(nm))
            R1 = A.alloc(12 * 1024, BF16)
            ynT3 = v3(R1[:, 0:8192], 16)
            ynbs = [R1[:, 8192:10240], R1[:, 10240:12288]]
            actT3 = v3(R1[:, 0:22 * 512], 22)
            b_R1 = Buf("R1")
            b_ynb = [Buf("ynb0"), Buf("ynb1")]
            PTs = [(A.alloc(512, BF16), Buf("PT%d" % i)) for i in range(12)]
            PxT = [(A.alloc(512, BF16), Buf("PxT%d" % i)) for i in range(2)]
            ths = [(A.alloc(512, BF16), Buf("th%d" % i)) for i in range(2)]
            th_rr = [0]
            ftm = [(A.alloc(512, F32), Buf("ftm%d" % i)) for i in range(3)]
            f_rr = [0]
            attn_bfs = [(A.alloc(1024, BF16), Buf("attnbf%d" % i)) for i in range(2)]
            outf = [(A.alloc(1024, F32), Buf("outf%d" % i)) for i in range(1)]
            dens = A.alloc(8, F32)
            b_dens = Buf("dens")

            def ftile():
                i = f_rr[0]
                f_rr[0] = (i + 1) % len(ftm)
                return ftm[i]

            def thtile():
                i = th_rr[0]
                th_rr[0] = (i + 1) % len(ths)
                return ths[i]

            def hntile():
                i = hn_rr[0]
                hn_rr[0] = (i + 1) % 2
                return hns[i]

            def proj(wv, wb, mcol, rhs, rbufs, n):
                pt, pb = psum()
                nk = wv.shape[1]
                for kc in range(nk):
                    mm(pt[:, 0:n], wv[:, kc, mcol:mcol + 128], rhs(kc), kc == 0, kc == nk - 1, [wb] + rbufs, pb, inc=(kc == nk - 1))
                return pt, pb

            S.op("pool", _I("memset", Vaug4[:, :, :, 64:65], 1.0), writes=[b_Vaug])

            for sb in range(NSB):
                cg0 = sb * 4
                valid = [li for li in range(6) if 0 <= cg0 - 1 + li < NCH]
                lo, hi = valid[0], valid[-1] + 1
                gl = (cg0 - 1 + lo) * 128
                S.dma("sp", _I("dma_start", out=cos6[:, lo * 128:hi * 128], in_=rope_d[:, gl:gl + (hi - lo) * 128]), writes=[b_cos])
                S.dma("sp", _I("dma_start", out=sin6[:, lo * 128:hi * 128], in_=rope_d[:, SEQ + gl:SEQ + gl + (hi - lo) * 128]), writes=[b_sin])
                for li in valid:
                    c = cg0 - 1 + li
                    if 1 <= li <= 4:
                        xt, bxt = xres3[:, li - 1, :], b_x[li - 1]
                    else:
                        xt, bxt = xh, b_xh
                    S.dma("sp", _I("dma_start", out=xt, in_=x_d[tok0 + c * 128:tok0 + (c + 1) * 128, :]), writes=[bxt])
                    hn, b_hn = hntile()
                    rms_rows(xt, bxt, g_mix, b_gmix, hn, b_hn)
                    transpose_blocks(hn, [b_hn], 8, lambda k0, n, li=li: hT63[:, k0:k0 + n, li * 128:(li + 1) * 128], [b_hT6[li]])
                own = [b_hT6[i] for i in range(1, 5)]
                hown = lambda kc: hT63[:, kc, 128:640]
                Wk, bWk = wload(wb_in, "in", 0, 8, OK_, 512)
                Wkp, bWkp = wload(wb_in, "in", 0, 8, OKP, 512)
                Wv, bWv = wload(wb_in, "in", 0, 8, OV, 256)
                segs = [(128, 640, own)]
                if 0 in valid:
                    segs.append((0, 128, [b_hT6[0]]))
                if 5 in valid:
                    segs.append((640, 768, [b_hT6[5]]))
                for (s0, s1, sbufs) in segs:
                    n = s1 - s0
                    for j in range(4):
                        p1, p1b = proj(Wk, bWk, j * 128, lambda kc: hT63[:, kc, s0:s1], sbufs, n)
                        p2, p2b = proj(Wkp, bWkp, j * 128, lambda kc: hT63[:, kc, s0:s1], sbufs, n)
                        t1, bt1 = ftile()
                        t2, bt2 = ftile()
                        S.op("dve", _I("tensor_tensor", out=t1[:, 0:n], in0=p1[:, 0:n], in1=cos6[:, s0:s1], op=ALU.mult), reads=[p1b, b_cos], writes=[bt1])
                        S.op("dve", _I("tensor_tensor", out=t2[:, 0:n], in0=p2[:, 0:n], in1=sin6[:, s0:s1], op=ALU.mult), reads=[p2b, b_sin], writes=[bt2])
                        S.op("pool", _I("tensor_tensor", out=KrT64[0:64, j, 0, s0:s1], in0=t1[0:64, 0:n], in1=t2[0:64, 0:n], op=ALU.add),
                             reads=[bt1, bt2], writes=[b_KrT])
                        S.op("pool", _I("tensor_tensor", out=KrT64[64:128, j, 1, s0:s1], in0=t1[64:128, 0:n], in1=t2[64:128, 0:n], op=ALU.add),
                             reads=[bt1, bt2], writes=[b_KrT])
                for li in valid:
                    pt, pb = psum()
                    for kc in range(8):
                        mm(pt[:, 0:256], hT63[:, kc, li * 128:(li + 1) * 128], Wv[:, kc, :], kc == 0, kc == 7, [bWv, b_hT6[li]], pb, inc=(kc == 7))
                    S.op("act", _I("copy", out=Vaug4[:, li, :, 0:64], in_=v3(pt[:, 0:256], 4)), reads=[pb], writes=[b_Vaug])
                QT3, b_QT = FM["QT"]
                for blk in range(2):
                    Wq, bWq = wload(wb_in, "in", 0, 8, OQ + blk * 512, 512)
                    Wqp, bWqp = wload(wb_in, "in", 0, 8, OQP + blk * 512, 512)
                    for mt in range(4):
                        t = blk * 4 + mt
                        p1, p1b = proj(Wq, bWq, mt * 128, hown, own, 512)
                        p2, p2b = proj(Wqp, bWqp, mt * 128, hown, own, 512)
                        t1, bt1 = ftile()
                        t2, bt2 = ftile()
                        S.op("dve", _I("tensor_tensor", out=t1, in0=p1[:, :], in1=cos6[:, 128:640], op=ALU.mult), reads=[p1b, b_cos], writes=[bt1])
                        S.op("dve", _I("tensor_tensor", out=t2, in0=p2[:, :], in1=sin6[:, 128:640], op=ALU.mult), reads=[p2b, b_sin], writes=[bt2])
                        S.op("pool", _I("tensor_tensor", out=QT3[:, t, :], in0=t1, in1=t2, op=ALU.add), reads=[bt1, bt2], writes=[b_QT])
                for cq in range(4):
                    ynb, bynb = ynbs[cq % 2], b_ynb[cq % 2]
                    r0 = tok0 + (cg0 + cq) * 128
                    S.dma("sp", _I("dma_start", out=ynb, in_=yn_scr[r0:r0 + 128, :]), reads=b_yn[sq * NCH + cg0 + cq], writes=[bynb])
                    transpose_blocks(ynb, [bynb], 16, lambda k0, n, cq=cq: ynT3[:, k0:k0 + n, cq * 128:(cq + 1) * 128], [b_R1])
                M13, b_M1 = FM["M1"]
                for blk in range(2):
                    Wgt, bWgt = wload(wb_in, "in", 0, 8, OG + blk * 512, 512)
                    Wb0, bWb0 = wload(wb_bs, "bs", 0, 8, blk * 512, 512)
                    Wb1, bWb1 = wload(wb_bs, "bs", 1024, 8, blk * 512, 512)
                    for mt in range(4):
                        t = blk * 4 + mt
                        pg, pgb = proj(Wgt, bWgt, mt * 128, hown, own, 512)
                        th, bth = thtile()
                        S.op("act", _I("activation", out=th, in_=pg[:, :], func=AF.Tanh, scale=0.5), reads=[pgb], writes=[bth])
                        pbr, pbrb = psum()
                        for kc in range(16):
                            wv_, wb_ = (Wb0, bWb0) if kc < 8 else (Wb1, bWb1)
                            mm(pbr[:, :], wv_[:, kc % 8, mt * 128:(mt + 1) * 128], ynT3[:, kc, :], kc == 0, kc == 15, [wb_, b_R1], pbrb, inc=(kc == 15))
                        S.op("dve", _I("scalar_tensor_tensor", out=M13[:, t, :], in0=th, scalar=1.0, in1=pbr[:, :], op0=ALU.add, op1=ALU.mult),
                             reads=[bth, pbrb], writes=[b_M1])
                AT3, b_AT = FM["AT"]
                ps_sub[0] = [0, 1, 2, 3]
                for cq in range(4):
                    lq = cq + 1
                    kbs = [kb for kb in (lq - 1, lq, lq + 1) if kb in valid]
                    Ob = [(PS[4 + i], PSB[4 + i]) for i in range(4)]
                    pt_i = 0
                    for j in range(4):
                        ptl = []
                        for kb in kbs:
                            pS_, pSb = psum()
                            for r in range(4):
                                i = 4 * j + r
                                mm(pS_[:, r * 128:(r + 1) * 128], KrT64[:, j, i % 2, kb * 128:(kb + 1) * 128],
                                   QT3[:, i // 2, cq * 128:(cq + 1) * 128], True, True, [b_KrT, b_QT], pSb, inc=(r == 3))
                            PT, bPT = PTs[pt_i]
                            pt_i += 1
                            S.op("act", _I("activation", out=PT, in_=pS_[:, :], func=AF.Exp, scale=0.125), reads=[pSb], writes=[bPT])
                            PT3 = v3(PT, 4)
                            if kb == lq - 1:
                                S.op("pool", _I("affine_select", out=PT3, in_=PT3, pattern=[[0, 4], [-1, 128]], compare_op=ALU.is_ge, fill=0.0,
                                                base=0, channel_multiplier=1), reads=[bPT], writes=[bPT])
                            elif kb == lq + 1:
                                S.op("pool", _I("affine_select", out=PT3, in_=PT3, pattern=[[0, 4], [1, 128]], compare_op=ALU.is_ge, fill=0.0,
                                                base=0, channel_multiplier=-1), reads=[bPT], writes=[bPT])
                            ptl.append((PT, bPT, kb))
                        for r in range(4):
                            i = 4 * j + r
                            O_, Ob_ = Ob[i // 4]
                            oc = (i % 4) * 65
                            for n_, (PT, bPT, kb) in enumerate(ptl):
                                mm(O_[:, oc:oc + 65], PT[:, r * 128:(r + 1) * 128], Vaug4[:, kb, j, :], n_ == 0, n_ == len(ptl) - 1,
                                   [bPT, b_Vaug], Ob_, inc=(n_ == len(ptl) - 1))
                    abf, b_abf = attn_bfs[cq % 2]
                    abf3 = v3(abf, 16)
                    for bnk in range(4):
                        O_, Ob_ = Ob[bnk]
                        O3 = O_[:, 0:260].rearrange("p (h d) -> p h d", h=4)
                        S.op("dve", _I("tensor_tensor", out=dens[:, 0:4], in0=O3[:, :, 64], in1=esink[:, bnk * 4:(bnk + 1) * 4], op=ALU.add),
                             reads=[Ob_, b_esink], writes=[b_dens])
                        S.op("dve", _I("reciprocal", out=dens[:, 4:8], in_=dens[:, 0:4]), reads=[b_dens], writes=[b_dens])
                        S.op("dve", _I("tensor_tensor", out=abf3[:, bnk * 4:(bnk + 1) * 4, :], in0=O3[:, :, 0:64],
                                       in1=dens[:, 4:8].unsqueeze(2).to_broadcast([128, 4, 64]), op=ALU.mult), reads=[Ob_, b_dens], writes=[b_abf])
                    if "attn" in dbg_d and sq == 0:
                        tf, btf = outf[0]
                        S.op("act", _I("copy", out=tf, in_=abf), reads=[b_abf], writes=[btf])
                        dbg_tap("attn", tf, [btf], lambda d_: d_[(cg0 + cq) * 128:(cg0 + cq + 1) * 128, :])
                    transpose_blocks(abf, [b_abf], 8, lambda k0, n, cq=cq: AT3[:, k0:k0 + n, cq * 128:(cq + 1) * 128], [b_AT])
                ps_sub[0] = list(range(8))
                MT3, b_MT = FM["MT"]
                for blk in range(2):
                    Wgt, bWgt = wload(wb_in, "in", 0, 8, OG + 1024 + blk * 512, 512)
                    Wa, bWa = wload(wb_ba, "ba", 0, 8, blk * 512, 512)
                    for mt in range(4):
                        t = blk * 4 + mt
                        pg, pgb = proj(Wgt, bWgt, mt * 128, hown, own, 512)
                        th, bth = thtile()
                        S.op("act", _I("activation", out=th, in_=pg[:, :], func=AF.Tanh, scale=0.5), reads=[pgb], writes=[bth])
                        pa_, pab_ = proj(Wa, bWa, mt * 128, lambda kc: AT3[:, kc, :], [b_AT], 512)
                        tm, btm = ftile()
                        S.op("dve", _I("scalar_tensor_tensor", out=tm, in0=th, scalar=1.0, in1=pa_[:, :], op0=ALU.add, op1=ALU.mult),
                             reads=[bth, pab_], writes=[btm])
                        S.op("pool", _I("tensor_tensor", out=MT3[:, t, :], in0=tm, in1=M13[:, t, :], op=ALU.add), reads=[btm, b_M1], writes=[b_MT])
                for half in range(2):
                    Wm, bWm = wload(wb_mo, "mo", 0, 8, half * 512, 512)
                    for cq in range(4):
                        pt, pb = psum()
                        for kc in range(8):
                            mm(pt[:, :], MT3[:, kc, cq * 128:(cq + 1) * 128], Wm[:, kc, :], kc == 0, kc == 7, [b_MT, bWm], pb, inc=(kc == 7))
                        xs = xres3[:, cq, half * 512:(half + 1) * 512]
                        S.op("dve", _I("scalar_tensor_tensor", out=xs, in0=pt[:, :], scalar=0.5, in1=xs, op0=ALU.mult, op1=ALU.add),
                             reads=[pb, b_x[cq]], writes=[b_x[cq]])
                if "x1" in dbg_d and sq == 0:
                    for cq in range(4):
                        dbg_tap("x1", xres3[:, cq, :], [b_x[cq]], lambda d_, cq=cq: d_[(cg0 + cq) * 128:(cg0 + cq + 1) * 128, :])
                H2T3, b_H2T = FM["AT"]
                for cq in range(4):
                    hn, b_hn = hntile()
                    rms_rows(xres3[:, cq, :], b_x[cq], g_xa, b_gxa, hn, b_hn)
                    transpose_blocks(hn, [b_hn], 8, lambda k0, n, cq=cq: H2T3[:, k0:k0 + n, cq * 128:(cq + 1) * 128], [b_H2T])
                QX3, b_QX = FM["QT"]
                for blk in range(2):
                    Wq, bWq = wload(wb_xq, "xq", 0, 8, blk * 512, 512)
                    for mt in range(4):
                        pq, pqb = proj(Wq, bWq, mt * 128, lambda kc: H2T3[:, kc, :], [b_H2T], 512)
                        S.op("act", _I("copy", out=QX3[:, blk * 4 + mt, :], in_=pq[:, :]), reads=[pqb], writes=[b_QX])
                XA3, b_XA = FM["M1"]
                KxT3 = v3(KxT, 8)
                Vx3 = v3(Vx, 2)
                for hx in range(4):
                    for mtile in range(2):
                        pS_, pSb = psum()
                        for dt_ in range(2):
                            mm(pS_[:, :], KxT3[:, 2 * hx + dt_, mtile * 128:(mtile + 1) * 128], QX3[:, 2 * hx + dt_, :], dt_ == 0, dt_ == 1,
                               [b_KxT, b_QX], pSb, inc=(dt_ == 1))
                        S.op("act", _I("activation", out=PxT[mtile][0], in_=pS_[:, :], func=AF.Exp, scale=0.0625), reads=[pSb], writes=[PxT[mtile][1]])
                    pD, pDb = psum()
                    for mtile in range(2):
                        mm(pD[:, :], ones_b, PxT[mtile][0], mtile == 0, mtile == 1, [b_cstb, PxT[mtile][1]], pDb, inc=(mtile == 1))
                    rc, brc = ftile()
                    S.op("dve", _I("reciprocal", out=rc, in_=pD[:, :]), reads=[pDb], writes=[brc])
                    for dt_ in range(2):
                        pO, pOb = psum()
                        for mtile in range(2):
                            mm(pO[:, :], Vx3[:, mtile, hx * 256 + dt_ * 128:hx * 256 + (dt_ + 1) * 128], PxT[mtile][0], mtile == 0, mtile == 1,
                               [b_Vx, PxT[mtile][1]], pOb, inc=(mtile == 1))
                        S.op("dve", _I("tensor_tensor", out=XA3[:, 2 * hx + dt_, :], in0=pO[:, :], in1=rc, op=ALU.mult), reads=[pOb, brc], writes=[b_XA])
                for half in range(2):
                    Wo, bWo = wload(wb_xo, "xo", 0, 8, half * 512, 512)
                    for cq in range(4):
                        pt, pb = psum()
                        for kc in range(8):
                            mm(pt[:, :], XA3[:, kc, cq * 128:(cq + 1) * 128], Wo[:, kc, :], kc == 0, kc == 7, [b_XA, bWo], pb, inc=(kc == 7))
                        xs = xres3[:, cq, half * 512:(half + 1) * 512]
                        S.op("dve", _I("tensor_tensor", out=xs, in0=pt[:, :], in1=xs, op=ALU.add), reads=[pb, b_x[cq]], writes=[b_x[cq]])
                if "x2" in dbg_d and sq == 0:
                    for cq in range(4):
                        dbg_tap("x2", xres3[:, cq, :], [b_x[cq]], lambda d_, cq=cq: d_[(cg0 + cq) * 128:(cg0 + cq + 1) * 128, :])
                H3T3, b_H3T = FM["MT"]
                for cq in range(4):
                    hn, b_hn = hntile()
                    rms_rows(xres3[:, cq, :], b_x[cq], g_ffn, b_gffn, hn, b_hn)
                    transpose_blocks(hn, [b_hn], 8, lambda k0, n, cq=cq: H3T3[:, k0:k0 + n, cq * 128:(cq + 1) * 128], [b_H3T])
                S.barrier(("pe", "act", "dve", "pool", "sp"))
                for blk in range(6):
                    ncols = min(512, FFN_H - blk * 512)
                    Wga, bWga = wload(wb_fi, "fi", 0, 8, blk * 512, ncols)
                    Wup, bWup = wload(wb_fi, "fi", 0, 8, FFN_H + blk * 512, ncols)
                    for mt in range(ncols // 128):
                        t = blk * 4 + mt
                        pg, pgb = proj(Wga, bWga, mt * 128, lambda kc: H3T3[:, kc, :], [b_H3T], 512)
                        pu, pub = proj(Wup, bWup, mt * 128, lambda kc: H3T3[:, kc, :], [b_H3T], 512)
                        th, bth = thtile()
                        S.op("act", _I("activation", out=th, in_=pg[:, :], func=AF.Tanh, scale=0.5), reads=[pgb], writes=[bth])
                        tm, btm = ftile()
                        S.op("dve", _I("scalar_tensor_tensor", out=tm, in0=th, scalar=1.0, in1=pg[:, :], op0=ALU.add, op1=ALU.mult),
                             reads=[bth, pgb], writes=[btm])
                        S.op("dve", _I("tensor_tensor", out=actT3[:, t, :], in0=tm, in1=pu[:, :], op=ALU.mult), reads=[btm, pub], writes=[b_R1])
                for half in range(2):
                    acc = [(PS[4 + i], PSB[4 + i]) for i in range(4)]
                    for part in range(3):
                        nkc = 8 if part < 2 else 6
                        Wf, bWf = wload(wb_fo, "fo", part * 1024, nkc, half * 512, 512)
                        for cq in range(4):
                            for kk in range(nkc):
                                kc = part * 8 + kk
                                mm(acc[cq][0][:, :], actT3[:, kc, cq * 128:(cq + 1) * 128], Wf[:, kk, :], kc == 0, kc == 21, [b_R1, bWf], acc[cq][1],
                                   inc=(kk == nkc - 1))
                    for cq in range(4):
                        xs = xres3[:, cq, half * 512:(half + 1) * 512]
                        S.op("dve", _I("scalar_tensor_tensor", out=xs, in0=acc[cq][0][:, :], scalar=0.5, in1=xs, op0=ALU.mult, op1=ALU.add),
                             reads=[acc[cq][1], b_x[cq]], writes=[b_x[cq]])
                for cq in range(4):
                    tf, btf = outf[0]
                    rms_rows(xres3[:, cq, :], b_x[cq], g_fin, b_gfin, tf, btf)
                    r0 = tok0 + (cg0 + cq) * 128
                    S.dma("pool", _I("dma_start", out=out_d[r0:r0 + 128, :], in_=tf), reads=[btf], writes=[Buf()], is_output=True)
                S.barrier(("pe", "act", "dve", "pool", "sp"))


        for sq in range(NSEQ):
            tok0 = sq * SEQ
            A.reset(PERSIST_MARK)
            m0 = A.mark()
            g_mem, b_gmem = load_gain(2, "mem")
            wkv = A.alloc(8 * 2048, BF16)
            b_wkv = Buf("wkv")
            wtile_load(v3(wkv, 8), b_wkv, wb_xkv, WB["xkv"], 0, 8, 0, 2048)
            memT = A.alloc(8 * 256, BF16)
            b_memT = [Buf("memT0"), Buf("memT1")]
            for mt in range(2):
                xm = A.alloc(1024, F32)
                b_xm = Buf("xm")
                S.dma("sp", _I("dma_start", out=xm, in_=mem_d[sq * 256 + mt * 128: sq * 256 + (mt + 1) * 128, :]), writes=[b_xm])
                mn = A.alloc(1024, BF16)
                b_mn = Buf("mn")
                rms_rows(xm, b_xm, g_mem, b_gmem, mn, b_mn)
                transpose_blocks(mn, [b_mn], 8, lambda k0, n, mt=mt: v3(memT, 8)[:, k0:k0 + n, mt * 128:(mt + 1) * 128], [b_memT[mt]])
            wkv3 = v3(wkv, 8)
            memT3 = v3(memT, 8)
            for dtile in range(8):
                pt, pb = psum()
                for kc in range(8):
                    mm(pt[:, 0:256], wkv3[:, kc, dtile * 128:(dtile + 1) * 128], memT3[:, kc, :], kc == 0, kc == 7,
                       [b_wkv] + b_memT, pb, inc=(kc == 7))
                S.op("act", _I("copy", out=v3(KxT, 8)[:, dtile, :], in_=pt[:, 0:256]), reads=[pb], writes=[b_KxT])
            for mt in range(2):
                for half in range(2):
                    pt, pb = psum()
                    for kc in range(8):
                        mm(pt[:, :], memT3[:, kc, mt * 128:(mt + 1) * 128], wkv3[:, kc, 1024 + half * 512:1024 + (half + 1) * 512],
                           kc == 0, kc == 7, [b_wkv] + b_memT, pb, inc=(kc == 7))
                    S.op("act", _I("copy", out=v3(Vx, 2)[:, mt, half * 512:(half + 1) * 512], in_=pt[:, :]),
                         reads=[pb], writes=[b_Vx])
            S.barrier(("pe", "act", "dve", "pool", "sp"))
            A.reset(m0)
            if sq == 0:
                cast_weight("bs", w_bs_d, wb_bs, 2048, 1024)
                cast_weight("ba", w_ba_d, wb_ba, 1024, 1024)
                cast_weight("mo", w_mo_d, wb_mo, 1024, 1024)
                cast_weight("xq", w_xq_d, wb_xq, 1024, 1024)
                cast_weight("xo", w_xo_d, wb_xo, 1024, 1024)
                cast_weight("fi", w_fi_d, wb_fi, 1024, 256)
                cast_weight("fo", w_fo_d, wb_fo, FFN_H, 704)

            hT = A.alloc(8 * SEQ, BF16)
            hT3 = v3(hT, 8)
            b_hT = [Buf("hT%d" % c) for c in range(NCH)]
            m1 = A.mark()
            g_mix, b_gmix = load_gain(0, "mix")
            xts = [(A.alloc(1024, F32), Buf("xt%d" % i)) for i in range(2)]
            hns = [(A.alloc(1024, BF16), Buf("hn%d" % i)) for i in range(2)]
            for c in range(NCH):
                xt, b_xt = xts[c % 2]
                hn, b_hn = hns[c % 2]
                S.dma("sp", _I("dma_start", out=xt, in_=x_d[tok0 + c * 128: tok0 + (c + 1) * 128, :]), writes=[b_xt])
                rms_rows(xt, b_xt, g_mix, b_gmix, hn, b_hn)
                transpose_blocks(hn, [b_hn], 8, lambda k0, n, c=c: hT3[:, k0:k0 + n, c * 128:(c + 1) * 128], [b_hT[c]])
            if "h" in dbg_d and sq == 0:
                for kc in range(8):
                    tmpf = A.alloc(SEQ, F32)
                    bt = Buf()
                    S.op("dve", _I("tensor_copy", out=tmpf, in_=hT3[:, kc, :]), reads=b_hT, writes=[bt])
                    dbg_tap("h", tmpf, [bt], lambda d_, kc=kc: d_[kc * 128:(kc + 1) * 128, :])
            S.barrier(("pe", "act", "dve", "pool", "sp"))
            A.reset(m1)

            if STAGE >= 2:
                ssd_phase(sq, tok0, hT3, b_hT)
            S.barrier(("pe", "act", "dve", "pool", "sp"))
            A.reset(PERSIST_MARK)
            if STAGE >= 3:
                token_phase(sq, tok0)
            S.barrier(("pe", "act", "dve", "pool", "sp"))

        S.finish()
        S.emit(nc, st)
    return nc, A.peak, S


def _const_tables(SEQ):
    ident = np.eye(128, dtype=np.float32)
    triF = np.triu(np.ones((128, 128), np.float32))
    triB = np.tril(np.ones((128, 128), np.float32))
    ones = np.ones((128, 128), np.float32)
    maskF = np.where(triF > 0, 0.0, -30000.0).astype(np.float32)
    maskB = np.where(triB > 0, 0.0, -30000.0).astype(np.float32)
    cst = np.concatenate([ident, triF, triB, ones, maskF, maskB], axis=1)
    half = 32
    inv_freq = (np.float32(10000.0) ** (-np.arange(half, dtype=np.float32) / np.float32(half))).astype(np.float32)
    ang = np.arange(SEQ, dtype=np.float32)[:, None] * inv_freq[None, :]
    cos = np.cos(ang).astype(np.float32)
    sin = np.sin(ang).astype(np.float32)
    p = np.arange(128)
    d = p % 64
    cosT = cos[:, d % 32].T
    sgn = np.where(d < 32, -1.0, 1.0).astype(np.float32)
    sinT = (sin[:, d % 32] * sgn[None, :]).T
    rope = np.ascontiguousarray(np.concatenate([cosT, sinT], axis=1), dtype=np.float32)
    return np.ascontiguousarray(cst), rope


def _win_cols():
    cols = list(range(0, 2048)) + list(range(2048, 5120))
    for g in range(4):
        cols += list(range(5120 + 8 * g, 5120 + 8 * g + 8)) + list(range(5152 + 8 * g, 5152 + 8 * g + 8))
    qb, kb, vb, gb = 5184, 6208, 6464, 6720
    cols += list(range(qb, qb + 1024))
    cols += [qb + h * 64 + (d + 32) % 64 for h in range(16) for d in range(64)]
    for j in range(4):
        cols += list(range(kb + j * 64, kb + (j + 1) * 64)) * 2
    for j in range(4):
        cols += [kb + j * 64 + (d + 32) % 64 for d in range(64)] * 2
    cols += list(range(vb, vb + 256))
    cols += list(range(gb, gb + 2048))
    assert len(cols) == WIN
    return np.asarray(cols)


def make_shared(inp, SEQ):
    f = lambda a: np.ascontiguousarray(a, dtype=np.float32)
    cst, rope = _const_tables(SEQ)
    cw = inp["conv_w"][0]
    cb = inp["conv_b"][0]
    convp = np.zeros((128, 24, 8), np.float32)
    convp[:, :, 0:5] = cw.reshape(5, 24, 128).transpose(2, 1, 0)
    convp[:, :, 5] = cb.reshape(24, 128).T
    gm = lambda v: np.concatenate([np.concatenate([v[0][8 * g:8 * g + 8], v[1][8 * g:8 * g + 8]]) for g in range(4)])
    vecs = np.zeros((1, 192), np.float32)
    vecs[0, 0:64] = gm((inp["dt_bias_fwd"][0], inp["dt_bias_bwd"][0]))
    vecs[0, 64:128] = gm((inp["a_log_fwd"][0], inp["a_log_bwd"][0]))
    vecs[0, 128:160] = inp["d_skip"][0]
    vecs[0, 160:176] = inp["attn_sink"][0]
    gains = np.stack([inp["norm_mix_g"][0], inp["norm_xattn_g"][0], inp["norm_mem_g"][0], inp["norm_ffn_g"][0],
                      inp["norm_final_g"]], axis=0)
    return {
        "w_in_r": f(inp["w_in"][0][:, _win_cols()]),
        "w_bs": f(inp["w_branch_ssd"][0]), "w_ba": f(inp["w_branch_attn"][0]), "w_mo": f(inp["w_mix_out"][0]),
        "w_xq": f(inp["w_xattn_q"][0]), "w_xkv": f(inp["w_xattn_kv"][0]), "w_xo": f(inp["w_xattn_out"][0]),
        "w_fi": f(inp["w_ffn_in"][0]), "w_fo": f(inp["w_ffn_out"][0]),
        "gains": f(gains), "convp": f(convp.reshape(128, 192)), "vecs": f(vecs),
        "ssdg": f(inp["ssd_norm_g"][0][None, :]), "cst": cst, "rope": rope,
    }


def make_inputs(inp, SEQ, NSEQ, core, shared=None):
    shared = shared if shared is not None else make_shared(inp, SEQ)
    b0 = core * NSEQ
    m = dict(shared)
    m["x"] = np.ascontiguousarray(inp["x"][b0:b0 + NSEQ, :SEQ].reshape(NSEQ * SEQ, 1024), dtype=np.float32)
    m["mem"] = np.ascontiguousarray(inp["mem"][b0:b0 + NSEQ].reshape(NSEQ * 256, 1024), dtype=np.float32)
    return m


_PROG = {}


def kernel(**inputs):
    SEQ, NSEQ, NCORES = 2048, 2, 8
    inp = {k: np.asarray(v) for k, v in inputs.items()}
    if "prog" not in _PROG:
        _PROG["prog"] = build_program(SEQ, NSEQ)[0]
    nc = _PROG["prog"]
    shared = make_shared(inp, SEQ)
    in_maps = [make_inputs(inp, SEQ, NSEQ, c, shared) for c in range(NCORES)]
    res = run_bass_kernel_spmd(nc, in_maps, core_ids=list(range(NCORES)))
    outs = [np.asarray(r["out"]).reshape(NSEQ, SEQ, 1024) for r in res.results]
    return np.concatenate(outs, axis=0).astype(np.float32)
```

```python
import math
from contextlib import ExitStack
import numpy as np
import concourse.bass as bass
import concourse.mybir as mybir
from concourse.bass_utils import run_bass_kernel_spmd

F32 = mybir.dt.float32
BF16 = mybir.dt.bfloat16
U8 = mybir.dt.uint8
AF = mybir.ActivationFunctionType
ALU = mybir.AluOpType
AX = mybir.AxisListType

D_MODEL = 1024
EPS = 1e-6
FFN_H = 2816
OZ, OXBC, ODT, OQ, OQP, OK_, OKP, OV, OG = 0, 2048, 5120, 5184, 6208, 7232, 7744, 8256, 8512
WIN = 10560


class Buf:
    __slots__ = ("name", "w", "r")

    def __init__(self, name=""):
        self.name = name
        self.w = None
        self.r = {}


class Ev:
    __slots__ = ("sem", "val", "clock", "eng")

    def __init__(self, eng):
        self.sem = None
        self.val = None
        self.clock = None
        self.eng = eng


class Sched:
    ENGS = ("pe", "act", "dve", "pool", "sp")
    EPOCH = 16000

    def __init__(self, n_dma_sems=48):
        self.streams = {e: [] for e in self.ENGS}
        self.cnt = {e: 0 for e in self.ENGS}
        self.know = {e: {} for e in self.ENGS}
        self.pending = {e: [] for e in self.ENGS}
        self.last = {e: None for e in self.ENGS}
        self.n_dma = n_dma_sems
        self.dma_cnt = [0] * n_dma_sems
        self.dma_last = [None] * n_dma_sems
        self.dma_rr = 0
        self.dma_rr_sw = 0
        self.dma_open = []
        self.out_events = []
        self.epoch = {e: 0 for e in self.ENGS}
        self.nops = 0

    def _need(self, eng, ev, waits):
        if ev is None:
            return
        if ev.sem is None and ev.eng == eng:
            return
        assert ev.sem is not None, "dependency on unresolved (non-inc) op"
        k = self.know[eng]
        if k.get(ev.sem, 0) >= ev.val:
            return
        if ev.sem[0] != "dma":
            for ep in range(ev.sem[1] + 1, self.epoch[ev.sem[0]] + 1):
                if k.get((ev.sem[0], ep), 0) >= 1:
                    return
        waits.append((ev.sem, ev.val))
        for s, v in ev.clock.items():
            if k.get(s, 0) < v:
                k[s] = v

    @staticmethod
    def _dedupe(waits):
        best = {}
        for s, v in waits:
            if best.get(s, 0) < v:
                best[s] = v
        return list(best.items())

    def _deps(self, eng, reads, writes):
        waits = []
        for b in reads:
            self._need(eng, b.w, waits)
        for b in writes:
            if b.w is not None and (b.w.eng != eng or eng != "pe"):
                self._need(eng, b.w, waits)
            for e2, ev in b.r.items():
                if e2 != eng or eng != "pe":
                    self._need(eng, ev, waits)
        return self._dedupe(waits)

    def _mark(self, ev, eng, reads, writes):
        for b in reads:
            b.r[eng] = ev
        for b in writes:
            b.w = ev
            b.r = {}

    def op(self, eng, fn, reads=(), writes=(), inc=True):
        self.nops += 1
        waits = self._deps(eng, reads, writes)
        if not inc:
            ev = Ev(eng)
            self.pending[eng].append(ev)
            self._mark(ev, eng, reads, writes)
            self.streams[eng].append((waits, fn, None))
            return ev
        self.cnt[eng] += 1
        if self.cnt[eng] > self.EPOCH:
            self.cnt[eng] = 1
            self.epoch[eng] += 1
        n = self.cnt[eng]
        ev = Ev(eng)
        ev.sem = (eng, self.epoch[eng])
        ev.val = n
        ev.clock = dict(self.know[eng])
        ev.clock[ev.sem] = n
        for p in self.pending[eng]:
            p.sem, p.val, p.clock = ev.sem, ev.val, ev.clock
        self.pending[eng] = []
        self._mark(ev, eng, reads, writes)
        self.streams[eng].append((waits, fn, (ev.sem, 1)))
        self.last[eng] = ev
        return ev

    def dma(self, eng, fn, reads=(), writes=(), is_output=False):
        self.nops += 1
        waits = self._deps(eng, reads, writes)
        n_sw = self.n_dma // 3
        if eng == "pool":
            j = self.dma_rr_sw
            self.dma_rr_sw = (j + 1) % n_sw
        else:
            j = n_sw + self.dma_rr
            self.dma_rr = (self.dma_rr + 1) % (self.n_dma - n_sw)
        prev = self.dma_last[j]
        if prev is not None:
            self._need(eng, prev, waits)
            waits = self._dedupe(waits)
        self.dma_cnt[j] += 16
        ev = Ev(eng)
        ev.sem = ("dma", j)
        ev.val = self.dma_cnt[j]
        ev.clock = dict(self.know[eng])
        ev.clock[ev.sem] = ev.val
        self.dma_last[j] = ev
        self._mark(ev, eng, reads, writes)
        self.streams[eng].append((waits, fn, (ev.sem, 16)))
        self.dma_open.append(ev)
        if is_output:
            self.out_events.append(ev)
        return ev

    def barrier(self, engines=("pe", "act", "dve", "pool")):
        for e in self.ENGS:
            assert not self.pending[e], "barrier with pending non-inc ops"
        for e in engines:
            waits = []
            for e2 in ("pe", "act", "dve", "pool"):
                if e2 != e:
                    self._need(e, self.last[e2], waits)
            for ev in self.dma_open:
                self._need(e, ev, waits)
            self.streams[e].append((self._dedupe(waits), None, None))
        self.dma_open = []

    def finish(self):
        waits = []
        for ev in self.out_events:
            self._need("sp", ev, waits)
        self.streams["sp"].append((self._dedupe(waits), None, None))

    def emit(self, nc, stack):
        sems = {}
        for e in ("pe", "act", "dve", "pool"):
            for ep in range(self.epoch[e] + 1):
                sems[(e, ep)] = stack.enter_context(nc.semaphore("s_%s%d" % (e, ep)))
        for j in range(self.n_dma):
            sems[("dma", j)] = stack.enter_context(nc.semaphore("s_dma%d" % j))
        block = stack.enter_context(nc.Block())
        streams = self.streams

        def run(handle, lst):
            for waits, fn, inc in lst:
                for s, v in waits:
                    handle.wait_ge(sems[s], v)
                if fn is None:
                    continue
                ins = fn(handle)
                if inc is not None:
                    ins.then_inc(sems[inc[0]], inc[1])

        @block.sync
        def _(e):
            run(e, streams["sp"])

        @block.tensor
        def _(e):
            run(e, streams["pe"])

        @block.scalar
        def _(e):
            run(e, streams["act"])

        @block.vector
        def _(e):
            run(e, streams["dve"])

        @block.gpsimd
        def _(e):
            run(e, streams["pool"])


class LazyReg:
    def __init__(self, value):
        self.value = value
        self.reg = None

    def get(self, e):
        if self.reg is None:
            self.reg = e.to_reg(self.value)
        return self.reg


def _I(name, *args, **kwargs):
    def f(e):
        kw = {k: (v.get(e) if isinstance(v, LazyReg) else v) for k, v in kwargs.items()}
        return getattr(e, name)(*args, **kw)
    return f


def _bytes(dt):
    return 4 if dt == F32 else 2


class Arena:
    def __init__(self, ap_u8, size):
        self.ap = ap_u8
        self.size = size
        self.off = 0
        self.peak = 0

    def alloc(self, n_elems, dt, parts=128):
        nb = n_elems * _bytes(dt)
        off = (self.off + 63) // 64 * 64
        assert off + nb <= self.size, "SBUF arena overflow: need %d have %d" % (off + nb, self.size)
        self.off = off + nb
        self.peak = max(self.peak, self.off)
        return self.ap[0:parts, off:off + nb].bitcast(dt)

    def mark(self):
        return self.off

    def reset(self, m):
        self.off = m


def v3(ap, a):
    return ap.rearrange("p (a b) -> p a b", a=a)


def v4(ap, a, b):
    return ap.rearrange("p (a b c) -> p a b c", a=a, b=b)


def build_program(SEQ, NSEQ, dbg=None, STAGE=3):
    NCH = SEQ // 128
    NSB = SEQ // 512
    NTOK = NSEQ * SEQ
    nc = bass.Bass("TRN2", target_bir_lowering=False)
    S = Sched()
    dbg = dbg or {}

    def din(name, shape, dt=F32):
        return nc.dram_tensor(name, shape, dt, kind="ExternalInput").ap()

    def dscr(name, shape, dt):
        return nc.dram_tensor(name, shape, dt, kind="Internal").ap()

    x_d = din("x", [NTOK, 1024])
    mem_d = din("mem", [NSEQ * 256, 1024])
    w_in_d = din("w_in_r", [1024, WIN])
    w_bs_d = din("w_bs", [2048, 1024])
    w_ba_d = din("w_ba", [1024, 1024])
    w_mo_d = din("w_mo", [1024, 1024])
    w_xq_d = din("w_xq", [1024, 1024])
    w_xkv_d = din("w_xkv", [1024, 2048])
    w_xo_d = din("w_xo", [1024, 1024])
    w_fi_d = din("w_fi", [1024, 2 * FFN_H])
    w_fo_d = din("w_fo", [FFN_H, 1024])
    gains_d = din("gains", [5, 1024])
    convp_d = din("convp", [128, 24 * 8])
    vecs_d = din("vecs", [1, 192])
    ssdg_d = din("ssdg", [1, 2048])
    cst_d = din("cst", [128, 6 * 128])
    rope_d = din("rope", [128, 2 * SEQ])
    out_d = nc.dram_tensor("out", [NTOK, 1024], F32, kind="ExternalOutput").ap()
    dbg_d = {k: nc.dram_tensor("dbg_" + k, list(shp), F32, kind="ExternalOutput").ap() for k, shp in dbg.items()}

    wb_in = dscr("wb_in", [1024, WIN], BF16)
    wb_bs = dscr("wb_bs", [2048, 1024], BF16)
    wb_ba = dscr("wb_ba", [1024, 1024], BF16)
    wb_mo = dscr("wb_mo", [1024, 1024], BF16)
    wb_xq = dscr("wb_xq", [1024, 1024], BF16)
    wb_xkv = dscr("wb_xkv", [1024, 2048], BF16)
    wb_xo = dscr("wb_xo", [1024, 1024], BF16)
    wb_fi = dscr("wb_fi", [1024, 2 * FFN_H], BF16)
    wb_fo = dscr("wb_fo", [FFN_H, 1024], BF16)
    yn_scr = dscr("yn_scr", [NTOK, 2048], BF16)
    ac_scr = dscr("ac_scr", [2 * NCH * 4096], F32)

    with ExitStack() as st:
        ARENA_BYTES = 212736
        arena_t = st.enter_context(nc.sbuf_tensor("arena", [128, ARENA_BYTES], U8))
        A = Arena(arena_t, ARENA_BYTES)
        PS = []
        PSB = []
        for i in range(8):
            t = st.enter_context(nc.psum_tensor("psb%d" % i, [128, 512], F32))
            PS.append(t)
            PSB.append(Buf("ps%d" % i))
        ps_rr = [0]

        ps_sub = [list(range(8))]
        ps_cnt = {}

        def psum(sub=None):
            sub = tuple(sub if sub is not None else ps_sub[0])
            k = ps_cnt.get(sub, 0)
            ps_cnt[sub] = k + 1
            i = sub[k % len(sub)]
            return PS[i], PSB[i]

        WB = {}

        def cast_weight(key, src, dst, rows, step):
            bl = []
            for r0 in range(0, rows, step):
                r1 = min(rows, r0 + step)
                b = Buf("w_%s_%d" % (key, r0))
                S.dma("pool", _I("dma_start", out=dst[r0:r1, :], in_=src[r0:r1, :]), writes=[b])
                bl.append(b)
            WB[key] = bl

        cast_weight("xkv", w_xkv_d, wb_xkv, 1024, 512)
        cast_weight("in", w_in_d, wb_in, 1024, 128)

        cst_f = A.alloc(768, F32)
        b_cst = Buf("cst")
        S.dma("sp", _I("dma_start", out=cst_f, in_=cst_d[:, :]), writes=[b_cst])
        ident_f = cst_f[:, 0:128]
        triF = cst_f[:, 128:256]
        triB = cst_f[:, 256:384]
        ones_f = cst_f[:, 384:512]
        maskF = cst_f[:, 512:640]
        maskB = cst_f[:, 640:768]
        cst_b = A.alloc(512, BF16)
        b_cstb = Buf("cstb")
        S.op("dve", _I("tensor_copy", out=cst_b, in_=cst_f[:, 0:512]), reads=[b_cst], writes=[b_cstb])
        ident_b = cst_b[:, 0:128]
        ones_b = cst_b[:, 384:512]
        vecs = A.alloc(192, F32)
        b_vecs = Buf("vecs")
        S.dma("sp", _I("dma_start", out=vecs, in_=vecs_d[0:1, :].partition_broadcast(128)[:, 0, :]), writes=[b_vecs])
        convp = A.alloc(24 * 8, F32)
        b_convp = Buf("convp")
        S.dma("sp", _I("dma_start", out=convp, in_=convp_d[:, :]), writes=[b_convp])
        negA = A.alloc(64, F32)
        esink = A.alloc(16, F32)
        b_negA = Buf("negA")
        b_esink = Buf("esink")
        S.op("act", _I("activation", out=negA, in_=vecs[:, 64:128], func=AF.Exp), reads=[b_vecs], writes=[b_negA])
        S.op("dve", _I("tensor_scalar", out=negA, in0=negA, scalar1=-1.0, scalar2=None, op0=ALU.mult), reads=[b_negA], writes=[b_negA])
        S.op("act", _I("activation", out=esink, in_=vecs[:, 160:176], func=AF.Exp), reads=[b_vecs], writes=[b_esink])
        KxT = A.alloc(8 * 256, BF16)
        Vx = A.alloc(2 * 1024, BF16)
        b_KxT = Buf("KxT")
        b_Vx = Buf("Vx")
        ss_t = [A.alloc(2, F32) for _ in range(4)]
        ss_b = [Buf("ss%d" % i) for i in range(4)]
        ss_rr = [0]
        junk = A.alloc(1024, BF16)
        b_junk = Buf("junk")
        neghalf = A.alloc(2, F32)
        b_neghalf = Buf("neghalf")
        S.op("pool", _I("memset", neghalf, -0.5), writes=[b_neghalf])
        PERSIST_MARK = A.mark()

        def dbg_tap(name, src_ap, src_bufs, dst_slice):
            if name in dbg_d:
                S.dma("pool", _I("dma_start", out=dst_slice(dbg_d[name]), in_=src_ap), reads=src_bufs,
                      writes=[Buf()], is_output=True)

        def rms_rows(xt, xb, g_ap, g_buf, out_bf, out_buf, dim=1024, eps=EPS, pre=1.0, post=1.0):
            i = ss_rr[0]
            ss_rr[0] = (i + 1) % 4
            ss, sb_ = ss_t[i], ss_b[i]
            S.op("act", _I("activation", out=junk[:, 0:dim], in_=xt, func=AF.Square, accum_out=ss[:, 0:1]),
                 reads=[xb], writes=[b_junk, sb_])
            S.op("pool", _I("tensor_scalar", out=ss[:, 1:2], in0=ss[:, 0:1], scalar1=pre / dim, scalar2=eps,
                                                   op0=ALU.mult, op1=ALU.add), reads=[sb_], writes=[sb_])
            S.op("pool", _I("tensor_tensor", out=ss[:, 1:2], in0=ss[:, 1:2], in1=neghalf[:, 0:1], op=ALU.pow),
                 reads=[sb_, b_neghalf], writes=[sb_])
            if post != 1.0:
                S.op("pool", _I("tensor_scalar", out=ss[:, 1:2], in0=ss[:, 1:2], scalar1=post, scalar2=None, op0=ALU.mult),
                     reads=[sb_], writes=[sb_])
            S.op("dve", _I("scalar_tensor_tensor", out=out_bf, in0=xt, scalar=ss[:, 1:2], in1=g_ap,
                                                         op0=ALU.mult, op1=ALU.mult),
                 reads=[xb, sb_, g_buf], writes=[out_buf])

        def transpose_blocks(src_bf, src_bufs, nblk, dst_view_fn, dst_bufs, evac="act", dst_bufs_fn=None):
            k = 0
            while k < nblk:
                n = min(8, nblk - k)
                pt, pb = psum()
                ptb = pt[:].bitcast(BF16)
                for j in range(n):
                    S.op("pe", _I("transpose", out=ptb[:, j * 128:(j + 1) * 128],
                                                                       in_=src_bf[:, (k + j) * 128:(k + j + 1) * 128],
                                                                       identity=ident_b),
                         reads=list(src_bufs) + [b_cstb], writes=[pb], inc=(j == n - 1))
                dst = dst_view_fn(k, n)
                srcv = v3(ptb[:, 0:n * 128], n)
                if dst_bufs_fn is not None:
                    dst_bufs = dst_bufs_fn(k, n)
                if evac == "act":
                    S.op("act", _I("copy", out=dst, in_=srcv), reads=[pb], writes=dst_bufs)
                else:
                    S.op(evac, _I("tensor_copy", out=dst, in_=srcv), reads=[pb], writes=dst_bufs)
                k += n

        def wtile_load(dst, dst_buf, wsrc, wbufs, r0, nkc, c0, ncols, eng="sp"):
            src = wsrc[r0:r0 + nkc * 128, c0:c0 + ncols].rearrange("(k p) n -> p k n", p=128)
            S.dma(eng, _I("dma_start", out=dst, in_=src), reads=wbufs, writes=[dst_buf])

        gains_t = {}

        def load_gain(idx, name):
            g = A.alloc(1024, F32)
            b = Buf("g_" + name)
            S.dma("sp", _I("dma_start", out=g, in_=gains_d[idx:idx + 1, :].partition_broadcast(128)[:, 0, :]), writes=[b])
            gains_t[name] = (g, b)
            return g, b

        def mm(out, lhsT, rhs, start, stop, reads, wbuf, inc, skip=False):
            if skip:
                S.op("pe", _I("matmul", out, lhsT=lhsT, rhs=rhs, start=start, stop=stop, skip_group_check=True),
                     reads=reads, writes=[wbuf], inc=inc)
            else:
                S.op("pe", _I("matmul", out, lhsT=lhsT, rhs=rhs, start=start, stop=stop),
                     reads=reads, writes=[wbuf], inc=inc)

        NEGBIG = LazyReg(-30000.0)

        def bc8(ap8):
            return ap8.unsqueeze(2).to_broadcast([128, 8, 64])

        acw = ac_scr.rearrange("(d c h l) -> d h c l", d=2, c=NCH, h=32, l=128)
        acr = ac_scr.rearrange("(r k) -> r k", k=1024)
        b_ac = [[Buf("ac%d_%d" % (d, c)) for c in range(NCH)] for d in range(2)]
        b_yn = [[Buf("yn%d_%d" % (c, g)) for g in range(4)] for c in range(NSEQ * NCH)]

        def ssd_phase(sq, tok0, hT3, b_hT):
            convp3 = v3(convp, 24)
            for g in range(4):
                mg = A.mark()
                Wz = A.alloc(8 * 512, BF16)
                Wz3 = v3(Wz, 8)
                b_Wz = Buf("Wz")
                wtile_load(Wz3, b_Wz, wb_in, WB["in"], 0, 8, OZ + g * 512, 512)
                Wdt = A.alloc(8 * 16, BF16)
                Wdt3 = v3(Wdt, 8)
                b_Wdt = Buf("Wdt")
                wtile_load(Wdt3, b_Wdt, wb_in, WB["in"], 0, 8, ODT + g * 16, 16)
                ssdg = A.alloc(512, F32)
                b_ssdg = Buf("ssdg")
                S.dma("sp", _I("dma_start", out=ssdg, in_=ssdg_d[0:1, g * 512:(g + 1) * 512].partition_broadcast(128)[:, 0, :]),
                      writes=[b_ssdg])

                NV = NCH * 16

                def small():
                    return A.alloc(NV, F32), Buf("small")

                dtv, b_dtv = small()
                dA, b_dA = small()
                a_sb, b_a = small()
                nega, b_nega = small()
                ea, b_ea = small()
                te, b_te = small()
                cd, b_cd = small()
                sdte, b_sdte = small()
                dtv3, dA3, a3, nega3, ea3, te3, cd3, sdte3 = [v3(t, NCH) for t in (dtv, dA, a_sb, nega, ea, te, cd, sdte)]
                pt, pb = psum()
                for c in range(NCH):
                    for kc in range(8):
                        mm(pt[:, c * 16:(c + 1) * 16], hT3[:, kc, c * 128:(c + 1) * 128], Wdt3[:, kc, :], kc == 0, kc == 7,
                           [b_hT[c], b_Wdt], pb, inc=(kc == 7 and c == NCH - 1))
                bias_g = vecs[:, g * 16:(g + 1) * 16]
                S.op("dve", _I("tensor_tensor", out=dtv3, in0=v3(pt[:, 0:NV], NCH),
                                                      in1=bias_g.unsqueeze(1).to_broadcast([128, NCH, 16]), op=ALU.add),
                     reads=[pb, b_vecs], writes=[b_dtv])
                S.op("act", _I("activation", out=dtv, in_=dtv, func=AF.Exp), reads=[b_dtv], writes=[b_dtv])
                S.op("act", _I("activation", out=dtv, in_=dtv, func=AF.Ln, bias=1.0), reads=[b_dtv], writes=[b_dtv])
                negA_g = negA[:, g * 16:(g + 1) * 16]
                S.op("dve", _I("tensor_tensor", out=dA3, in0=dtv3, in1=negA_g.unsqueeze(1).to_broadcast([128, NCH, 16]), op=ALU.mult),
                     reads=[b_dtv, b_negA], writes=[b_dA])
                pa, pab = psum()
                ptot, ptotb = psum()
                for c in range(NCH):
                    mm(pa[:, c * 16:c * 16 + 8], triF, dA3[:, c, 0:8], True, True, [b_cst, b_dA], pab, inc=False)
                    mm(pa[:, c * 16 + 8:c * 16 + 16], triB, dA3[:, c, 8:16], True, True, [b_cst, b_dA], pab, inc=(c == NCH - 1))
                for c in range(NCH):
                    mm(ptot[:, c * 16:(c + 1) * 16], ones_f, dA3[:, c, :], True, True, [b_cst, b_dA], ptotb, inc=(c == NCH - 1))
                S.op("act", _I("copy", out=a_sb, in_=pa[:, 0:NV]), reads=[pab], writes=[b_a])
                S.op("dve", _I("tensor_scalar", out=nega, in0=a_sb, scalar1=-1.0, scalar2=None, op0=ALU.mult), reads=[b_a], writes=[b_nega])
                S.op("act", _I("activation", out=ea, in_=a_sb, func=AF.Exp), reads=[b_a], writes=[b_ea])
                S.op("dve", _I("tensor_tensor", out=te, in0=ptot[:, 0:NV], in1=a_sb, op=ALU.subtract), reads=[ptotb, b_a], writes=[b_te])
                S.op("act", _I("activation", out=te, in_=te, func=AF.Exp), reads=[b_te], writes=[b_te])
                S.op("act", _I("activation", out=cd, in_=ptot[:, 0:NV], func=AF.Exp), reads=[ptotb], writes=[b_cd])
                S.op("dve", _I("tensor_tensor", out=sdte, in0=dtv, in1=te, op=ALU.mult), reads=[b_dtv, b_te], writes=[b_sdte])
                aTs = [(A.alloc(512, F32, parts=8), Buf("aT%d" % i)) for i in range(2)]
                for d in range(2):
                    tri = triF if d == 0 else triB
                    for cb in range(0, NCH, 4):
                        pT_, pTb = psum()
                        for j in range(4):
                            mm(pT_[0:8, j * 128:(j + 1) * 128], dA3[:, cb + j, d * 8:(d + 1) * 8], tri, True, True, [b_cst, b_dA], pTb, inc=(j == 3))
                        aT, b_aT = aTs[(d * (NCH // 4) + cb // 4) % 2]
                        S.op("act", _I("copy", out=aT, in_=pT_[0:8, :]), reads=[pTb], writes=[b_aT])
                        S.dma("pool", _I("dma_start", out=acw[d, 8 * g:8 * g + 8, cb:cb + 4, :], in_=v3(aT, 4)),
                              reads=[b_aT], writes=[b_ac[d][cb + j] for j in range(4)])

                BT = A.alloc(SEQ, BF16)
                b_BT = [Buf("BT%d" % s_) for s_ in range(NSB)]
                CT = A.alloc(SEQ, BF16)
                b_CT = [Buf("CT%d" % s_) for s_ in range(NSB)]
                x_tok = A.alloc(NCH * 512, BF16)
                x_tok3 = v3(x_tok, NCH)
                b_xtok = [Buf("xtok%d" % c) for c in range(NCH)]
                B_tok = A.alloc(NCH * 128, BF16)
                B_tok3 = v3(B_tok, NCH)
                b_Btok = [Buf("Btok%d" % c) for c in range(NCH)]
                mG2 = A.mark()
                Wg = A.alloc(8 * 768, BF16)
                Wg3 = v3(Wg, 8)
                b_Wg = [Buf("Wg%d" % i) for i in range(3)]
                wtile_load(Wg3[:, :, 0:512], b_Wg[0], wb_in, WB["in"], 0, 8, OXBC + g * 512, 512)
                wtile_load(Wg3[:, :, 512:640], b_Wg[1], wb_in, WB["in"], 0, 8, OXBC + 2048 + g * 128, 128)
                wtile_load(Wg3[:, :, 640:768], b_Wg[2], wb_in, WB["in"], 0, 8, OXBC + 2560 + g * 128, 128)
                Dg = A.alloc(30 * 128, BF16)
                Dg3 = v3(Dg, 30)
                b_Dg = Buf("Dg")

                def ctg(j):
                    return g * 4 + j if j < 4 else (16 + g if j == 4 else 20 + g)

                for j in range(6):
                    for k in range(5):
                        S.op("dve", _I("tensor_scalar", out=Dg3[:, j * 5 + k, :], in0=ident_b,
                                                                        scalar1=convp3[:, ctg(j), k:k + 1], scalar2=None, op0=ALU.mult),
                             reads=[b_cstb, b_convp], writes=[b_Dg])
                upad = []
                for i in range(2):
                    u = A.alloc(SEQ + 4, BF16)
                    bu = [Buf("u%d_%d" % (i, s_)) for s_ in range(NSB)] + [Buf("upad%d" % i)]
                    S.op("pool", _I("memset", u[:, 0:2], 0.0), writes=[bu[NSB]], inc=False)
                    S.op("pool", _I("memset", u[:, SEQ + 2:SEQ + 4], 0.0), writes=[bu[NSB]])
                    upad.append((u, bu))
                xTt = [(A.alloc(SEQ, BF16), [Buf("xTt%d_%d" % (i, s_)) for s_ in range(NSB)]) for i in range(2)]
                for j in range(6):
                    u, bu = upad[j % 2]
                    wcol = j * 128
                    wb_ = b_Wg[0] if j < 4 else b_Wg[j - 3]
                    for sb in range(NSB):
                        pt, pb = psum()
                        for kc in range(8):
                            mm(pt[:, :], Wg3[:, kc, wcol:wcol + 128], hT3[:, kc, sb * 512:(sb + 1) * 512], kc == 0, kc == 7,
                               [wb_] + b_hT[sb * 4:sb * 4 + 4], pb, inc=(kc == 7))
                        S.op("act", _I("copy", out=u[:, 2 + sb * 512:2 + (sb + 1) * 512], in_=pt[:, :]),
                             reads=[pb], writes=[bu[sb]])
                    if j < 4:
                        dst, b_dst = xTt[j % 2]
                    elif j == 4:
                        dst, b_dst = BT, b_BT
                    else:
                        dst, b_dst = CT, b_CT
                    for sb in range(NSB):
                        pt, pb = psum()
                        rd = [b_Dg, bu[NSB]] + [bu[s_] for s_ in range(max(0, sb - 1), min(NSB, sb + 2))]
                        for k in range(5):
                            mm(pt[:, :], Dg3[:, j * 5 + k, :], u[:, sb * 512 + k:sb * 512 + k + 512], k == 0, k == 4, rd, pb, inc=(k == 4))
                        S.op("act", _I("activation", out=dst[:, sb * 512:(sb + 1) * 512], in_=pt[:, :], func=AF.Silu,
                                                                                     bias=convp3[:, ctg(j), 5:6], scale=1.0),
                             reads=[pb, b_convp], writes=[b_dst[sb]])
                    if j < 4:
                        transpose_blocks(dst, b_dst, NCH, lambda k0, n, j=j: x_tok3[:, k0:k0 + n, j * 128:(j + 1) * 128],
                                         None, evac="dve", dst_bufs_fn=lambda k0, n: b_xtok[k0:k0 + n])
                    elif j == 4:
                        transpose_blocks(dst, b_dst, NCH, lambda k0, n: B_tok3[:, k0:k0 + n, :],
                                         None, evac="dve", dst_bufs_fn=lambda k0, n: b_Btok[k0:k0 + n])

                S.barrier(("pe", "act", "dve", "pool", "sp"))
                A.reset(mG2)
                hbin = A.alloc(NCH * 512, BF16)
                hbin3 = v3(hbin, NCH)
                b_hbin = [Buf("hbin%d" % c) for c in range(NCH)]
                hb = A.alloc(512, F32)
                b_hb = Buf("hb")
                S.op("pool", _I("memset", hb, 0.0), writes=[b_hb])
                xvs = [(A.alloc(512, BF16), Buf("xv%d" % i)) for i in range(3)]
                xv_rr = [0]

                def variant(c, scale_ap8, reads, eng="dve"):
                    i = xv_rr[0]
                    xv_rr[0] = (i + 1) % 3
                    xv, bx = xvs[i]
                    S.op(eng, _I("tensor_tensor", out=v3(xv, 8), in0=v3(x_tok3[:, c, :], 8), in1=bc8(scale_ap8), op=ALU.mult),
                         reads=[b_xtok[c]] + reads, writes=[bx])
                    return xv, bx

                for c in range(NCH - 1, -1, -1):
                    S.op("act", _I("copy", out=hbin3[:, c, :], in_=hb), reads=[b_hb], writes=[b_hbin[c]])
                    if c == 0:
                        break
                    xv, bx = variant(c, sdte3[:, c, 8:16], [b_sdte])
                    pt, pb = psum()
                    mm(pt[:, :], B_tok3[:, c, :], xv, True, True, [b_Btok[c], bx], pb, inc=True)
                    S.op("dve", _I("tensor_tensor", out=v3(hb, 8), in0=v3(hb, 8), in1=bc8(cd3[:, c, 8:16]), op=ALU.mult),
                         reads=[b_hb, b_cd], writes=[b_hb])
                    S.op("dve", _I("tensor_tensor", out=hb, in0=hb, in1=pt[:, :], op=ALU.add), reads=[b_hb, pb], writes=[b_hb])

                hf = A.alloc(512, F32)
                b_hf = Buf("hf")
                hfb = A.alloc(512, BF16)
                b_hfb = Buf("hfb")
                S.op("pool", _I("memset", hf, 0.0), writes=[b_hf])
                S.op("pool", _I("memset", hfb, 0.0), writes=[b_hfb])
                bcs = [[(A.alloc(1024, F32), Buf("bc%d_%d" % (d, i))) for i in range(2)] for d in range(2)]
                Es = [[(A.alloc(1024, BF16), Buf("E%d_%d" % (d, i))) for i in range(2)] for d in range(2)]
                Ms = [[(A.alloc(1024, BF16), Buf("M%d_%d" % (d, i))) for i in range(2)] for d in range(2)]
                f32s = [(A.alloc(512, F32), Buf("f32_%d" % i)) for i in range(4)]
                f_rr = [0]

                def f32tile():
                    i = f_rr[0]
                    f_rr[0] = (i + 1) % 4
                    return f32s[i]

                ths = [(A.alloc(512, F32), Buf("thA%d" % i)) for i in range(2)]
                xAs = [[(A.alloc(512, BF16), Buf("xA%d_%d" % (i, k))) for k in range(3)] for i in range(2)]
                xEs = [(A.alloc(512, BF16), Buf("xE%d" % i)) for i in range(2)]
                yns = [(A.alloc(512, BF16), Buf("yn%d" % i)) for i in range(2)]
                D_g = vecs[:, 128 + g * 8:128 + (g + 1) * 8]

                def variant_to(tile_buf, c, scale_ap8, reads, eng="dve"):
                    xv, bx = tile_buf
                    S.op(eng, _I("tensor_tensor", out=v3(xv, 8), in0=v3(x_tok3[:, c, :], 8), in1=bc8(scale_ap8), op=ALU.mult),
                         reads=[b_xtok[c]] + reads, writes=[bx])
                    return xv, bx

                def stageA(c):
                    par = c % 2
                    EM = []
                    for d in range(2):
                        bc, b_bc = bcs[d][par]
                        row = (d * NCH + c) * 4 + g
                        S.dma("sp", _I("dma_start", out=bc, in_=acr[row:row + 1, :].partition_broadcast(128)[:, 0, :]),
                              reads=[b_ac[d][c]], writes=[b_bc])
                        E, b_E = Es[d][par]
                        E3 = v3(E, 8)
                        bc3 = v3(bc, 8)
                        msk = maskF if d == 0 else maskB
                        S.op("pool", _I("tensor_tensor", out=bc3, in0=bc3, in1=msk.unsqueeze(1).to_broadcast([128, 8, 128]), op=ALU.add),
                             reads=[b_bc, b_cst], writes=[b_bc])
                        for h in range(8):
                            S.op("act", _I("activation", out=E3[:, h, :], in_=bc3[:, h, :], func=AF.Exp,
                                           bias=nega3[:, c, d * 8 + h:d * 8 + h + 1], scale=1.0),
                                 reads=[b_bc, b_nega], writes=[b_E], inc=(h == 7))
                        EM.append((E3, b_E))
                    pcb, pcbb = psum()
                    sbi = c // 4
                    mm(pcb[:, 0:128], BT[:, c * 128:(c + 1) * 128], CT[:, c * 128:(c + 1) * 128], True, True, [b_BT[sbi], b_CT[sbi]], pcbb, inc=True)
                    Mt = []
                    for d in range(2):
                        M, b_M = Ms[d][par]
                        M3 = v3(M, 8)
                        E3, b_E = EM[d]
                        S.op("dve", _I("tensor_tensor", out=M3, in0=E3, in1=pcb[:, 0:128].unsqueeze(1).to_broadcast([128, 8, 128]),
                                       op=ALU.mult), reads=[b_E, pcbb], writes=[b_M])
                        Mt.append((M3, b_M))
                    xdf, b_xdf = variant_to(xAs[par][0], c, dtv3[:, c, 0:8], [b_dtv])
                    xdb, b_xdb = variant_to(xAs[par][1], c, dtv3[:, c, 8:16], [b_dtv])
                    xD, b_xD = variant_to(xAs[par][2], c, D_g, [b_vecs], eng="pool")
                    pZ, pZb = psum()
                    for kc in range(8):
                        mm(pZ[:, :], hT3[:, kc, c * 128:(c + 1) * 128], Wz3[:, kc, :], kc == 0, kc == 7, [b_hT[c], b_Wz], pZb, inc=(kc == 7))
                    th, b_th = ths[par]
                    S.op("act", _I("activation", out=th, in_=pZ[:, :], func=AF.Tanh, scale=0.5), reads=[pZb], writes=[b_th])
                    S.op("dve", _I("scalar_tensor_tensor", out=th, in0=th, scalar=1.0, in1=pZ[:, :], op0=ALU.add, op1=ALU.mult),
                         reads=[b_th, pZb], writes=[b_th])
                    return dict(Mt=Mt, xdf=(xdf, b_xdf), xdb=(xdb, b_xdb), xD=(xD, b_xD), th=(th, b_th), sbi=sbi, par=par)

                def stageB(c, ctx):
                    Mt = ctx["Mt"]
                    xdf, b_xdf = ctx["xdf"]
                    xdb, b_xdb = ctx["xdb"]
                    xD, b_xD = ctx["xD"]
                    th, b_th = ctx["th"]
                    sbi, par = ctx["sbi"], ctx["par"]
                    pY, pYb = psum()
                    mm(pY[:, :], ident_b, xD, True, False, [b_cstb, b_xD], pYb, inc=False, skip=True)
                    for h in range(8):
                        mm(pY[:, h * 64:(h + 1) * 64], Mt[0][0][:, h, :], xdf[:, h * 64:(h + 1) * 64], False, False, [Mt[0][1], b_xdf], pYb, inc=False, skip=True)
                        mm(pY[:, h * 64:(h + 1) * 64], Mt[1][0][:, h, :], xdb[:, h * 64:(h + 1) * 64], False, h == 7, [Mt[1][1], b_xdb], pYb, inc=(h == 7), skip=True)
                    pF, pFb = psum()
                    mm(pF[:, :], CT[:, c * 128:(c + 1) * 128], hfb, True, True, [b_CT[sbi], b_hfb], pFb, inc=True)
                    pB, pBb = psum()
                    mm(pB[:, :], CT[:, c * 128:(c + 1) * 128], hbin3[:, c, :], True, True, [b_CT[sbi], b_hbin[c]], pBb, inc=True)
                    if c < NCH - 1:
                        xef, b_xef = variant_to(xEs[par], c, sdte3[:, c, 0:8], [b_sdte], eng="pool")
                        pS, pSb = psum()
                        mm(pS[:, :], B_tok3[:, c, :], xef, True, True, [b_Btok[c], b_xef], pSb, inc=True)
                        S.op("dve", _I("tensor_tensor", out=v3(hf, 8), in0=v3(hf, 8), in1=bc8(cd3[:, c, 0:8]), op=ALU.mult),
                             reads=[b_hf, b_cd], writes=[b_hf])
                        S.op("dve", _I("tensor_tensor", out=hf, in0=hf, in1=pS[:, :], op=ALU.add), reads=[b_hf, pSb], writes=[b_hf])
                        S.op("act", _I("copy", out=hfb, in_=hf), reads=[b_hf], writes=[b_hfb])
                    t1, b_t1 = f32tile()
                    t2, b_t2 = f32tile()
                    S.op("dve", _I("tensor_tensor", out=v3(t1, 8), in0=v3(pF[:, :], 8), in1=bc8(ea3[:, c, 0:8]), op=ALU.mult),
                         reads=[pFb, b_ea], writes=[b_t1])
                    S.op("dve", _I("tensor_tensor", out=v3(t2, 8), in0=v3(pB[:, :], 8), in1=bc8(ea3[:, c, 8:16]), op=ALU.mult),
                         reads=[pBb, b_ea], writes=[b_t2])
                    S.op("pool", _I("tensor_tensor", out=t1, in0=t1, in1=t2, op=ALU.add), reads=[b_t1, b_t2], writes=[b_t1])
                    S.op("dve", _I("tensor_tensor", out=t1, in0=t1, in1=pY[:, :], op=ALU.add), reads=[b_t1, pYb], writes=[b_t1])
                    if "y" in dbg_d and sq == 0:
                        dbg_tap("y", t1, [b_t1], lambda d_, c=c: d_[c * 128:(c + 1) * 128, g * 512:(g + 1) * 512])
                    S.op("pool", _I("tensor_tensor", out=t1, in0=t1, in1=th, op=ALU.mult), reads=[b_th, b_t1], writes=[b_t1])
                    yn, b_ynt = yns[par]
                    rms_rows(t1, b_t1, ssdg, b_ssdg, yn, b_ynt, dim=512, pre=0.25, post=0.5)
                    S.dma("pool", _I("dma_start", out=yn_scr[tok0 + c * 128:tok0 + (c + 1) * 128, g * 512:(g + 1) * 512], in_=yn),
                          reads=[b_ynt], writes=[b_yn[sq * NCH + c][g]])

                ctxs = {0: stageA(0)}
                for c in range(NCH):
                    if c + 1 < NCH:
                        ctxs[c + 1] = stageA(c + 1)
                    stageB(c, ctxs.pop(c))
                S.barrier(("pe", "act", "dve", "pool", "sp"))
                A.reset(mg)

        def token_phase(sq, tok0):
            g_mix, b_gmix = load_gain(0, "mix")
            g_xa, b_gxa = load_gain(1, "xattn")
            g_ffn, b_gffn = load_gain(3, "ffn")
            g_fin, b_gfin = load_gain(4, "final")
            wslots = [(A.alloc(8 * 512, BF16), Buf("wslot%d" % i)) for i in range(4)]
            w_rr = [0]

            def wload(wsrc, wkey, r0, nkc, c0, ncols):
                i = w_rr[0]
                w_rr[0] = (i + 1) % len(wslots)
                t, b = wslots[i]
                view = v3(t[:, 0:nkc * ncols], nkc)
                wtile_load(view, b, wsrc, WB[wkey], r0, nkc, c0, ncols)
                return view, b

            xres = A.alloc(4 * 1024, F32)
            xres3 = v3(xres, 4)
            b_x = [Buf("xres%d" % i) for i in range(4)]
            xh = A.alloc(1024, F32)
            b_xh = Buf("xh")
            hns = [(A.alloc(1024, BF16), Buf("hn%d" % i)) for i in range(2)]
            hn_rr = [0]
            hT6 = A.alloc(8 * 768, BF16)
            hT63 = v3(hT6, 8)
            b_hT6 = [Buf("hT6_%d" % i) for i in range(6)]
            KrT6 = A.alloc(4 * 2 * 768, BF16)
            KrT64 = v4(KrT6, 4, 2)
            b_KrT = Buf("KrT6")
            S.op("pool", _I("memset", KrT6, 0.0), writes=[b_KrT])
            Vaug = A.alloc(6 * 260, BF16)
            Vaug4 = v4(Vaug, 6, 4)
            b_Vaug = Buf("Vaug")
            cos6 = A.alloc(768, F32)
            sin6 = A.alloc(768, F32)
            b_cos = Buf("cos6")
            b_sin = Buf("sin6")
            FM = {}
            for nm in ("M1", "QT", "AT", "MT"):
                t = A.alloc(8 * 512, BF16)
                FM[nm] = (v3(t, 8), Buf(nm))
            R1 = A.alloc(12 * 1024, BF16)
            ynT3 = v3(R1[:, 0:8192], 16)
            ynbs = [R1[:, 8192:10240], R1[:, 10240:12288]]
            actT3 = v3(R1[:, 0:22 * 512], 22)
            b_R1 = Buf("R1")
            b_ynb = [Buf("ynb0"), Buf("ynb1")]
            PTs = [(A.alloc(512, BF16), Buf("PT%d" % i)) for i in range(12)]
            PxT = [(A.alloc(512, BF16), Buf("PxT%d" % i)) for i in range(2)]
            ths = [(A.alloc(512, BF16), Buf("th%d" % i)) for i in range(2)]
            th_rr = [0]
            ftm = [(A.alloc(512, F32), Buf("ftm%d" % i)) for i in range(3)]
            f_rr = [0]
            attn_bfs = [(A.alloc(1024, BF16), Buf("attnbf%d" % i)) for i in range(2)]
            outf = [(A.alloc(1024, F32), Buf("outf%d" % i)) for i in range(1)]
            dens = A.alloc(8, F32)
            b_dens = Buf("dens")

            def ftile():
                i = f_rr[0]
                f_rr[0] = (i + 1) % len(ftm)
                return ftm[i]

            def thtile():
                i = th_rr[0]
                th_rr[0] = (i + 1) % len(ths)
                return ths[i]

            def hntile():
                i = hn_rr[0]
                hn_rr[0] = (i + 1) % 2
                return hns[i]

            def proj(wv, wb, mcol, rhs, rbufs, n):
                pt, pb = psum()
                nk = wv.shape[1]
                for kc in range(nk):
                    mm(pt[:, 0:n], wv[:, kc, mcol:mcol + 128], rhs(kc), kc == 0, kc == nk - 1, [wb] + rbufs, pb, inc=(kc == nk - 1))
                return pt, pb

            S.op("pool", _I("memset", Vaug4[:, :, :, 64:65], 1.0), writes=[b_Vaug])

            for sb in range(NSB):
                cg0 = sb * 4
                valid = [li for li in range(6) if 0 <= cg0 - 1 + li < NCH]
                lo, hi = valid[0], valid[-1] + 1
                gl = (cg0 - 1 + lo) * 128
                S.dma("sp", _I("dma_start", out=cos6[:, lo * 128:hi * 128], in_=rope_d[:, gl:gl + (hi - lo) * 128]), writes=[b_cos])
                S.dma("sp", _I("dma_start", out=sin6[:, lo * 128:hi * 128], in_=rope_d[:, SEQ + gl:SEQ + gl + (hi - lo) * 128]), writes=[b_sin])
                for li in valid:
                    c = cg0 - 1 + li
                    if 1 <= li <= 4:
                        xt, bxt = xres3[:, li - 1, :], b_x[li - 1]
                    else:
                        xt, bxt = xh, b_xh
                    S.dma("sp", _I("dma_start", out=xt, in_=x_d[tok0 + c * 128:tok0 + (c + 1) * 128, :]), writes=[bxt])
                    hn, b_hn = hntile()
                    rms_rows(xt, bxt, g_mix, b_gmix, hn, b_hn)
                    transpose_blocks(hn, [b_hn], 8, lambda k0, n, li=li: hT63[:, k0:k0 + n, li * 128:(li + 1) * 128], [b_hT6[li]])
                own = [b_hT6[i] for i in range(1, 5)]
                hown = lambda kc: hT63[:, kc, 128:640]
                Wk, bWk = wload(wb_in, "in", 0, 8, OK_, 512)
                Wkp, bWkp = wload(wb_in, "in", 0, 8, OKP, 512)
                Wv, bWv = wload(wb_in, "in", 0, 8, OV, 256)
                segs = [(128, 640, own)]
                if 0 in valid:
                    segs.append((0, 128, [b_hT6[0]]))
                if 5 in valid:
                    segs.append((640, 768, [b_hT6[5]]))
                for (s0, s1, sbufs) in segs:
                    n = s1 - s0
                    for j in range(4):
                        p1, p1b = proj(Wk, bWk, j * 128, lambda kc: hT63[:, kc, s0:s1], sbufs, n)
                        p2, p2b = proj(Wkp, bWkp, j * 128, lambda kc: hT63[:, kc, s0:s1], sbufs, n)
                        t1, bt1 = ftile()
                        t2, bt2 = ftile()
                        S.op("dve", _I("tensor_tensor", out=t1[:, 0:n], in0=p1[:, 0:n], in1=cos6[:, s0:s1], op=ALU.mult), reads=[p1b, b_cos], writes=[bt1])
                        S.op("dve", _I("tensor_tensor", out=t2[:, 0:n], in0=p2[:, 0:n], in1=sin6[:, s0:s1], op=ALU.mult), reads=[p2b, b_sin], writes=[bt2])
                        S.op("pool", _I("tensor_tensor", out=KrT64[0:64, j, 0, s0:s1], in0=t1[0:64, 0:n], in1=t2[0:64, 0:n], op=ALU.add),
                             reads=[bt1, bt2], writes=[b_KrT])
                        S.op("pool", _I("tensor_tensor", out=KrT64[64:128, j, 1, s0:s1], in0=t1[64:128, 0:n], in1=t2[64:128, 0:n], op=ALU.add),
                             reads=[bt1, bt2], writes=[b_KrT])
                for li in valid:
                    pt, pb = psum()
                    for kc in range(8):
                        mm(pt[:, 0:256], hT63[:, kc, li * 128:(li + 1) * 128], Wv[:, kc, :], kc == 0, kc == 7, [bWv, b_hT6[li]], pb, inc=(kc == 7))
                    S.op("act", _I("copy", out=Vaug4[:, li, :, 0:64], in_=v3(pt[:, 0:256], 4)), reads=[pb], writes=[b_Vaug])
                QT3, b_QT = FM["QT"]
                for blk in range(2):
                    Wq, bWq = wload(wb_in, "in", 0, 8, OQ + blk * 512, 512)
                    Wqp, bWqp = wload(wb_in, "in", 0, 8, OQP + blk * 512, 512)
                    for mt in range(4):
                        t = blk * 4 + mt
                        p1, p1b = proj(Wq, bWq, mt * 128, hown, own, 512)
                        p2, p2b = proj(Wqp, bWqp, mt * 128, hown, own, 512)
                        t1, bt1 = ftile()
                        t2, bt2 = ftile()
                        S.op("dve", _I("tensor_tensor", out=t1, in0=p1[:, :], in1=cos6[:, 128:640], op=ALU.mult), reads=[p1b, b_cos], writes=[bt1])
                        S.op("dve", _I("tensor_tensor", out=t2, in0=p2[:, :], in1=sin6[:, 128:640], op=ALU.mult), reads=[p2b, b_sin], writes=[bt2])
                        S.op("pool", _I("tensor_tensor", out=QT3[:, t, :], in0=t1, in1=t2, op=ALU.add), reads=[bt1, bt2], writes=[b_QT])
                for cq in range(4):
                    ynb, bynb = ynbs[cq % 2], b_ynb[cq % 2]
                    r0 = tok0 + (cg0 + cq) * 128
                    S.dma("sp", _I("dma_start", out=ynb, in_=yn_scr[r0:r0 + 128, :]), reads=b_yn[sq * NCH + cg0 + cq], writes=[bynb, b_R1])
                    transpose_blocks(ynb, [bynb], 16, lambda k0, n, cq=cq: ynT3[:, k0:k0 + n, cq * 128:(cq + 1) * 128], [b_R1])
                M13, b_M1 = FM["M1"]
                for blk in range(2):
                    Wgt, bWgt = wload(wb_in, "in", 0, 8, OG + blk * 512, 512)
                    Wb0, bWb0 = wload(wb_bs, "bs", 0, 8, blk * 512, 512)
                    Wb1, bWb1 = wload(wb_bs, "bs", 1024, 8, blk * 512, 512)
                    for mt in range(4):
                        t = blk * 4 + mt
                        pg, pgb = proj(Wgt, bWgt, mt * 128, hown, own, 512)
                        th, bth = thtile()
                        S.op("act", _I("activation", out=th, in_=pg[:, :], func=AF.Tanh, scale=0.5), reads=[pgb], writes=[bth])
                        pbr, pbrb = psum()
                        for kc in range(16):
                            wv_, wb_ = (Wb0, bWb0) if kc < 8 else (Wb1, bWb1)
                            mm(pbr[:, :], wv_[:, kc % 8, mt * 128:(mt + 1) * 128], ynT3[:, kc, :], kc == 0, kc == 15, [wb_, b_R1], pbrb, inc=(kc == 15))
                        S.op("dve", _I("scalar_tensor_tensor", out=M13[:, t, :], in0=th, scalar=1.0, in1=pbr[:, :], op0=ALU.add, op1=ALU.mult),
                             reads=[bth, pbrb], writes=[b_M1])
                AT3, b_AT = FM["AT"]
                ps_sub[0] = [0, 1, 2, 3]
                for cq in range(4):
                    lq = cq + 1
                    kbs = [kb for kb in (lq - 1, lq, lq + 1) if kb in valid]
                    Ob = [(PS[4 + i], PSB[4 + i]) for i in range(4)]
                    pt_i = 0
                    for j in range(4):
                        ptl = []
                        for kb in kbs:
                            pS_, pSb = psum()
                            for r in range(4):
                                i = 4 * j + r
                                mm(pS_[:, r * 128:(r + 1) * 128], KrT64[:, j, i % 2, kb * 128:(kb + 1) * 128],
                                   QT3[:, i // 2, cq * 128:(cq + 1) * 128], True, True, [b_KrT, b_QT], pSb, inc=(r == 3))
                            PT, bPT = PTs[pt_i]
                            pt_i += 1
                            S.op("act", _I("activation", out=PT, in_=pS_[:, :], func=AF.Exp, scale=0.125), reads=[pSb], writes=[bPT])
                            PT3 = v3(PT, 4)
                            if kb == lq - 1:
                                S.op("pool", _I("affine_select", out=PT3, in_=PT3, pattern=[[0, 4], [-1, 128]], compare_op=ALU.is_ge, fill=0.0,
                                                base=0, channel_multiplier=1), reads=[bPT], writes=[bPT])
                            elif kb == lq + 1:
                                S.op("pool", _I("affine_select", out=PT3, in_=PT3, pattern=[[0, 4], [1, 128]], compare_op=ALU.is_ge, fill=0.0,
                                                base=0, channel_multiplier=-1), reads=[bPT], writes=[bPT])
                            ptl.append((PT, bPT, kb))
                        for r in range(4):
                            i = 4 * j + r
                            O_, Ob_ = Ob[i // 4]
                            oc = (i % 4) * 65
                            for n_, (PT, bPT, kb) in enumerate(ptl):
                                mm(O_[:, oc:oc + 65], PT[:, r * 128:(r + 1) * 128], Vaug4[:, kb, j, :], n_ == 0, n_ == len(ptl) - 1,
                                   [bPT, b_Vaug], Ob_, inc=(n_ == len(ptl) - 1))
                    abf, b_abf = attn_bfs[cq % 2]
                    abf3 = v3(abf, 16)
                    for bnk in range(4):
                        O_, Ob_ = Ob[bnk]
                        O3 = O_[:, 0:260].rearrange("p (h d) -> p h d", h=4)
                        S.op("dve", _I("tensor_tensor", out=dens[:, 0:4], in0=O3[:, :, 64], in1=esink[:, bnk * 4:(bnk + 1) * 4], op=ALU.add),
                             reads=[Ob_, b_esink], writes=[b_dens])
                        S.op("dve", _I("reciprocal", out=dens[:, 4:8], in_=dens[:, 0:4]), reads=[b_dens], writes=[b_dens])
                        S.op("dve", _I("tensor_tensor", out=abf3[:, bnk * 4:(bnk + 1) * 4, :], in0=O3[:, :, 0:64],
                                       in1=dens[:, 4:8].unsqueeze(2).to_broadcast([128, 4, 64]), op=ALU.mult), reads=[Ob_, b_dens], writes=[b_abf])
                    if "attn" in dbg_d and sq == 0:
                        tf, btf = outf[0]
                        S.op("act", _I("copy", out=tf, in_=abf), reads=[b_abf], writes=[btf])
                        dbg_tap("attn", tf, [btf], lambda d_: d_[(cg0 + cq) * 128:(cg0 + cq + 1) * 128, :])
                    transpose_blocks(abf, [b_abf], 8, lambda k0, n, cq=cq: AT3[:, k0:k0 + n, cq * 128:(cq + 1) * 128], [b_AT])
                ps_sub[0] = list(range(8))
                MT3, b_MT = FM["MT"]
                for blk in range(2):
                    Wgt, bWgt = wload(wb_in, "in", 0, 8, OG + 1024 + blk * 512, 512)
                    Wa, bWa = wload(wb_ba, "ba", 0, 8, blk * 512, 512)
                    for mt in range(4):
                        t = blk * 4 + mt
                        pg, pgb = proj(Wgt, bWgt, mt * 128, hown, own, 512)
                        th, bth = thtile()
                        S.op("act", _I("activation", out=th, in_=pg[:, :], func=AF.Tanh, scale=0.5), reads=[pgb], writes=[bth])
                        pa_, pab_ = proj(Wa, bWa, mt * 128, lambda kc: AT3[:, kc, :], [b_AT], 512)
                        tm, btm = ftile()
                        S.op("dve", _I("scalar_tensor_tensor", out=tm, in0=th, scalar=1.0, in1=pa_[:, :], op0=ALU.add, op1=ALU.mult),
                             reads=[bth, pab_], writes=[btm])
                        S.op("pool", _I("tensor_tensor", out=MT3[:, t, :], in0=tm, in1=M13[:, t, :], op=ALU.add), reads=[btm, b_M1], writes=[b_MT])
                for half in range(2):
                    Wm, bWm = wload(wb_mo, "mo", 0, 8, half * 512, 512)
                    for cq in range(4):
                        pt, pb = psum()
                        for kc in range(8):
                            mm(pt[:, :], MT3[:, kc, cq * 128:(cq + 1) * 128], Wm[:, kc, :], kc == 0, kc == 7, [b_MT, bWm], pb, inc=(kc == 7))
                        xs = xres3[:, cq, half * 512:(half + 1) * 512]
                        S.op("dve", _I("scalar_tensor_tensor", out=xs, in0=pt[:, :], scalar=0.5, in1=xs, op0=ALU.mult, op1=ALU.add),
                             reads=[pb, b_x[cq]], writes=[b_x[cq]])
                if "x1" in dbg_d and sq == 0:
                    for cq in range(4):
                        dbg_tap("x1", xres3[:, cq, :], [b_x[cq]], lambda d_, cq=cq: d_[(cg0 + cq) * 128:(cg0 + cq + 1) * 128, :])
                H2T3, b_H2T = FM["AT"]
                for cq in range(4):
                    hn, b_hn = hntile()
                    rms_rows(xres3[:, cq, :], b_x[cq], g_xa, b_gxa, hn, b_hn)
                    transpose_blocks(hn, [b_hn], 8, lambda k0, n, cq=cq: H2T3[:, k0:k0 + n, cq * 128:(cq + 1) * 128], [b_H2T])
                QX3, b_QX = FM["QT"]
                for blk in range(2):
                    Wq, bWq = wload(wb_xq, "xq", 0, 8, blk * 512, 512)
                    for mt in range(4):
                        pq, pqb = proj(Wq, bWq, mt * 128, lambda kc: H2T3[:, kc, :], [b_H2T], 512)
                        S.op("act", _I("copy", out=QX3[:, blk * 4 + mt, :], in_=pq[:, :]), reads=[pqb], writes=[b_QX])
                XA3, b_XA = FM["M1"]
                KxT3 = v3(KxT, 8)
                Vx3 = v3(Vx, 2)
                for hx in range(4):
                    for mtile in range(2):
                        pS_, pSb = psum()
                        for dt_ in range(2):
                            mm(pS_[:, :], KxT3[:, 2 * hx + dt_, mtile * 128:(mtile + 1) * 128], QX3[:, 2 * hx + dt_, :], dt_ == 0, dt_ == 1,
                               [b_KxT, b_QX], pSb, inc=(dt_ == 1))
                        S.op("act", _I("activation", out=PxT[mtile][0], in_=pS_[:, :], func=AF.Exp, scale=0.0625), reads=[pSb], writes=[PxT[mtile][1]])
                    pD, pDb = psum()
                    for mtile in range(2):
                        mm(pD[:, :], ones_b, PxT[mtile][0], mtile == 0, mtile == 1, [b_cstb, PxT[mtile][1]], pDb, inc=(mtile == 1))
                    rc, brc = ftile()
                    S.op("dve", _I("reciprocal", out=rc, in_=pD[:, :]), reads=[pDb], writes=[brc])
                    for dt_ in range(2):
                        pO, pOb = psum()
                        for mtile in range(2):
                            mm(pO[:, :], Vx3[:, mtile, hx * 256 + dt_ * 128:hx * 256 + (dt_ + 1) * 128], PxT[mtile][0], mtile == 0, mtile == 1,
                               [b_Vx, PxT[mtile][1]], pOb, inc=(mtile == 1))
                        S.op("dve", _I("tensor_tensor", out=XA3[:, 2 * hx + dt_, :], in0=pO[:, :], in1=rc, op=ALU.mult), reads=[pOb, brc], writes=[b_XA])
                for half in range(2):
                    Wo, bWo = wload(wb_xo, "xo", 0, 8, half * 512, 512)
                    for cq in range(4):
                        pt, pb = psum()
                        for kc in range(8):
                            mm(pt[:, :], XA3[:, kc, cq * 128:(cq + 1) * 128], Wo[:, kc, :], kc == 0, kc == 7, [b_XA, bWo], pb, inc=(kc == 7))
                        xs = xres3[:, cq, half * 512:(half + 1) * 512]
                        S.op("dve", _I("tensor_tensor", out=xs, in0=pt[:, :], in1=xs, op=ALU.add), reads=[pb, b_x[cq]], writes=[b_x[cq]])
                if "x2" in dbg_d and sq == 0:
                    for cq in range(4):
                        dbg_tap("x2", xres3[:, cq, :], [b_x[cq]], lambda d_, cq=cq: d_[(cg0 + cq) * 128:(cg0 + cq + 1) * 128, :])
                H3T3, b_H3T = FM["MT"]
                for cq in range(4):
                    hn, b_hn = hntile()
                    rms_rows(xres3[:, cq, :], b_x[cq], g_ffn, b_gffn, hn, b_hn)
                    transpose_blocks(hn, [b_hn], 8, lambda k0, n, cq=cq: H3T3[:, k0:k0 + n, cq * 128:(cq + 1) * 128], [b_H3T])
                S.barrier(("pe", "act", "dve", "pool"))
                for blk in range(6):
                    ncols = min(512, FFN_H - blk * 512)
                    Wga, bWga = wload(wb_fi, "fi", 0, 8, blk * 512, ncols)
                    Wup, bWup = wload(wb_fi, "fi", 0, 8, FFN_H + blk * 512, ncols)
                    for mt in range(ncols // 128):
                        t = blk * 4 + mt
                        pg, pgb = proj(Wga, bWga, mt * 128, lambda kc: H3T3[:, kc, :], [b_H3T], 512)
                        pu, pub = proj(Wup, bWup, mt * 128, lambda kc: H3T3[:, kc, :], [b_H3T], 512)
                        th, bth = thtile()
                        S.op("act", _I("activation", out=th, in_=pg[:, :], func=AF.Tanh, scale=0.5), reads=[pgb], writes=[bth])
                        tm, btm = ftile()
                        S.op("dve", _I("scalar_tensor_tensor", out=tm, in0=th, scalar=1.0, in1=pg[:, :], op0=ALU.add, op1=ALU.mult),
                             reads=[bth, pgb], writes=[btm])
                        S.op("dve", _I("tensor_tensor", out=actT3[:, t, :], in0=tm, in1=pu[:, :], op=ALU.mult), reads=[btm, pub], writes=[b_R1])
                for half in range(2):
                    acc = [(PS[4 + i], PSB[4 + i]) for i in range(4)]
                    for part in range(3):
                        nkc = 8 if part < 2 else 6
                        Wf, bWf = wload(wb_fo, "fo", part * 1024, nkc, half * 512, 512)
                        for cq in range(4):
                            for kk in range(nkc):
                                kc = part * 8 + kk
                                mm(acc[cq][0][:, :], actT3[:, kc, cq * 128:(cq + 1) * 128], Wf[:, kk, :], kc == 0, kc == 21, [b_R1, bWf], acc[cq][1],
                                   inc=(kk == nkc - 1))
                    for cq in range(4):
                        xs = xres3[:, cq, half * 512:(half + 1) * 512]
                        S.op("dve", _I("scalar_tensor_tensor", out=xs, in0=acc[cq][0][:, :], scalar=0.5, in1=xs, op0=ALU.mult, op1=ALU.add),
                             reads=[acc[cq][1], b_x[cq]], writes=[b_x[cq]])
                for cq in range(4):
                    tf, btf = outf[0]
                    rms_rows(xres3[:, cq, :], b_x[cq], g_fin, b_gfin, tf, btf)
                    r0 = tok0 + (cg0 + cq) * 128
                    S.dma("pool", _I("dma_start", out=out_d[r0:r0 + 128, :], in_=tf), reads=[btf], writes=[Buf()], is_output=True)
                S.barrier(("pe", "act", "dve", "pool"))


        for sq in range(NSEQ):
            tok0 = sq * SEQ
            A.reset(PERSIST_MARK)
            m0 = A.mark()
            g_mem, b_gmem = load_gain(2, "mem")
            wkv = A.alloc(8 * 2048, BF16)
            b_wkv = Buf("wkv")
            wtile_load(v3(wkv, 8), b_wkv, wb_xkv, WB["xkv"], 0, 8, 0, 2048)
            memT = A.alloc(8 * 256, BF16)
            b_memT = [Buf("memT0"), Buf("memT1")]
            for mt in range(2):
                xm = A.alloc(1024, F32)
                b_xm = Buf("xm")
                S.dma("sp", _I("dma_start", out=xm, in_=mem_d[sq * 256 + mt * 128: sq * 256 + (mt + 1) * 128, :]), writes=[b_xm])
                mn = A.alloc(1024, BF16)
                b_mn = Buf("mn")
                rms_rows(xm, b_xm, g_mem, b_gmem, mn, b_mn)
                transpose_blocks(mn, [b_mn], 8, lambda k0, n, mt=mt: v3(memT, 8)[:, k0:k0 + n, mt * 128:(mt + 1) * 128], [b_memT[mt]])
            wkv3 = v3(wkv, 8)
            memT3 = v3(memT, 8)
            for dtile in range(8):
                pt, pb = psum()
                for kc in range(8):
                    mm(pt[:, 0:256], wkv3[:, kc, dtile * 128:(dtile + 1) * 128], memT3[:, kc, :], kc == 0, kc == 7,
                       [b_wkv] + b_memT, pb, inc=(kc == 7))
                S.op("act", _I("copy", out=v3(KxT, 8)[:, dtile, :], in_=pt[:, 0:256]), reads=[pb], writes=[b_KxT])
            for mt in range(2):
                for half in range(2):
                    pt, pb = psum()
                    for kc in range(8):
                        mm(pt[:, :], memT3[:, kc, mt * 128:(mt + 1) * 128], wkv3[:, kc, 1024 + half * 512:1024 + (half + 1) * 512],
                           kc == 0, kc == 7, [b_wkv] + b_memT, pb, inc=(kc == 7))
                    S.op("act", _I("copy", out=v3(Vx, 2)[:, mt, half * 512:(half + 1) * 512], in_=pt[:, :]),
                         reads=[pb], writes=[b_Vx])
            S.barrier(("pe", "act", "dve", "pool", "sp"))
            A.reset(m0)
            if sq == 0:
                cast_weight("bs", w_bs_d, wb_bs, 2048, 1024)
                cast_weight("ba", w_ba_d, wb_ba, 1024, 1024)
                cast_weight("mo", w_mo_d, wb_mo, 1024, 1024)
                cast_weight("xq", w_xq_d, wb_xq, 1024, 1024)
                cast_weight("xo", w_xo_d, wb_xo, 1024, 1024)
                cast_weight("fi", w_fi_d, wb_fi, 1024, 256)
                cast_weight("fo", w_fo_d, wb_fo, FFN_H, 704)

            hT = A.alloc(8 * SEQ, BF16)
            hT3 = v3(hT, 8)
            b_hT = [Buf("hT%d" % c) for c in range(NCH)]
            m1 = A.mark()
            g_mix, b_gmix = load_gain(0, "mix")
            xts = [(A.alloc(1024, F32), Buf("xt%d" % i)) for i in range(2)]
            hns = [(A.alloc(1024, BF16), Buf("hn%d" % i)) for i in range(2)]
            for c in range(NCH):
                xt, b_xt = xts[c % 2]
                hn, b_hn = hns[c % 2]
                S.dma("sp", _I("dma_start", out=xt, in_=x_d[tok0 + c * 128: tok0 + (c + 1) * 128, :]), writes=[b_xt])
                rms_rows(xt, b_xt, g_mix, b_gmix, hn, b_hn)
                transpose_blocks(hn, [b_hn], 8, lambda k0, n, c=c: hT3[:, k0:k0 + n, c * 128:(c + 1) * 128], [b_hT[c]])
            if "h" in dbg_d and sq == 0:
                for kc in range(8):
                    tmpf = A.alloc(SEQ, F32)
                    bt = Buf()
                    S.op("dve", _I("tensor_copy", out=tmpf, in_=hT3[:, kc, :]), reads=b_hT, writes=[bt])
                    dbg_tap("h", tmpf, [bt], lambda d_, kc=kc: d_[kc * 128:(kc + 1) * 128, :])
            S.barrier(("pe", "act", "dve", "pool", "sp"))
            A.reset(m1)

            if STAGE >= 2:
                ssd_phase(sq, tok0, hT3, b_hT)
            S.barrier(("pe", "act", "dve", "pool", "sp"))
            A.reset(PERSIST_MARK)
            if STAGE >= 3:
                token_phase(sq, tok0)
            S.barrier(("pe", "act", "dve", "pool", "sp"))

        S.finish()
        S.emit(nc, st)
    return nc, A.peak, S


def _const_tables(SEQ):
    ident = np.eye(128, dtype=np.float32)
    triF = np.triu(np.ones((128, 128), np.float32))
    triB = np.tril(np.ones((128, 128), np.float32))
    ones = np.ones((128, 128), np.float32)
    maskF = np.where(triF > 0, 0.0, -30000.0).astype(np.float32)
    maskB = np.where(triB > 0, 0.0, -30000.0).astype(np.float32)
    cst = np.concatenate([ident, triF, triB, ones, maskF, maskB], axis=1)
    half = 32
    inv_freq = (np.float32(10000.0) ** (-np.arange(half, dtype=np.float32) / np.float32(half))).astype(np.float32)
    ang = np.arange(SEQ, dtype=np.float32)[:, None] * inv_freq[None, :]
    cos = np.cos(ang).astype(np.float32)
    sin = np.sin(ang).astype(np.float32)
    p = np.arange(128)
    d = p % 64
    cosT = cos[:, d % 32].T
    sgn = np.where(d < 32, -1.0, 1.0).astype(np.float32)
    sinT = (sin[:, d % 32] * sgn[None, :]).T
    rope = np.ascontiguousarray(np.concatenate([cosT, sinT], axis=1), dtype=np.float32)
    return np.ascontiguousarray(cst), rope


def _win_cols():
    cols = list(range(0, 2048)) + list(range(2048, 5120))
    for g in range(4):
        cols += list(range(5120 + 8 * g, 5120 + 8 * g + 8)) + list(range(5152 + 8 * g, 5152 + 8 * g + 8))
    qb, kb, vb, gb = 5184, 6208, 6464, 6720
    cols += list(range(qb, qb + 1024))
    cols += [qb + h * 64 + (d + 32) % 64 for h in range(16) for d in range(64)]
    for j in range(4):
        cols += list(range(kb + j * 64, kb + (j + 1) * 64)) * 2
    for j in range(4):
        cols += [kb + j * 64 + (d + 32) % 64 for d in range(64)] * 2
    cols += list(range(vb, vb + 256))
    cols += list(range(gb, gb + 2048))
    assert len(cols) == WIN
    return np.asarray(cols)


def make_shared(inp, SEQ):
    f = lambda a: np.ascontiguousarray(a, dtype=np.float32)
    cst, rope = _const_tables(SEQ)
    cw = inp["conv_w"][0]
    cb = inp["conv_b"][0]
    convp = np.zeros((128, 24, 8), np.float32)
    convp[:, :, 0:5] = cw.reshape(5, 24, 128).transpose(2, 1, 0)
    convp[:, :, 5] = cb.reshape(24, 128).T
    gm = lambda v: np.concatenate([np.concatenate([v[0][8 * g:8 * g + 8], v[1][8 * g:8 * g + 8]]) for g in range(4)])
    vecs = np.zeros((1, 192), np.float32)
    vecs[0, 0:64] = gm((inp["dt_bias_fwd"][0], inp["dt_bias_bwd"][0]))
    vecs[0, 64:128] = gm((inp["a_log_fwd"][0], inp["a_log_bwd"][0]))
    vecs[0, 128:160] = inp["d_skip"][0]
    vecs[0, 160:176] = inp["attn_sink"][0]
    gains = np.stack([inp["norm_mix_g"][0], inp["norm_xattn_g"][0], inp["norm_mem_g"][0], inp["norm_ffn_g"][0],
                      inp["norm_final_g"]], axis=0)
    return {
        "w_in_r": f(inp["w_in"][0][:, _win_cols()]),
        "w_bs": f(inp["w_branch_ssd"][0]), "w_ba": f(inp["w_branch_attn"][0]), "w_mo": f(inp["w_mix_out"][0]),
        "w_xq": f(inp["w_xattn_q"][0]), "w_xkv": f(inp["w_xattn_kv"][0]), "w_xo": f(inp["w_xattn_out"][0]),
        "w_fi": f(inp["w_ffn_in"][0]), "w_fo": f(inp["w_ffn_out"][0]),
        "gains": f(gains), "convp": f(convp.reshape(128, 192)), "vecs": f(vecs),
        "ssdg": f(inp["ssd_norm_g"][0][None, :]), "cst": cst, "rope": rope,
    }


def make_inputs(inp, SEQ, NSEQ, core, shared=None):
    shared = shared if shared is not None else make_shared(inp, SEQ)
    b0 = core * NSEQ
    m = dict(shared)
    m["x"] = np.ascontiguousarray(inp["x"][b0:b0 + NSEQ, :SEQ].reshape(NSEQ * SEQ, 1024), dtype=np.float32)
    m["mem"] = np.ascontiguousarray(inp["mem"][b0:b0 + NSEQ].reshape(NSEQ * 256, 1024), dtype=np.float32)
    return m


_PROG = {}


def kernel(**inputs):
    SEQ, NSEQ, NCORES = 2048, 2, 8
    inp = {k: np.asarray(v) for k, v in inputs.items()}
    if "prog" not in _PROG:
        _PROG["prog"] = build_program(SEQ, NSEQ)[0]
    nc = _PROG["prog"]
    shared = make_shared(inp, SEQ)
    in_maps = [make_inputs(inp, SEQ, NSEQ, c, shared) for c in range(NCORES)]
    res = run_bass_kernel_spmd(nc, in_maps, core_ids=list(range(NCORES)))
    outs = [np.asarray(r["out"]).reshape(NSEQ, SEQ, 1024) for r in res.results]
    return np.concatenate(outs, axis=0).astype(np.float32)
```

```python
import math
from contextlib import ExitStack
import numpy as np
import concourse.bass as bass
import concourse.mybir as mybir
from concourse.bass_utils import run_bass_kernel_spmd

F32 = mybir.dt.float32
BF16 = mybir.dt.bfloat16
U8 = mybir.dt.uint8
AF = mybir.ActivationFunctionType
ALU = mybir.AluOpType
AX = mybir.AxisListType

D_MODEL = 1024
EPS = 1e-6
FFN_H = 2816
OZ, OXBC, ODT, OQ, OQP, OK_, OKP, OV, OG = 0, 2048, 5120, 5184, 6208, 7232, 7744, 8256, 8512
WIN = 10560


class Buf:
    __slots__ = ("name", "w", "r")

    def __init__(self, name=""):
        self.name = name
        self.w = None
        self.r = {}


class Ev:
    __slots__ = ("sem", "val", "clock", "eng")

    def __init__(self, eng):
        self.sem = None
        self.val = None
        self.clock = None
        self.eng = eng


class Sched:
    ENGS = ("pe", "act", "dve", "pool", "sp")
    EPOCH = 16000

    def __init__(self, n_dma_sems=48):
        self.streams = {e: [] for e in self.ENGS}
        self.cnt = {e: 0 for e in self.ENGS}
        self.know = {e: {} for e in self.ENGS}
        self.pending = {e: [] for e in self.ENGS}
        self.last = {e: None for e in self.ENGS}
        self.n_dma = n_dma_sems
        self.dma_cnt = [0] * n_dma_sems
        self.dma_last = [None] * n_dma_sems
        self.dma_rr = 0
        self.dma_rr_sw = 0
        self.dma_open = []
        self.out_events = []
        self.epoch = {e: 0 for e in self.ENGS}
        self.nops = 0

    def _need(self, eng, ev, waits):
        if ev is None:
            return
        if ev.sem is None and ev.eng == eng:
            return
        assert ev.sem is not None, "dependency on unresolved (non-inc) op"
        k = self.know[eng]
        if k.get(ev.sem, 0) >= ev.val:
            return
        if ev.sem[0] != "dma":
            for ep in range(ev.sem[1] + 1, self.epoch[ev.sem[0]] + 1):
                if k.get((ev.sem[0], ep), 0) >= 1:
                    return
        waits.append((ev.sem, ev.val))
        for s, v in ev.clock.items():
            if k.get(s, 0) < v:
                k[s] = v

    @staticmethod
    def _dedupe(waits):
        best = {}
        for s, v in waits:
            if best.get(s, 0) < v:
                best[s] = v
        return list(best.items())

    def _deps(self, eng, reads, writes):
        waits = []
        for b in reads:
            self._need(eng, b.w, waits)
        for b in writes:
            if b.w is not None and (b.w.eng != eng or eng != "pe"):
                self._need(eng, b.w, waits)
            for e2, ev in b.r.items():
                if e2 != eng or eng != "pe":
                    self._need(eng, ev, waits)
        return self._dedupe(waits)

    def _mark(self, ev, eng, reads, writes):
        for b in reads:
            b.r[eng] = ev
        for b in writes:
            b.w = ev
            b.r = {}

    def op(self, eng, fn, reads=(), writes=(), inc=True):
        self.nops += 1
        waits = self._deps(eng, reads, writes)
        if not inc:
            ev = Ev(eng)
            self.pending[eng].append(ev)
            self._mark(ev, eng, reads, writes)
            self.streams[eng].append((waits, fn, None))
            return ev
        self.cnt[eng] += 1
        if self.cnt[eng] > self.EPOCH:
            self.cnt[eng] = 1
            self.epoch[eng] += 1
        n = self.cnt[eng]
        ev = Ev(eng)
        ev.sem = (eng, self.epoch[eng])
        ev.val = n
        ev.clock = dict(self.know[eng])
        ev.clock[ev.sem] = n
        for p in self.pending[eng]:
            p.sem, p.val, p.clock = ev.sem, ev.val, ev.clock
        self.pending[eng] = []
        self._mark(ev, eng, reads, writes)
        self.streams[eng].append((waits, fn, (ev.sem, 1)))
        self.last[eng] = ev
        return ev

    def dma(self, eng, fn, reads=(), writes=(), is_output=False):
        self.nops += 1
        waits = self._deps(eng, reads, writes)
        n_sw = self.n_dma // 3
        if eng == "pool":
            j = self.dma_rr_sw
            self.dma_rr_sw = (j + 1) % n_sw
        else:
            j = n_sw + self.dma_rr
            self.dma_rr = (self.dma_rr + 1) % (self.n_dma - n_sw)
        prev = self.dma_last[j]
        if prev is not None:
            self._need(eng, prev, waits)
            waits = self._dedupe(waits)
        self.dma_cnt[j] += 16
        ev = Ev(eng)
        ev.sem = ("dma", j)
        ev.val = self.dma_cnt[j]
        ev.clock = dict(self.know[eng])
        ev.clock[ev.sem] = ev.val
        self.dma_last[j] = ev
        self._mark(ev, eng, reads, writes)
        self.streams[eng].append((waits, fn, (ev.sem, 16)))
        self.dma_open.append(ev)
        if is_output:
            self.out_events.append(ev)
        return ev

    def barrier(self, engines=("pe", "act", "dve", "pool")):
        for e in self.ENGS:
            assert not self.pending[e], "barrier with pending non-inc ops"
        for e in engines:
            waits = []
            for e2 in ("pe", "act", "dve", "pool"):
                if e2 != e:
                    self._need(e, self.last[e2], waits)
            for ev in self.dma_open:
                self._need(e, ev, waits)
            self.streams[e].append((self._dedupe(waits), None, None))
        self.dma_open = []

    def finish(self):
        waits = []
        for ev in self.out_events:
            self._need("sp", ev, waits)
        self.streams["sp"].append((self._dedupe(waits), None, None))

    def emit(self, nc, stack):
        sems = {}
        for e in ("pe", "act", "dve", "pool"):
            for ep in range(self.epoch[e] + 1):
                sems[(e, ep)] = stack.enter_context(nc.semaphore("s_%s%d" % (e, ep)))
        for j in range(self.n_dma):
            sems[("dma", j)] = stack.enter_context(nc.semaphore("s_dma%d" % j))
        block = stack.enter_context(nc.Block())
        streams = self.streams

        def run(handle, lst):
            for waits, fn, inc in lst:
                for s, v in waits:
                    handle.wait_ge(sems[s], v)
                if fn is None:
                    continue
                ins = fn(handle)
                if inc is not None:
                    ins.then_inc(sems[inc[0]], inc[1])

        @block.sync
        def _(e):
            run(e, streams["sp"])

        @block.tensor
        def _(e):
            run(e, streams["pe"])

        @block.scalar
        def _(e):
            run(e, streams["act"])

        @block.vector
        def _(e):
            run(e, streams["dve"])

        @block.gpsimd
        def _(e):
            run(e, streams["pool"])


class LazyReg:
    def __init__(self, value):
        self.value = value
        self.reg = None

    def get(self, e):
        if self.reg is None:
            self.reg = e.to_reg(self.value)
        return self.reg


def _I(name, *args, **kwargs):
    def f(e):
        kw = {k: (v.get(e) if isinstance(v, LazyReg) else v) for k, v in kwargs.items()}
        return getattr(e, name)(*args, **kw)
    return f


def _bytes(dt):
    return 4 if dt == F32 else 2


class Arena:
    def __init__(self, ap_u8, size):
        self.ap = ap_u8
        self.size = size
        self.off = 0
        self.peak = 0

    def alloc(self, n_elems, dt, parts=128):
        nb = n_elems * _bytes(dt)
        off = (self.off + 63) // 64 * 64
        assert off + nb <= self.size, "SBUF arena overflow: need %d have %d" % (off + nb, self.size)
        self.off = off + nb
        self.peak = max(self.peak, self.off)
        return self.ap[0:parts, off:off + nb].bitcast(dt)

    def mark(self):
        return self.off

    def reset(self, m):
        self.off = m


def v3(ap, a):
    return ap.rearrange("p (a b) -> p a b", a=a)


def v4(ap, a, b):
    return ap.rearrange("p (a b c) -> p a b c", a=a, b=b)


def build_program(SEQ, NSEQ, dbg=None, STAGE=3):
    NCH = SEQ // 128
    NSB = SEQ // 512
    NTOK = NSEQ * SEQ
    nc = bass.Bass("TRN2", target_bir_lowering=False)
    S = Sched()
    dbg = dbg or {}

    def din(name, shape, dt=F32):
        return nc.dram_tensor(name, shape, dt, kind="ExternalInput").ap()

    def dscr(name, shape, dt):
        return nc.dram_tensor(name, shape, dt, kind="Internal").ap()

    x_d = din("x", [NTOK, 1024])
    mem_d = din("mem", [NSEQ * 256, 1024])
    w_in_d = din("w_in_r", [1024, WIN])
    w_bs_d = din("w_bs", [2048, 1024])
    w_ba_d = din("w_ba", [1024, 1024])
    w_mo_d = din("w_mo", [1024, 1024])
    w_xq_d = din("w_xq", [1024, 1024])
    w_xkv_d = din("w_xkv", [1024, 2048])
    w_xo_d = din("w_xo", [1024, 1024])
    w_fi_d = din("w_fi", [1024, 2 * FFN_H])
    w_fo_d = din("w_fo", [FFN_H, 1024])
    gains_d = din("gains", [5, 1024])
    convp_d = din("convp", [128, 24 * 8])
    vecs_d = din("vecs", [1, 192])
    ssdg_d = din("ssdg", [1, 2048])
    cst_d = din("cst", [128, 6 * 128])
    rope_d = din("rope", [128, 2 * SEQ])
    out_d = nc.dram_tensor("out", [NTOK, 1024], F32, kind="ExternalOutput").ap()
    dbg_d = {k: nc.dram_tensor("dbg_" + k, list(shp), F32, kind="ExternalOutput").ap() for k, shp in dbg.items()}

    wb_in = dscr("wb_in", [1024, WIN], BF16)
    wb_bs = dscr("wb_bs", [2048, 1024], BF16)
    wb_ba = dscr("wb_ba", [1024, 1024], BF16)
    wb_mo = dscr("wb_mo", [1024, 1024], BF16)
    wb_xq = dscr("wb_xq", [1024, 1024], BF16)
    wb_xkv = dscr("wb_xkv", [1024, 2048], BF16)
    wb_xo = dscr("wb_xo", [1024, 1024], BF16)
    wb_fi = dscr("wb_fi", [1024, 2 * FFN_H], BF16)
    wb_fo = dscr("wb_fo", [FFN_H, 1024], BF16)
    yn_scr = dscr("yn_scr", [NTOK, 2048], BF16)
    ac_scr = dscr("ac_scr", [2 * NCH * 4096], F32)

    with ExitStack() as st:
        ARENA_BYTES = 212736
        arena_t = st.enter_context(nc.sbuf_tensor("arena", [128, ARENA_BYTES], U8))
        A = Arena(arena_t, ARENA_BYTES)
        PS = []
        PSB = []
        for i in range(8):
            t = st.enter_context(nc.psum_tensor("psb%d" % i, [128, 512], F32))
            PS.append(t)
            PSB.append(Buf("ps%d" % i))
        ps_rr = [0]

        ps_sub = [list(range(8))]
        ps_cnt = {}

        def psum(sub=None):
            sub = tuple(sub if sub is not None else ps_sub[0])
            k = ps_cnt.get(sub, 0)
            ps_cnt[sub] = k + 1
            i = sub[k % len(sub)]
            return PS[i], PSB[i]

        WB = {}

        def cast_weight(key, src, dst, rows, step):
            bl = []
            for r0 in range(0, rows, step):
                r1 = min(rows, r0 + step)
                b = Buf("w_%s_%d" % (key, r0))
                S.dma("pool", _I("dma_start", out=dst[r0:r1, :], in_=src[r0:r1, :]), writes=[b])
                bl.append(b)
            WB[key] = bl

        cast_weight("xkv", w_xkv_d, wb_xkv, 1024, 512)
        cast_weight("in", w_in_d, wb_in, 1024, 128)

        cst_f = A.alloc(768, F32)
        b_cst = Buf("cst")
        S.dma("sp", _I("dma_start", out=cst_f, in_=cst_d[:, :]), writes=[b_cst])
        ident_f = cst_f[:, 0:128]
        triF = cst_f[:, 128:256]
        triB = cst_f[:, 256:384]
        ones_f = cst_f[:, 384:512]
        maskF = cst_f[:, 512:640]
        maskB = cst_f[:, 640:768]
        cst_b = A.alloc(512, BF16)
        b_cstb = Buf("cstb")
        S.op("dve", _I("tensor_copy", out=cst_b, in_=cst_f[:, 0:512]), reads=[b_cst], writes=[b_cstb])
        ident_b = cst_b[:, 0:128]
        ones_b = cst_b[:, 384:512]
        vecs = A.alloc(192, F32)
        b_vecs = Buf("vecs")
        S.dma("sp", _I("dma_start", out=vecs, in_=vecs_d[0:1, :].partition_broadcast(128)[:, 0, :]), writes=[b_vecs])
        convp = A.alloc(24 * 8, F32)
        b_convp = Buf("convp")
        S.dma("sp", _I("dma_start", out=convp, in_=convp_d[:, :]), writes=[b_convp])
        negA = A.alloc(64, F32)
        esink = A.alloc(16, F32)
        b_negA = Buf("negA")
        b_esink = Buf("esink")
        S.op("act", _I("activation", out=negA, in_=vecs[:, 64:128], func=AF.Exp), reads=[b_vecs], writes=[b_negA])
        S.op("dve", _I("tensor_scalar", out=negA, in0=negA, scalar1=-1.0, scalar2=None, op0=ALU.mult), reads=[b_negA], writes=[b_negA])
        S.op("act", _I("activation", out=esink, in_=vecs[:, 160:176], func=AF.Exp), reads=[b_vecs], writes=[b_esink])
        KxT = A.alloc(8 * 256, BF16)
        Vx = A.alloc(2 * 1024, BF16)
        b_KxT = Buf("KxT")
        b_Vx = Buf("Vx")
        ss_t = [A.alloc(2, F32) for _ in range(4)]
        ss_b = [Buf("ss%d" % i) for i in range(4)]
        ss_rr = [0]
        junk = A.alloc(1024, BF16)
        b_junk = Buf("junk")
        neghalf = A.alloc(2, F32)
        b_neghalf = Buf("neghalf")
        S.op("pool", _I("memset", neghalf, -0.5), writes=[b_neghalf])
        PERSIST_MARK = A.mark()

        def dbg_tap(name, src_ap, src_bufs, dst_slice):
            if name in dbg_d:
                S.dma("pool", _I("dma_start", out=dst_slice(dbg_d[name]), in_=src_ap), reads=src_bufs,
                      writes=[Buf()], is_output=True)

        def rms_rows(xt, xb, g_ap, g_buf, out_bf, out_buf, dim=1024, eps=EPS, pre=1.0, post=1.0):
            i = ss_rr[0]
            ss_rr[0] = (i + 1) % 4
            ss, sb_ = ss_t[i], ss_b[i]
            S.op("act", _I("activation", out=junk[:, 0:dim], in_=xt, func=AF.Square, accum_out=ss[:, 0:1]),
                 reads=[xb], writes=[b_junk, sb_])
            S.op("pool", _I("tensor_scalar", out=ss[:, 1:2], in0=ss[:, 0:1], scalar1=pre / dim, scalar2=eps,
                                                   op0=ALU.mult, op1=ALU.add), reads=[sb_], writes=[sb_])
            S.op("pool", _I("tensor_tensor", out=ss[:, 1:2], in0=ss[:, 1:2], in1=neghalf[:, 0:1], op=ALU.pow),
                 reads=[sb_, b_neghalf], writes=[sb_])
            if post != 1.0:
                S.op("pool", _I("tensor_scalar", out=ss[:, 1:2], in0=ss[:, 1:2], scalar1=post, scalar2=None, op0=ALU.mult),
                     reads=[sb_], writes=[sb_])
            S.op("dve", _I("scalar_tensor_tensor", out=out_bf, in0=xt, scalar=ss[:, 1:2], in1=g_ap,
                                                         op0=ALU.mult, op1=ALU.mult),
                 reads=[xb, sb_, g_buf], writes=[out_buf])

        def transpose_blocks(src_bf, src_bufs, nblk, dst_view_fn, dst_bufs, evac="act", dst_bufs_fn=None):
            k = 0
            while k < nblk:
                n = min(8, nblk - k)
                pt, pb = psum()
                ptb = pt[:].bitcast(BF16)
                for j in range(n):
                    S.op("pe", _I("transpose", out=ptb[:, j * 128:(j + 1) * 128],
                                                                       in_=src_bf[:, (k + j) * 128:(k + j + 1) * 128],
                                                                       identity=ident_b),
                         reads=list(src_bufs) + [b_cstb], writes=[pb], inc=(j == n - 1))
                dst = dst_view_fn(k, n)
                srcv = v3(ptb[:, 0:n * 128], n)
                if dst_bufs_fn is not None:
                    dst_bufs = dst_bufs_fn(k, n)
                if evac == "act":
                    S.op("act", _I("copy", out=dst, in_=srcv), reads=[pb], writes=dst_bufs)
                else:
                    S.op(evac, _I("tensor_copy", out=dst, in_=srcv), reads=[pb], writes=dst_bufs)
                k += n

        def wtile_load(dst, dst_buf, wsrc, wbufs, r0, nkc, c0, ncols, eng="sp"):
            src = wsrc[r0:r0 + nkc * 128, c0:c0 + ncols].rearrange("(k p) n -> p k n", p=128)
            S.dma(eng, _I("dma_start", out=dst, in_=src), reads=wbufs, writes=[dst_buf])

        gains_t = {}

        def load_gain(idx, name):
            g = A.alloc(1024, F32)
            b = Buf("g_" + name)
            S.dma("sp", _I("dma_start", out=g, in_=gains_d[idx:idx + 1, :].partition_broadcast(128)[:, 0, :]), writes=[b])
            gains_t[name] = (g, b)
            return g, b

        def mm(out, lhsT, rhs, start, stop, reads, wbuf, inc, skip=False):
            if skip:
                S.op("pe", _I("matmul", out, lhsT=lhsT, rhs=rhs, start=start, stop=stop, skip_group_check=True),
                     reads=reads, writes=[wbuf], inc=inc)
            else:
                S.op("pe", _I("matmul", out, lhsT=lhsT, rhs=rhs, start=start, stop=stop),
                     reads=reads, writes=[wbuf], inc=inc)

        NEGBIG = LazyReg(-30000.0)

        def bc8(ap8):
            return ap8.unsqueeze(2).to_broadcast([128, 8, 64])

        acw = ac_scr.rearrange("(d c h l) -> d h c l", d=2, c=NCH, h=32, l=128)
        acr = ac_scr.rearrange("(r k) -> r k", k=1024)
        b_ac = [[Buf("ac%d_%d" % (d, c)) for c in range(NCH)] for d in range(2)]
        b_yn = [[Buf("yn%d_%d" % (c, g)) for g in range(4)] for c in range(NSEQ * NCH)]

        def ssd_phase(sq, tok0, hT3, b_hT):
            convp3 = v3(convp, 24)
            for g in range(4):
                mg = A.mark()
                Wz = A.alloc(8 * 512, BF16)
                Wz3 = v3(Wz, 8)
                b_Wz = Buf("Wz")
                wtile_load(Wz3, b_Wz, wb_in, WB["in"], 0, 8, OZ + g * 512, 512)
                Wdt = A.alloc(8 * 16, BF16)
                Wdt3 = v3(Wdt, 8)
                b_Wdt = Buf("Wdt")
                wtile_load(Wdt3, b_Wdt, wb_in, WB["in"], 0, 8, ODT + g * 16, 16)
                ssdg = A.alloc(512, F32)
                b_ssdg = Buf("ssdg")
                S.dma("sp", _I("dma_start", out=ssdg, in_=ssdg_d[0:1, g * 512:(g + 1) * 512].partition_broadcast(128)[:, 0, :]),
                      writes=[b_ssdg])

                NV = NCH * 16

                def small():
                    return A.alloc(NV, F32), Buf("small")

                dtv, b_dtv = small()
                dA, b_dA = small()
                a_sb, b_a = small()
                nega, b_nega = small()
                ea, b_ea = small()
                te, b_te = small()
                cd, b_cd = small()
                sdte, b_sdte = small()
                dtv3, dA3, a3, nega3, ea3, te3, cd3, sdte3 = [v3(t, NCH) for t in (dtv, dA, a_sb, nega, ea, te, cd, sdte)]
                pt, pb = psum()
                for c in range(NCH):
                    for kc in range(8):
                        mm(pt[:, c * 16:(c + 1) * 16], hT3[:, kc, c * 128:(c + 1) * 128], Wdt3[:, kc, :], kc == 0, kc == 7,
                           [b_hT[c], b_Wdt], pb, inc=(kc == 7 and c == NCH - 1))
                bias_g = vecs[:, g * 16:(g + 1) * 16]
                S.op("dve", _I("tensor_tensor", out=dtv3, in0=v3(pt[:, 0:NV], NCH),
                                                      in1=bias_g.unsqueeze(1).to_broadcast([128, NCH, 16]), op=ALU.add),
                     reads=[pb, b_vecs], writes=[b_dtv])
                S.op("act", _I("activation", out=dtv, in_=dtv, func=AF.Exp), reads=[b_dtv], writes=[b_dtv])
                S.op("act", _I("activation", out=dtv, in_=dtv, func=AF.Ln, bias=1.0), reads=[b_dtv], writes=[b_dtv])
                negA_g = negA[:, g * 16:(g + 1) * 16]
                S.op("dve", _I("tensor_tensor", out=dA3, in0=dtv3, in1=negA_g.unsqueeze(1).to_broadcast([128, NCH, 16]), op=ALU.mult),
                     reads=[b_dtv, b_negA], writes=[b_dA])
                pa, pab = psum()
                ptot, ptotb = psum()
                for c in range(NCH):
                    mm(pa[:, c * 16:c * 16 + 8], triF, dA3[:, c, 0:8], True, True, [b_cst, b_dA], pab, inc=False)
                    mm(pa[:, c * 16 + 8:c * 16 + 16], triB, dA3[:, c, 8:16], True, True, [b_cst, b_dA], pab, inc=(c == NCH - 1))
                for c in range(NCH):
                    mm(ptot[:, c * 16:(c + 1) * 16], ones_f, dA3[:, c, :], True, True, [b_cst, b_dA], ptotb, inc=(c == NCH - 1))
                S.op("act", _I("copy", out=a_sb, in_=pa[:, 0:NV]), reads=[pab], writes=[b_a])
                S.op("dve", _I("tensor_scalar", out=nega, in0=a_sb, scalar1=-1.0, scalar2=None, op0=ALU.mult), reads=[b_a], writes=[b_nega])
                S.op("act", _I("activation", out=ea, in_=a_sb, func=AF.Exp), reads=[b_a], writes=[b_ea])
                S.op("dve", _I("tensor_tensor", out=te, in0=ptot[:, 0:NV], in1=a_sb, op=ALU.subtract), reads=[ptotb, b_a], writes=[b_te])
                S.op("act", _I("activation", out=te, in_=te, func=AF.Exp), reads=[b_te], writes=[b_te])
                S.op("act", _I("activation", out=cd, in_=ptot[:, 0:NV], func=AF.Exp), reads=[ptotb], writes=[b_cd])
                S.op("dve", _I("tensor_tensor", out=sdte, in0=dtv, in1=te, op=ALU.mult), reads=[b_dtv, b_te], writes=[b_sdte])
                aTs = [(A.alloc(512, F32, parts=8), Buf("aT%d" % i)) for i in range(2)]
                for d in range(2):
                    tri = triF if d == 0 else triB
                    for cb in range(0, NCH, 4):
                        pT_, pTb = psum()
                        for j in range(4):
                            mm(pT_[0:8, j * 128:(j + 1) * 128], dA3[:, cb + j, d * 8:(d + 1) * 8], tri, True, True, [b_cst, b_dA], pTb, inc=(j == 3))
                        aT, b_aT = aTs[(d * (NCH // 4) + cb // 4) % 2]
                        S.op("act", _I("copy", out=aT, in_=pT_[0:8, :]), reads=[pTb], writes=[b_aT])
                        S.dma("pool", _I("dma_start", out=acw[d, 8 * g:8 * g + 8, cb:cb + 4, :], in_=v3(aT, 4)),
                              reads=[b_aT], writes=[b_ac[d][cb + j] for j in range(4)])

                BT = A.alloc(SEQ, BF16)
                b_BT = [Buf("BT%d" % s_) for s_ in range(NSB)]
                CT = A.alloc(SEQ, BF16)
                b_CT = [Buf("CT%d" % s_) for s_ in range(NSB)]
                x_tok = A.alloc(NCH * 512, BF16)
                x_tok3 = v3(x_tok, NCH)
                b_xtok = [Buf("xtok%d" % c) for c in range(NCH)]
                B_tok = A.alloc(NCH * 128, BF16)
                B_tok3 = v3(B_tok, NCH)
                b_Btok = [Buf("Btok%d" % c) for c in range(NCH)]
                mG2 = A.mark()
                Wg = A.alloc(8 * 768, BF16)
                Wg3 = v3(Wg, 8)
                b_Wg = [Buf("Wg%d" % i) for i in range(3)]
                wtile_load(Wg3[:, :, 0:512], b_Wg[0], wb_in, WB["in"], 0, 8, OXBC + g * 512, 512)
                wtile_load(Wg3[:, :, 512:640], b_Wg[1], wb_in, WB["in"], 0, 8, OXBC + 2048 + g * 128, 128)
                wtile_load(Wg3[:, :, 640:768], b_Wg[2], wb_in, WB["in"], 0, 8, OXBC + 2560 + g * 128, 128)
                Dg = A.alloc(30 * 128, BF16)
                Dg3 = v3(Dg, 30)
                b_Dg = Buf("Dg")

                def ctg(j):
                    return g * 4 + j if j < 4 else (16 + g if j == 4 else 20 + g)

                for j in range(6):
                    for k in range(5):
                        S.op("dve", _I("tensor_scalar", out=Dg3[:, j * 5 + k, :], in0=ident_b,
                                                                        scalar1=convp3[:, ctg(j), k:k + 1], scalar2=None, op0=ALU.mult),
                             reads=[b_cstb, b_convp], writes=[b_Dg])
                upad = []
                for i in range(2):
                    u = A.alloc(SEQ + 4, BF16)
                    bu = [Buf("u%d_%d" % (i, s_)) for s_ in range(NSB)] + [Buf("upad%d" % i)]
                    S.op("pool", _I("memset", u[:, 0:2], 0.0), writes=[bu[NSB]], inc=False)
                    S.op("pool", _I("memset", u[:, SEQ + 2:SEQ + 4], 0.0), writes=[bu[NSB]])
                    upad.append((u, bu))
                xTt = [(A.alloc(SEQ, BF16), [Buf("xTt%d_%d" % (i, s_)) for s_ in range(NSB)]) for i in range(2)]
                for j in range(6):
                    u, bu = upad[j % 2]
                    wcol = j * 128
                    wb_ = b_Wg[0] if j < 4 else b_Wg[j - 3]
                    for sb in range(NSB):
                        pt, pb = psum()
                        for kc in range(8):
                            mm(pt[:, :], Wg3[:, kc, wcol:wcol + 128], hT3[:, kc, sb * 512:(sb + 1) * 512], kc == 0, kc == 7,
                               [wb_] + b_hT[sb * 4:sb * 4 + 4], pb, inc=(kc == 7))
                        S.op("act", _I("copy", out=u[:, 2 + sb * 512:2 + (sb + 1) * 512], in_=pt[:, :]),
                             reads=[pb], writes=[bu[sb]])
                    if j < 4:
                        dst, b_dst = xTt[j % 2]
                    elif j == 4:
                        dst, b_dst = BT, b_BT
                    else:
                        dst, b_dst = CT, b_CT
                    for sb in range(NSB):
                        pt, pb = psum()
                        rd = [b_Dg, bu[NSB]] + [bu[s_] for s_ in range(max(0, sb - 1), min(NSB, sb + 2))]
                        for k in range(5):
                            mm(pt[:, :], Dg3[:, j * 5 + k, :], u[:, sb * 512 + k:sb * 512 + k + 512], k == 0, k == 4, rd, pb, inc=(k == 4))
                        S.op("act", _I("activation", out=dst[:, sb * 512:(sb + 1) * 512], in_=pt[:, :], func=AF.Silu,
                                                                                     bias=convp3[:, ctg(j), 5:6], scale=1.0),
                             reads=[pb, b_convp], writes=[b_dst[sb]])
                    if j < 4:
                        transpose_blocks(dst, b_dst, NCH, lambda k0, n, j=j: x_tok3[:, k0:k0 + n, j * 128:(j + 1) * 128],
                                         None, evac="dve", dst_bufs_fn=lambda k0, n: b_xtok[k0:k0 + n])
                    elif j == 4:
                        transpose_blocks(dst, b_dst, NCH, lambda k0, n: B_tok3[:, k0:k0 + n, :],
                                         None, evac="dve", dst_bufs_fn=lambda k0, n: b_Btok[k0:k0 + n])

                S.barrier(("pe", "act", "dve", "pool", "sp"))
                A.reset(mG2)
                hbin = A.alloc(NCH * 512, BF16)
                hbin3 = v3(hbin, NCH)
                b_hbin = [Buf("hbin%d" % c) for c in range(NCH)]
                hb = A.alloc(512, F32)
                b_hb = Buf("hb")
                S.op("pool", _I("memset", hb, 0.0), writes=[b_hb])
                xvs = [(A.alloc(512, BF16), Buf("xv%d" % i)) for i in range(3)]
                xv_rr = [0]

                def variant(c, scale_ap8, reads, eng="dve"):
                    i = xv_rr[0]
                    xv_rr[0] = (i + 1) % 3
                    xv, bx = xvs[i]
                    S.op(eng, _I("tensor_tensor", out=v3(xv, 8), in0=v3(x_tok3[:, c, :], 8), in1=bc8(scale_ap8), op=ALU.mult),
                         reads=[b_xtok[c]] + reads, writes=[bx])
                    return xv, bx

                for c in range(NCH - 1, -1, -1):
                    S.op("act", _I("copy", out=hbin3[:, c, :], in_=hb), reads=[b_hb], writes=[b_hbin[c]])
                    if c == 0:
                        break
                    xv, bx = variant(c, sdte3[:, c, 8:16], [b_sdte])
                    pt, pb = psum()
                    mm(pt[:, :], B_tok3[:, c, :], xv, True, True, [b_Btok[c], bx], pb, inc=True)
                    S.op("dve", _I("tensor_tensor", out=v3(hb, 8), in0=v3(hb, 8), in1=bc8(cd3[:, c, 8:16]), op=ALU.mult),
                         reads=[b_hb, b_cd], writes=[b_hb])
                    S.op("dve", _I("tensor_tensor", out=hb, in0=hb, in1=pt[:, :], op=ALU.add), reads=[b_hb, pb], writes=[b_hb])

                hf = A.alloc(512, F32)
                b_hf = Buf("hf")
                hfb = A.alloc(512, BF16)
                b_hfb = Buf("hfb")
                S.op("pool", _I("memset", hf, 0.0), writes=[b_hf])
                S.op("pool", _I("memset", hfb, 0.0), writes=[b_hfb])
                bcs = [[(A.alloc(1024, F32), Buf("bc%d_%d" % (d, i))) for i in range(2)] for d in range(2)]
                Es = [[(A.alloc(1024, BF16), Buf("E%d_%d" % (d, i))) for i in range(2)] for d in range(2)]
                Ms = [[(A.alloc(1024, BF16), Buf("M%d_%d" % (d, i))) for i in range(2)] for d in range(2)]
                f32s = [(A.alloc(512, F32), Buf("f32_%d" % i)) for i in range(4)]
                f_rr = [0]

                def f32tile():
                    i = f_rr[0]
                    f_rr[0] = (i + 1) % 4
                    return f32s[i]

                ths = [(A.alloc(512, F32), Buf("thA%d" % i)) for i in range(2)]
                xAs = [[(A.alloc(512, BF16), Buf("xA%d_%d" % (i, k))) for k in range(3)] for i in range(2)]
                xEs = [(A.alloc(512, BF16), Buf("xE%d" % i)) for i in range(2)]
                yns = [(A.alloc(512, BF16), Buf("yn%d" % i)) for i in range(2)]
                D_g = vecs[:, 128 + g * 8:128 + (g + 1) * 8]

                def variant_to(tile_buf, c, scale_ap8, reads, eng="dve"):
                    xv, bx = tile_buf
                    S.op(eng, _I("tensor_tensor", out=v3(xv, 8), in0=v3(x_tok3[:, c, :], 8), in1=bc8(scale_ap8), op=ALU.mult),
                         reads=[b_xtok[c]] + reads, writes=[bx])
                    return xv, bx

                def stageA(c):
                    par = c % 2
                    EM = []
                    for d in range(2):
                        bc, b_bc = bcs[d][par]
                        row = (d * NCH + c) * 4 + g
                        S.dma("sp", _I("dma_start", out=bc, in_=acr[row:row + 1, :].partition_broadcast(128)[:, 0, :]),
                              reads=[b_ac[d][c]], writes=[b_bc])
                        E, b_E = Es[d][par]
                        E3 = v3(E, 8)
                        bc3 = v3(bc, 8)
                        msk = maskF if d == 0 else maskB
                        S.op("pool", _I("tensor_tensor", out=bc3, in0=bc3, in1=msk.unsqueeze(1).to_broadcast([128, 8, 128]), op=ALU.add),
                             reads=[b_bc, b_cst], writes=[b_bc])
                        for h in range(8):
                            S.op("act", _I("activation", out=E3[:, h, :], in_=bc3[:, h, :], func=AF.Exp,
                                           bias=nega3[:, c, d * 8 + h:d * 8 + h + 1], scale=1.0),
                                 reads=[b_bc, b_nega], writes=[b_E], inc=(h == 7))
                        EM.append((E3, b_E))
                    pcb, pcbb = psum()
                    sbi = c // 4
                    mm(pcb[:, 0:128], BT[:, c * 128:(c + 1) * 128], CT[:, c * 128:(c + 1) * 128], True, True, [b_BT[sbi], b_CT[sbi]], pcbb, inc=True)
                    Mt = []
                    for d in range(2):
                        M, b_M = Ms[d][par]
                        M3 = v3(M, 8)
                        E3, b_E = EM[d]
                        S.op("dve", _I("tensor_tensor", out=M3, in0=E3, in1=pcb[:, 0:128].unsqueeze(1).to_broadcast([128, 8, 128]),
                                       op=ALU.mult), reads=[b_E, pcbb], writes=[b_M])
                        Mt.append((M3, b_M))
                    xdf, b_xdf = variant_to(xAs[par][0], c, dtv3[:, c, 0:8], [b_dtv])
                    xdb, b_xdb = variant_to(xAs[par][1], c, dtv3[:, c, 8:16], [b_dtv])
                    xD, b_xD = variant_to(xAs[par][2], c, D_g, [b_vecs], eng="pool")
                    pZ, pZb = psum()
                    for kc in range(8):
                        mm(pZ[:, :], hT3[:, kc, c * 128:(c + 1) * 128], Wz3[:, kc, :], kc == 0, kc == 7, [b_hT[c], b_Wz], pZb, inc=(kc == 7))
                    th, b_th = ths[par]
                    S.op("act", _I("activation", out=th, in_=pZ[:, :], func=AF.Tanh, scale=0.5), reads=[pZb], writes=[b_th])
                    S.op("dve", _I("scalar_tensor_tensor", out=th, in0=th, scalar=1.0, in1=pZ[:, :], op0=ALU.add, op1=ALU.mult),
                         reads=[b_th, pZb], writes=[b_th])
                    return dict(Mt=Mt, xdf=(xdf, b_xdf), xdb=(xdb, b_xdb), xD=(xD, b_xD), th=(th, b_th), sbi=sbi, par=par)

                def stageB(c, ctx):
                    Mt = ctx["Mt"]
                    xdf, b_xdf = ctx["xdf"]
                    xdb, b_xdb = ctx["xdb"]
                    xD, b_xD = ctx["xD"]
                    th, b_th = ctx["th"]
                    sbi, par = ctx["sbi"], ctx["par"]
                    pY, pYb = psum()
                    mm(pY[:, :], ident_b, xD, True, False, [b_cstb, b_xD], pYb, inc=False, skip=True)
                    for h in range(8):
                        mm(pY[:, h * 64:(h + 1) * 64], Mt[0][0][:, h, :], xdf[:, h * 64:(h + 1) * 64], False, False, [Mt[0][1], b_xdf], pYb, inc=False, skip=True)
                        mm(pY[:, h * 64:(h + 1) * 64], Mt[1][0][:, h, :], xdb[:, h * 64:(h + 1) * 64], False, h == 7, [Mt[1][1], b_xdb], pYb, inc=(h == 7), skip=True)
                    pF, pFb = psum()
                    mm(pF[:, :], CT[:, c * 128:(c + 1) * 128], hfb, True, True, [b_CT[sbi], b_hfb], pFb, inc=True)
                    pB, pBb = psum()
                    mm(pB[:, :], CT[:, c * 128:(c + 1) * 128], hbin3[:, c, :], True, True, [b_CT[sbi], b_hbin[c]], pBb, inc=True)
                    if c < NCH - 1:
                        xef, b_xef = variant_to(xEs[par], c, sdte3[:, c, 0:8], [b_sdte], eng="pool")
                        pS, pSb = psum()
                        mm(pS[:, :], B_tok3[:, c, :], xef, True, True, [b_Btok[c], b_xef], pSb, inc=True)
                        S.op("dve", _I("tensor_tensor", out=v3(hf, 8), in0=v3(hf, 8), in1=bc8(cd3[:, c, 0:8]), op=ALU.mult),
                             reads=[b_hf, b_cd], writes=[b_hf])
                        S.op("dve", _I("tensor_tensor", out=hf, in0=hf, in1=pS[:, :], op=ALU.add), reads=[b_hf, pSb], writes=[b_hf])
                        S.op("act", _I("copy", out=hfb, in_=hf), reads=[b_hf], writes=[b_hfb])
                    t1, b_t1 = f32tile()
                    t2, b_t2 = f32tile()
                    S.op("dve", _I("tensor_tensor", out=v3(t1, 8), in0=v3(pF[:, :], 8), in1=bc8(ea3[:, c, 0:8]), op=ALU.mult),
                         reads=[pFb, b_ea], writes=[b_t1])
                    S.op("dve", _I("tensor_tensor", out=v3(t2, 8), in0=v3(pB[:, :], 8), in1=bc8(ea3[:, c, 8:16]), op=ALU.mult),
                         reads=[pBb, b_ea], writes=[b_t2])
                    S.op("pool", _I("tensor_tensor", out=t1, in0=t1, in1=t2, op=ALU.add), reads=[b_t1, b_t2], writes=[b_t1])
                    S.op("dve", _I("tensor_tensor", out=t1, in0=t1, in1=pY[:, :], op=ALU.add), reads=[b_t1, pYb], writes=[b_t1])
                    if "y" in dbg_d and sq == 0:
                        dbg_tap("y", t1, [b_t1], lambda d_, c=c: d_[c * 128:(c + 1) * 128, g * 512:(g + 1) * 512])
                    S.op("pool", _I("tensor_tensor", out=t1, in0=t1, in1=th, op=ALU.mult), reads=[b_th, b_t1], writes=[b_t1])
                    yn, b_ynt = yns[par]
                    rms_rows(t1, b_t1, ssdg, b_ssdg, yn, b_ynt, dim=512, pre=0.25, post=0.5)
                    S.dma("pool", _I("dma_start", out=yn_scr[tok0 + c * 128:tok0 + (c + 1) * 128, g * 512:(g + 1) * 512], in_=yn),
                          reads=[b_ynt], writes=[b_yn[sq * NCH + c][g]])

                ctxs = {0: stageA(0)}
                for c in range(NCH):
                    if c + 1 < NCH:
                        ctxs[c + 1] = stageA(c + 1)
                    stageB(c, ctxs.pop(c))
                S.barrier(("pe", "act", "dve", "pool", "sp"))
                A.reset(mg)

        def token_phase(sq, tok0):
            g_mix, b_gmix = load_gain(0, "mix")
            g_xa, b_gxa = load_gain(1, "xattn")
            g_ffn, b_gffn = load_gain(3, "ffn")
            g_fin, b_gfin = load_gain(4, "final")
            wslots = [(A.alloc(8 * 512, BF16), Buf("wslot%d" % i)) for i in range(4)]
            w_rr = [0]

            def wload(wsrc, wkey, r0, nkc, c0, ncols):
                i = w_rr[0]
                w_rr[0] = (i + 1) % len(wslots)
                t, b = wslots[i]
                view = v3(t[:, 0:nkc * ncols], nkc)
                wtile_load(view, b, wsrc, WB[wkey], r0, nkc, c0, ncols)
                return view, b

            xres = A.alloc(4 * 1024, F32)
            xres3 = v3(xres, 4)
            b_x = [Buf("xres%d" % i) for i in range(4)]
            xh = A.alloc(1024, F32)
            b_xh = Buf("xh")
            hns = [(A.alloc(1024, BF16), Buf("hn%d" % i)) for i in range(2)]
            hn_rr = [0]
            hT6 = A.alloc(8 * 768, BF16)
            hT63 = v3(hT6, 8)
            b_hT6 = [Buf("hT6_%d" % i) for i in range(6)]
            KrT6 = A.alloc(4 * 2 * 768, BF16)
            KrT64 = v4(KrT6, 4, 2)
            b_KrT = Buf("KrT6")
            S.op("pool", _I("memset", KrT6, 0.0), writes=[b_KrT])
            Vaug = A.alloc(6 * 260, BF16)
            Vaug4 = v4(Vaug, 6, 4)
            b_Vaug = Buf("Vaug")
            cos6 = A.alloc(768, F32)
            sin6 = A.alloc(768, F32)
            b_cos = Buf("cos6")
            b_sin = Buf("sin6")
            FM = {}
            for nm in ("M1", "QT", "AT", "MT"):
                t = A.alloc(8 * 512, BF16)
                FM[nm] = (v3(t, 8), Buf(nm))
            R1 = A.alloc(12 * 1024, BF16)
            ynT3 = v3(R1[:, 0:8192], 16)
            ynbs = [R1[:, 8192:10240], R1[:, 10240:12288]]
            actT3 = v3(R1[:, 0:22 * 512], 22)
            b_R1 = Buf("R1")
            b_ynb = [Buf("ynb0"), Buf("ynb1")]
            PTs = [(A.alloc(512, BF16), Buf("PT%d" % i)) for i in range(12)]
            PxT = [(A.alloc(512, BF16), Buf("PxT%d" % i)) for i in range(2)]
            ths = [(A.alloc(512, BF16), Buf("th%d" % i)) for i in range(2)]
            th_rr = [0]
            ftm = [(A.alloc(512, F32), Buf("ftm%d" % i)) for i in range(3)]
            f_rr = [0]
            attn_bfs = [(A.alloc(1024, BF16), Buf("attnbf%d" % i)) for i in range(2)]
            outf = [(A.alloc(1024, F32), Buf("outf%d" % i)) for i in range(1)]
            dens = A.alloc(8, F32)
            b_dens = Buf("dens")

            def ftile():
                i = f_rr[0]
                f_rr[0] = (i + 1) % len(ftm)
                return ftm[i]

            def thtile():
                i = th_rr[0]
                th_rr[0] = (i + 1) % len(ths)
                return ths[i]

            def hntile():
                i = hn_rr[0]
                hn_rr[0] = (i + 1) % 2
                return hns[i]

            def proj(wv, wb, mcol, rhs, rbufs, n):
                pt, pb = psum()
                nk = wv.shape[1]
                for kc in range(nk):
                    mm(pt[:, 0:n], wv[:, kc, mcol:mcol + 128], rhs(kc), kc == 0, kc == nk - 1, [wb] + rbufs, pb, inc=(kc == nk - 1))
                return pt, pb

            S.op("pool", _I("memset", Vaug4[:, :, :, 64:65], 1.0), writes=[b_Vaug])

            for sb in range(NSB):
                cg0 = sb * 4
                valid = [li for li in range(6) if 0 <= cg0 - 1 + li < NCH]
                lo, hi = valid[0], valid[-1] + 1
                gl = (cg0 - 1 + lo) * 128
                S.dma("sp", _I("dma_start", out=cos6[:, lo * 128:hi * 128], in_=rope_d[:, gl:gl + (hi - lo) * 128]), writes=[b_cos])
                S.dma("sp", _I("dma_start", out=sin6[:, lo * 128:hi * 128], in_=rope_d[:, SEQ + gl:SEQ + gl + (hi - lo) * 128]), writes=[b_sin])
                for li in valid:
                    c = cg0 - 1 + li
                    if 1 <= li <= 4:
                        xt, bxt = xres3[:, li - 1, :], b_x[li - 1]
                    else:
                        xt, bxt = xh, b_xh
                    S.dma("sp", _I("dma_start", out=xt, in_=x_d[tok0 + c * 128:tok0 + (c + 1) * 128, :]), writes=[bxt])
                    hn, b_hn = hntile()
                    rms_rows(xt, bxt, g_mix, b_gmix, hn, b_hn)
                    transpose_blocks(hn, [b_hn], 8, lambda k0, n, li=li: hT63[:, k0:k0 + n, li * 128:(li + 1) * 128], [b_hT6[li]])
                own = [b_hT6[i] for i in range(1, 5)]
                hown = lambda kc: hT63[:, kc, 128:640]
                Wk, bWk = wload(wb_in, "in", 0, 8, OK_, 512)
                Wkp, bWkp = wload(wb_in, "in", 0, 8, OKP, 512)
                Wv, bWv = wload(wb_in, "in", 0, 8, OV, 256)
                segs = [(128, 640, own)]
                if 0 in valid:
                    segs.append((0, 128, [b_hT6[0]]))
                if 5 in valid:
                    segs.append((640, 768, [b_hT6[5]]))
                for (s0, s1, sbufs) in segs:
                    n = s1 - s0
                    for j in range(4):
                        p1, p1b = proj(Wk, bWk, j * 128, lambda kc: hT63[:, kc, s0:s1], sbufs, n)
                        p2, p2b = proj(Wkp, bWkp, j * 128, lambda kc: hT63[:, kc, s0:s1], sbufs, n)
                        t1, bt1 = ftile()
                        t2, bt2 = ftile()
                        S.op("dve", _I("tensor_tensor", out=t1[:, 0:n], in0=p1[:, 0:n], in1=cos6[:, s0:s1], op=ALU.mult), reads=[p1b, b_cos], writes=[bt1])
                        S.op("dve", _I("tensor_tensor", out=t2[:, 0:n], in0=p2[:, 0:n], in1=sin6[:, s0:s1], op=ALU.mult), reads=[p2b, b_sin], writes=[bt2])
                        S.op("pool", _I("tensor_tensor", out=KrT64[0:64, j, 0, s0:s1], in0=t1[0:64, 0:n], in1=t2[0:64, 0:n], op=ALU.add),
                             reads=[bt1, bt2], writes=[b_KrT])
                        S.op("pool", _I("tensor_tensor", out=KrT64[64:128, j, 1, s0:s1], in0=t1[64:128, 0:n], in1=t2[64:128, 0:n], op=ALU.add),
                             reads=[bt1, bt2], writes=[b_KrT])
                for li in valid:
                    pt, pb = psum()
                    for kc in range(8):
                        mm(pt[:, 0:256], hT63[:, kc, li * 128:(li + 1) * 128], Wv[:, kc, :], kc == 0, kc == 7, [bWv, b_hT6[li]], pb, inc=(kc == 7))
                    S.op("act", _I("copy", out=Vaug4[:, li, :, 0:64], in_=v3(pt[:, 0:256], 4)), reads=[pb], writes=[b_Vaug])
                QT3, b_QT = FM["QT"]
                for blk in range(2):
                    Wq, bWq = wload(wb_in, "in", 0, 8, OQ + blk * 512, 512)
                    Wqp, bWqp = wload(wb_in, "in", 0, 8, OQP + blk * 512, 512)
                    for mt in range(4):
                        t = blk * 4 + mt
                        p1, p1b = proj(Wq, bWq, mt * 128, hown, own, 512)
                        p2, p2b = proj(Wqp, bWqp, mt * 128, hown, own, 512)
                        t1, bt1 = ftile()
                        t2, bt2 = ftile()
                        S.op("dve", _I("tensor_tensor", out=t1, in0=p1[:, :], in1=cos6[:, 128:640], op=ALU.mult), reads=[p1b, b_cos], writes=[bt1])
                        S.op("dve", _I("tensor_tensor", out=t2, in0=p2[:, :], in1=sin6[:, 128:640], op=ALU.mult), reads=[p2b, b_sin], writes=[bt2])
                        S.op("pool", _I("tensor_tensor", out=QT3[:, t, :], in0=t1, in1=t2, op=ALU.add), reads=[bt1, bt2], writes=[b_QT])
                for cq in range(4):
                    ynb, bynb = ynbs[cq % 2], b_ynb[cq % 2]
                    r0 = tok0 + (cg0 + cq) * 128
                    S.dma("sp", _I("dma_start", out=ynb, in_=yn_scr[r0:r0 + 128, :]), reads=b_yn[sq * NCH + cg0 + cq], writes=[bynb, b_R1])
                    transpose_blocks(ynb, [bynb], 16, lambda k0, n, cq=cq: ynT3[:, k0:k0 + n, cq * 128:(cq + 1) * 128], [b_R1])
                M13, b_M1 = FM["M1"]
                TH3, b_TH = FM["MT"]
                for blk in range(2):
                    Wgt, bWgt = wload(wb_in, "in", 0, 8, OG + blk * 512, 512)
                    for mt in range(4):
                        t = blk * 4 + mt
                        pg, pgb = proj(Wgt, bWgt, mt * 128, hown, own, 512)
                        S.op("act", _I("activation", out=TH3[:, t, :], in_=pg[:, :], func=AF.Tanh, scale=0.5), reads=[pgb], writes=[b_TH])
                for blk in range(2):
                    Wb0, bWb0 = wload(wb_bs, "bs", 0, 8, blk * 512, 512)
                    Wb1, bWb1 = wload(wb_bs, "bs", 1024, 8, blk * 512, 512)
                    for mt in range(4):
                        t = blk * 4 + mt
                        pbr, pbrb = psum()
                        for kc in range(16):
                            wv_, wb_ = (Wb0, bWb0) if kc < 8 else (Wb1, bWb1)
                            mm(pbr[:, :], wv_[:, kc % 8, mt * 128:(mt + 1) * 128], ynT3[:, kc, :], kc == 0, kc == 15, [wb_, b_R1], pbrb, inc=(kc == 15))
                        S.op("dve", _I("scalar_tensor_tensor", out=M13[:, t, :], in0=TH3[:, t, :], scalar=1.0, in1=pbr[:, :], op0=ALU.add, op1=ALU.mult),
                             reads=[b_TH, pbrb], writes=[b_M1])
                AT3, b_AT = FM["AT"]
                ps_sub[0] = [0, 1, 2, 3]
                for cq in range(4):
                    lq = cq + 1
                    kbs = [kb for kb in (lq - 1, lq, lq + 1) if kb in valid]
                    Ob = [(PS[4 + i], PSB[4 + i]) for i in range(4)]
                    pt_i = 0
                    for j in range(4):
                        ptl = []
                        for kb in kbs:
                            pS_, pSb = psum()
                            for r in range(4):
                                i = 4 * j + r
                                mm(pS_[:, r * 128:(r + 1) * 128], KrT64[:, j, i % 2, kb * 128:(kb + 1) * 128],
                                   QT3[:, i // 2, cq * 128:(cq + 1) * 128], True, True, [b_KrT, b_QT], pSb, inc=(r == 3))
                            PT, bPT = PTs[pt_i]
                            pt_i += 1
                            S.op("act", _I("activation", out=PT, in_=pS_[:, :], func=AF.Exp, scale=0.125), reads=[pSb], writes=[bPT])
                            PT3 = v3(PT, 4)
                            if kb == lq - 1:
                                S.op("pool", _I("affine_select", out=PT3, in_=PT3, pattern=[[0, 4], [-1, 128]], compare_op=ALU.is_ge, fill=0.0,
                                                base=0, channel_multiplier=1), reads=[bPT], writes=[bPT])
                            elif kb == lq + 1:
                                S.op("pool", _I("affine_select", out=PT3, in_=PT3, pattern=[[0, 4], [1, 128]], compare_op=ALU.is_ge, fill=0.0,
                                                base=0, channel_multiplier=-1), reads=[bPT], writes=[bPT])
                            ptl.append((PT, bPT, kb))
                        for r in range(4):
                            i = 4 * j + r
                            O_, Ob_ = Ob[i // 4]
                            oc = (i % 4) * 65
                            for n_, (PT, bPT, kb) in enumerate(ptl):
                                mm(O_[:, oc:oc + 65], PT[:, r * 128:(r + 1) * 128], Vaug4[:, kb, j, :], n_ == 0, n_ == len(ptl) - 1,
                                   [bPT, b_Vaug], Ob_, inc=(n_ == len(ptl) - 1))
                    abf, b_abf = attn_bfs[cq % 2]
                    abf3 = v3(abf, 16)
                    for bnk in range(4):
                        O_, Ob_ = Ob[bnk]
                        O3 = O_[:, 0:260].rearrange("p (h d) -> p h d", h=4)
                        S.op("dve", _I("tensor_tensor", out=dens[:, 0:4], in0=O3[:, :, 64], in1=esink[:, bnk * 4:(bnk + 1) * 4], op=ALU.add),
                             reads=[Ob_, b_esink], writes=[b_dens])
                        S.op("dve", _I("reciprocal", out=dens[:, 4:8], in_=dens[:, 0:4]), reads=[b_dens], writes=[b_dens])
                        S.op("dve", _I("tensor_tensor", out=abf3[:, bnk * 4:(bnk + 1) * 4, :], in0=O3[:, :, 0:64],
                                       in1=dens[:, 4:8].unsqueeze(2).to_broadcast([128, 4, 64]), op=ALU.mult), reads=[Ob_, b_dens], writes=[b_abf])
                    if "attn" in dbg_d and sq == 0:
                        tf, btf = outf[0]
                        S.op("act", _I("copy", out=tf, in_=abf), reads=[b_abf], writes=[btf])
                        dbg_tap("attn", tf, [btf], lambda d_: d_[(cg0 + cq) * 128:(cg0 + cq + 1) * 128, :])
                    transpose_blocks(abf, [b_abf], 8, lambda k0, n, cq=cq: AT3[:, k0:k0 + n, cq * 128:(cq + 1) * 128], [b_AT])
                ps_sub[0] = list(range(8))
                MT3, b_MT = FM["MT"]
                TA3, b_TA = FM["QT"]
                for blk in range(2):
                    Wgt, bWgt = wload(wb_in, "in", 0, 8, OG + 1024 + blk * 512, 512)
                    for mt in range(4):
                        t = blk * 4 + mt
                        pg, pgb = proj(Wgt, bWgt, mt * 128, hown, own, 512)
                        S.op("act", _I("activation", out=TA3[:, t, :], in_=pg[:, :], func=AF.Tanh, scale=0.5), reads=[pgb], writes=[b_TA])
                for blk in range(2):
                    Wa, bWa = wload(wb_ba, "ba", 0, 8, blk * 512, 512)
                    for mt in range(4):
                        t = blk * 4 + mt
                        pa_, pab_ = proj(Wa, bWa, mt * 128, lambda kc: AT3[:, kc, :], [b_AT], 512)
                        tm, btm = ftile()
                        S.op("dve", _I("scalar_tensor_tensor", out=tm, in0=TA3[:, t, :], scalar=1.0, in1=pa_[:, :], op0=ALU.add, op1=ALU.mult),
                             reads=[b_TA, pab_], writes=[btm])
                        S.op("pool", _I("tensor_tensor", out=MT3[:, t, :], in0=tm, in1=M13[:, t, :], op=ALU.add), reads=[btm, b_M1], writes=[b_MT])
                for half in range(2):
                    Wm, bWm = wload(wb_mo, "mo", 0, 8, half * 512, 512)
                    for cq in range(4):
                        pt, pb = psum()
                        for kc in range(8):
                            mm(pt[:, :], MT3[:, kc, cq * 128:(cq + 1) * 128], Wm[:, kc, :], kc == 0, kc == 7, [b_MT, bWm], pb, inc=(kc == 7))
                        xs = xres3[:, cq, half * 512:(half + 1) * 512]
                        S.op("dve", _I("scalar_tensor_tensor", out=xs, in0=pt[:, :], scalar=0.5, in1=xs, op0=ALU.mult, op1=ALU.add),
                             reads=[pb, b_x[cq]], writes=[b_x[cq]])
                if "x1" in dbg_d and sq == 0:
                    for cq in range(4):
                        dbg_tap("x1", xres3[:, cq, :], [b_x[cq]], lambda d_, cq=cq: d_[(cg0 + cq) * 128:(cg0 + cq + 1) * 128, :])
                H2T3, b_H2T = FM["AT"]
                for cq in range(4):
                    hn, b_hn = hntile()
                    rms_rows(xres3[:, cq, :], b_x[cq], g_xa, b_gxa, hn, b_hn)
                    transpose_blocks(hn, [b_hn], 8, lambda k0, n, cq=cq: H2T3[:, k0:k0 + n, cq * 128:(cq + 1) * 128], [b_H2T])
                QX3, b_QX = FM["QT"]
                for blk in range(2):
                    Wq, bWq = wload(wb_xq, "xq", 0, 8, blk * 512, 512)
                    for mt in range(4):
                        pq, pqb = proj(Wq, bWq, mt * 128, lambda kc: H2T3[:, kc, :], [b_H2T], 512)
                        S.op("act", _I("copy", out=QX3[:, blk * 4 + mt, :], in_=pq[:, :]), reads=[pqb], writes=[b_QX])
                XA3, b_XA = FM["M1"]
                KxT3 = v3(KxT, 8)
                Vx3 = v3(Vx, 2)
                for hx in range(4):
                    for mtile in range(2):
                        pS_, pSb = psum()
                        for dt_ in range(2):
                            mm(pS_[:, :], KxT3[:, 2 * hx + dt_, mtile * 128:(mtile + 1) * 128], QX3[:, 2 * hx + dt_, :], dt_ == 0, dt_ == 1,
                               [b_KxT, b_QX], pSb, inc=(dt_ == 1))
                        S.op("act", _I("activation", out=PxT[mtile][0], in_=pS_[:, :], func=AF.Exp, scale=0.0625), reads=[pSb], writes=[PxT[mtile][1]])
                    pD, pDb = psum()
                    for mtile in range(2):
                        mm(pD[:, :], ones_b, PxT[mtile][0], mtile == 0, mtile == 1, [b_cstb, PxT[mtile][1]], pDb, inc=(mtile == 1))
                    rc, brc = ftile()
                    S.op("dve", _I("reciprocal", out=rc, in_=pD[:, :]), reads=[pDb], writes=[brc])
                    for dt_ in range(2):
                        pO, pOb = psum()
                        for mtile in range(2):
                            mm(pO[:, :], Vx3[:, mtile, hx * 256 + dt_ * 128:hx * 256 + (dt_ + 1) * 128], PxT[mtile][0], mtile == 0, mtile == 1,
                               [b_Vx, PxT[mtile][1]], pOb, inc=(mtile == 1))
                        S.op("dve", _I("tensor_tensor", out=XA3[:, 2 * hx + dt_, :], in0=pO[:, :], in1=rc, op=ALU.mult), reads=[pOb, brc], writes=[b_XA])
                for half in range(2):
                    Wo, bWo = wload(wb_xo, "xo", 0, 8, half * 512, 512)
                    for cq in range(4):
                        pt, pb = psum()
                        for kc in range(8):
                            mm(pt[:, :], XA3[:, kc, cq * 128:(cq + 1) * 128], Wo[:, kc, :], kc == 0, kc == 7, [b_XA, bWo], pb, inc=(kc == 7))
                        xs = xres3[:, cq, half * 512:(half + 1) * 512]
                        S.op("dve", _I("tensor_tensor", out=xs, in0=pt[:, :], in1=xs, op=ALU.add), reads=[pb, b_x[cq]], writes=[b_x[cq]])
                if "x2" in dbg_d and sq == 0:
                    for cq in range(4):
                        dbg_tap("x2", xres3[:, cq, :], [b_x[cq]], lambda d_, cq=cq: d_[(cg0 + cq) * 128:(cg0 + cq + 1) * 128, :])
                H3T3, b_H3T = FM["MT"]
                for cq in range(4):
                    hn, b_hn = hntile()
                    rms_rows(xres3[:, cq, :], b_x[cq], g_ffn, b_gffn, hn, b_hn)
                    transpose_blocks(hn, [b_hn], 8, lambda k0, n, cq=cq: H3T3[:, k0:k0 + n, cq * 128:(cq + 1) * 128], [b_H3T])
                S.barrier(("pe", "act", "dve", "pool"))
                for blk in range(6):
                    ncols = min(512, FFN_H - blk * 512)
                    Wga, bWga = wload(wb_fi, "fi", 0, 8, blk * 512, ncols)
                    Wup, bWup = wload(wb_fi, "fi", 0, 8, FFN_H + blk * 512, ncols)
                    for mt in range(ncols // 128):
                        t = blk * 4 + mt
                        pg, pgb = proj(Wga, bWga, mt * 128, lambda kc: H3T3[:, kc, :], [b_H3T], 512)
                        pu, pub = proj(Wup, bWup, mt * 128, lambda kc: H3T3[:, kc, :], [b_H3T], 512)
                        th, bth = thtile()
                        S.op("act", _I("activation", out=th, in_=pg[:, :], func=AF.Tanh, scale=0.5), reads=[pgb], writes=[bth])
                        tm, btm = ftile()
                        S.op("dve", _I("scalar_tensor_tensor", out=tm, in0=th, scalar=1.0, in1=pg[:, :], op0=ALU.add, op1=ALU.mult),
                             reads=[bth, pgb], writes=[btm])
                        S.op("dve", _I("tensor_tensor", out=actT3[:, t, :], in0=tm, in1=pu[:, :], op=ALU.mult), reads=[btm, pub], writes=[b_R1])
                for half in range(2):
                    acc = [(PS[4 + i], PSB[4 + i]) for i in range(4)]
                    for part in range(3):
                        nkc = 8 if part < 2 else 6
                        Wf, bWf = wload(wb_fo, "fo", part * 1024, nkc, half * 512, 512)
                        for cq in range(4):
                            for kk in range(nkc):
                                kc = part * 8 + kk
                                mm(acc[cq][0][:, :], actT3[:, kc, cq * 128:(cq + 1) * 128], Wf[:, kk, :], kc == 0, kc == 21, [b_R1, bWf], acc[cq][1],
                                   inc=(kk == nkc - 1))
                    for cq in range(4):
                        xs = xres3[:, cq, half * 512:(half + 1) * 512]
                        S.op("dve", _I("scalar_tensor_tensor", out=xs, in0=acc[cq][0][:, :], scalar=0.5, in1=xs, op0=ALU.mult, op1=ALU.add),
                             reads=[acc[cq][1], b_x[cq]], writes=[b_x[cq]])
                for cq in range(4):
                    tf, btf = outf[0]
                    rms_rows(xres3[:, cq, :], b_x[cq], g_fin, b_gfin, tf, btf)
                    r0 = tok0 + (cg0 + cq) * 128
                    S.dma("pool", _I("dma_start", out=out_d[r0:r0 + 128, :], in_=tf), reads=[btf], writes=[Buf()], is_output=True)
                S.barrier(("pe", "act", "dve", "pool"))


        for sq in range(NSEQ):
            tok0 = sq * SEQ
            A.reset(PERSIST_MARK)
            m0 = A.mark()
            g_mem, b_gmem = load_gain(2, "mem")
            wkv = A.alloc(8 * 2048, BF16)
            b_wkv = Buf("wkv")
            wtile_load(v3(wkv, 8), b_wkv, wb_xkv, WB["xkv"], 0, 8, 0, 2048)
            memT = A.alloc(8 * 256, BF16)
            b_memT = [Buf("memT0"), Buf("memT1")]
            for mt in range(2):
                xm = A.alloc(1024, F32)
                b_xm = Buf("xm")
                S.dma("sp", _I("dma_start", out=xm, in_=mem_d[sq * 256 + mt * 128: sq * 256 + (mt + 1) * 128, :]), writes=[b_xm])
                mn = A.alloc(1024, BF16)
                b_mn = Buf("mn")
                rms_rows(xm, b_xm, g_mem, b_gmem, mn, b_mn)
                transpose_blocks(mn, [b_mn], 8, lambda k0, n, mt=mt: v3(memT, 8)[:, k0:k0 + n, mt * 128:(mt + 1) * 128], [b_memT[mt]])
            wkv3 = v3(wkv, 8)
            memT3 = v3(memT, 8)
            for dtile in range(8):
                pt, pb = psum()
                for kc in range(8):
                    mm(pt[:, 0:256], wkv3[:, kc, dtile * 128:(dtile + 1) * 128], memT3[:, kc, :], kc == 0, kc == 7,
                       [b_wkv] + b_memT, pb, inc=(kc == 7))
                S.op("act", _I("copy", out=v3(KxT, 8)[:, dtile, :], in_=pt[:, 0:256]), reads=[pb], writes=[b_KxT])
            for mt in range(2):
                for half in range(2):
                    pt, pb = psum()
                    for kc in range(8):
                        mm(pt[:, :], memT3[:, kc, mt * 128:(mt + 1) * 128], wkv3[:, kc, 1024 + half * 512:1024 + (half + 1) * 512],
                           kc == 0, kc == 7, [b_wkv] + b_memT, pb, inc=(kc == 7))
                    S.op("act", _I("copy", out=v3(Vx, 2)[:, mt, half * 512:(half + 1) * 512], in_=pt[:, :]),
                         reads=[pb], writes=[b_Vx])
            S.barrier(("pe", "act", "dve", "pool", "sp"))
            A.reset(m0)
            if sq == 0:
                cast_weight("bs", w_bs_d, wb_bs, 2048, 1024)
                cast_weight("ba", w_ba_d, wb_ba, 1024, 1024)
                cast_weight("mo", w_mo_d, wb_mo, 1024, 1024)
                cast_weight("xq", w_xq_d, wb_xq, 1024, 1024)
                cast_weight("xo", w_xo_d, wb_xo, 1024, 1024)
                cast_weight("fi", w_fi_d, wb_fi, 1024, 256)
                cast_weight("fo", w_fo_d, wb_fo, FFN_H, 704)

            hT = A.alloc(8 * SEQ, BF16)
            hT3 = v3(hT, 8)
            b_hT = [Buf("hT%d" % c) for c in range(NCH)]
            m1 = A.mark()
            g_mix, b_gmix = load_gain(0, "mix")
            xts = [(A.alloc(1024, F32), Buf("xt%d" % i)) for i in range(2)]
            hns = [(A.alloc(1024, BF16), Buf("hn%d" % i)) for i in range(2)]
            for c in range(NCH):
                xt, b_xt = xts[c % 2]
                hn, b_hn = hns[c % 2]
                S.dma("sp", _I("dma_start", out=xt, in_=x_d[tok0 + c * 128: tok0 + (c + 1) * 128, :]), writes=[b_xt])
                rms_rows(xt, b_xt, g_mix, b_gmix, hn, b_hn)
                transpose_blocks(hn, [b_hn], 8, lambda k0, n, c=c: hT3[:, k0:k0 + n, c * 128:(c + 1) * 128], [b_hT[c]])
            if "h" in dbg_d and sq == 0:
                for kc in range(8):
                    tmpf = A.alloc(SEQ, F32)
                    bt = Buf()
                    S.op("dve", _I("tensor_copy", out=tmpf, in_=hT3[:, kc, :]), reads=b_hT, writes=[bt])
                    dbg_tap("h", tmpf, [bt], lambda d_, kc=kc: d_[kc * 128:(kc + 1) * 128, :])
            S.barrier(("pe", "act", "dve", "pool", "sp"))
            A.reset(m1)

            if STAGE >= 2:
                ssd_phase(sq, tok0, hT3, b_hT)
            S.barrier(("pe", "act", "dve", "pool", "sp"))
            A.reset(PERSIST_MARK)
            if STAGE >= 3:
                token_phase(sq, tok0)
            S.barrier(("pe", "act", "dve", "pool", "sp"))

        S.finish()
        S.emit(nc, st)
    return nc, A.peak, S


def _const_tables(SEQ):
    ident = np.eye(128, dtype=np.float32)
    triF = np.triu(np.ones((128, 128), np.float32))
    triB = np.tril(np.ones((128, 128), np.float32))
    ones = np.ones((128, 128), np.float32)
    maskF = np.where(triF > 0, 0.0, -30000.0).astype(np.float32)
    maskB = np.where(triB > 0, 0.0, -30000.0).astype(np.float32)
    cst = np.concatenate([ident, triF, triB, ones, maskF, maskB], axis=1)
    half = 32
    inv_freq = (np.float32(10000.0) ** (-np.arange(half, dtype=np.float32) / np.float32(half))).astype(np.float32)
    ang = np.arange(SEQ, dtype=np.float32)[:, None] * inv_freq[None, :]
    cos = np.cos(ang).astype(np.float32)
    sin = np.sin(ang).astype(np.float32)
    p = np.arange(128)
    d = p % 64
    cosT = cos[:, d % 32].T
    sgn = np.where(d < 32, -1.0, 1.0).astype(np.float32)
    sinT = (sin[:, d % 32] * sgn[None, :]).T
    rope = np.ascontiguousarray(np.concatenate([cosT, sinT], axis=1), dtype=np.float32)
    return np.ascontiguousarray(cst), rope


def _win_cols():
    cols = list(range(0, 2048)) + list(range(2048, 5120))
    for g in range(4):
        cols += list(range(5120 + 8 * g, 5120 + 8 * g + 8)) + list(range(5152 + 8 * g, 5152 + 8 * g + 8))
    qb, kb, vb, gb = 5184, 6208, 6464, 6720
    cols += list(range(qb, qb + 1024))
    cols += [qb + h * 64 + (d + 32) % 64 for h in range(16) for d in range(64)]
    for j in range(4):
        cols += list(range(kb + j * 64, kb + (j + 1) * 64)) * 2
    for j in range(4):
        cols += [kb + j * 64 + (d + 32) % 64 for d in range(64)] * 2
    cols += list(range(vb, vb + 256))
    cols += list(range(gb, gb + 2048))
    assert len(cols) == WIN
    return np.asarray(cols)


def make_shared(inp, SEQ):
    f = lambda a: np.ascontiguousarray(a, dtype=np.float32)
    cst, rope = _const_tables(SEQ)
    cw = inp["conv_w"][0]
    cb = inp["conv_b"][0]
    convp = np.zeros((128, 24, 8), np.float32)
    convp[:, :, 0:5] = cw.reshape(5, 24, 128).transpose(2, 1, 0)
    convp[:, :, 5] = cb.reshape(24, 128).T
    gm = lambda v: np.concatenate([np.concatenate([v[0][8 * g:8 * g + 8], v[1][8 * g:8 * g + 8]]) for g in range(4)])
    vecs = np.zeros((1, 192), np.float32)
    vecs[0, 0:64] = gm((inp["dt_bias_fwd"][0], inp["dt_bias_bwd"][0]))
    vecs[0, 64:128] = gm((inp["a_log_fwd"][0], inp["a_log_bwd"][0]))
    vecs[0, 128:160] = inp["d_skip"][0]
    vecs[0, 160:176] = inp["attn_sink"][0]
    gains = np.stack([inp["norm_mix_g"][0], inp["norm_xattn_g"][0], inp["norm_mem_g"][0], inp["norm_ffn_g"][0],
                      inp["norm_final_g"]], axis=0)
    return {
        "w_in_r": f(inp["w_in"][0][:, _win_cols()]),
        "w_bs": f(inp["w_branch_ssd"][0]), "w_ba": f(inp["w_branch_attn"][0]), "w_mo": f(inp["w_mix_out"][0]),
        "w_xq": f(inp["w_xattn_q"][0]), "w_xkv": f(inp["w_xattn_kv"][0]), "w_xo": f(inp["w_xattn_out"][0]),
        "w_fi": f(inp["w_ffn_in"][0]), "w_fo": f(inp["w_ffn_out"][0]),
        "gains": f(gains), "convp": f(convp.reshape(128, 192)), "vecs": f(vecs),
        "ssdg": f(inp["ssd_norm_g"][0][None, :]), "cst": cst, "rope": rope,
    }


def make_inputs(inp, SEQ, NSEQ, core, shared=None):
    shared = shared if shared is not None else make_shared(inp, SEQ)
    b0 = core * NSEQ
    m = dict(shared)
    m["x"] = np.ascontiguousarray(inp["x"][b0:b0 + NSEQ, :SEQ].reshape(NSEQ * SEQ, 1024), dtype=np.float32)
    m["mem"] = np.ascontiguousarray(inp["mem"][b0:b0 + NSEQ].reshape(NSEQ * 256, 1024), dtype=np.float32)
    return m


_PROG = {}


def kernel(**inputs):
    SEQ, NSEQ, NCORES = 2048, 2, 8
    inp = {k: np.asarray(v) for k, v in inputs.items()}
    if "prog" not in _PROG:
        _PROG["prog"] = build_program(SEQ, NSEQ)[0]
    nc = _PROG["prog"]
    shared = make_shared(inp, SEQ)
    in_maps = [make_inputs(inp, SEQ, NSEQ, c, shared) for c in range(NCORES)]
    res = run_bass_kernel_spmd(nc, in_maps, core_ids=list(range(NCORES)))
    outs = [np.asarray(r["out"]).reshape(NSEQ, SEQ, 1024) for r in res.results]
    return np.concatenate(outs, axis=0).astype(np.float32)
```
